# Optimizing a Trainium2 kernel written in Bass

```python
import jax, jax.numpy as jnp
from jax import lax
import numpy as np

D_MODEL = 2048
BATCH = 4
SEQ = 4096
DEPTH = 1

MIX_WIDTH = D_MODEL
MLA_WIDTH = D_MODEL // 2
HGRN_WIDTH = MIX_WIDTH - MLA_WIDTH
MLA_HEADS = 8
MLA_V_DIM = MLA_WIDTH // MLA_HEADS
MLA_NOPE_DIM = 128
MLA_ROPE_DIM = 64
MLA_Q_RANK = D_MODEL // 4
MLA_KV_RANK = D_MODEL // 8
ROPE_BASE = 10000.0
Q_BLOCK = 128
HGRN_HEADS = 8
HGRN_KEY_DIM = 128
HGRN_VAL_DIM = HGRN_WIDTH // HGRN_HEADS
HGRN_QK_WIDTH = HGRN_HEADS * HGRN_KEY_DIM
HGRN_CHUNK = 64
D_FF = 4 * D_MODEL
NORM_EPS = 1e-6
N_MOD = 6
IN_WIDTH = MLA_Q_RANK + MLA_KV_RANK + MLA_ROPE_DIM + 3 * HGRN_QK_WIDTH + 2 * HGRN_WIDTH

kernel_name = "hymba_mla_hgrn2_adaln_encoder_layer"


def _in_proj_offsets():
    widths = (MLA_Q_RANK, MLA_KV_RANK, MLA_ROPE_DIM, HGRN_QK_WIDTH, HGRN_QK_WIDTH,
              HGRN_QK_WIDTH, HGRN_WIDTH, HGRN_WIDTH)
    return [int(o) for o in np.cumsum(widths)[:-1]]


def _rmsnorm(x, g):
    xf = x.astype(jnp.float32)
    y = xf * lax.rsqrt(jnp.mean(xf * xf, axis=-1, keepdims=True) + NORM_EPS)
    return (y * g.astype(jnp.float32)).astype(x.dtype)


def _rope_angles(positions):
    half = MLA_ROPE_DIM // 2
    inv_freq = ROPE_BASE ** (-jnp.arange(half, dtype=jnp.float32) / half)
    ang = positions.astype(jnp.float32)[..., None] * inv_freq
    return jnp.cos(ang), jnp.sin(ang)


def _apply_rope(x, cos, sin):
    x1, x2 = jnp.split(x.astype(jnp.float32), 2, axis=-1)
    return jnp.concatenate([x1 * cos - x2 * sin, x1 * sin + x2 * cos], axis=-1).astype(x.dtype)


def _mla(q_lat, kv_lat, k_rope, cos, sin, q_norm_g, kv_norm_g, w_uq, w_ukv):
    B, S, _ = q_lat.shape
    H = MLA_HEADS
    q = (_rmsnorm(q_lat, q_norm_g) @ w_uq).reshape(B, S, H, MLA_NOPE_DIM + MLA_ROPE_DIM)
    q_nope = q[..., :MLA_NOPE_DIM]
    q_rope = _apply_rope(q[..., MLA_NOPE_DIM:], cos[:, :, None, :], sin[:, :, None, :])
    kv = (_rmsnorm(kv_lat, kv_norm_g) @ w_ukv).reshape(B, S, H, MLA_NOPE_DIM + MLA_V_DIM)
    k_nope, v = kv[..., :MLA_NOPE_DIM], kv[..., MLA_NOPE_DIM:]
    k_rope = _apply_rope(k_rope, cos, sin)
    scale = (MLA_NOPE_DIM + MLA_ROPE_DIM) ** -0.5
    nb = S // Q_BLOCK

    def to_blocks(t):
        return t.reshape(B, nb, Q_BLOCK, *t.shape[2:]).swapaxes(0, 1)

    def attend(blk):
        qn, qr = blk
        s = (jnp.einsum('bqhd,bkhd->bhqk', qn, k_nope)
             + jnp.einsum('bqhr,bkr->bhqk', qr, k_rope))
        p = jax.nn.softmax(s.astype(jnp.float32) * scale, axis=-1).astype(v.dtype)
        return jnp.einsum('bhqk,bkhd->bqhd', p, v)

    o = lax.map(attend, (to_blocks(q_nope), to_blocks(q_rope)))
    return o.swapaxes(0, 1).reshape(B, S, H * MLA_V_DIM)


def _gla_chunkwise(q, k, v, log_f):
    B, S, H, Dk = q.shape
    Dv = v.shape[-1]
    n = S // HGRN_CHUNK

    def to_chunks(t):
        return t.reshape(B, n, HGRN_CHUNK, H, t.shape[-1]).transpose(1, 0, 3, 2, 4)

    qc, kc, vc = to_chunks(q), to_chunks(k), to_chunks(v)
    bc = jnp.cumsum(to_chunks(log_f), axis=3)
    mask = jnp.tril(jnp.ones((HGRN_CHUNK, HGRN_CHUNK), dtype=bool))[None, None, :, :, None]

    def step(state, inp):
        q_, k_, v_, b_ = inp
        rel = jnp.where(mask, b_[:, :, :, None, :] - b_[:, :, None, :, :], -jnp.inf)
        a = jnp.einsum('bhtd,bhsd,bhtsd->bhts', q_, k_, jnp.exp(rel))
        o = (jnp.einsum('bhts,bhse->bhte', a, v_)
             + jnp.einsum('bhtd,bhde->bhte', q_ * jnp.exp(b_), state))
        b_last = b_[:, :, -1:, :]
        state = (state * jnp.exp(b_last)[:, :, 0, :, None]
                 + jnp.einsum('bhsd,bhse->bhde', k_ * jnp.exp(b_last - b_), v_))
        return state, o

    state0 = jnp.zeros((B, H, Dk, Dv), jnp.float32)
    _, o = lax.scan(step, state0, (qc, kc, vc, bc))
    return o.transpose(1, 0, 3, 2, 4).reshape(B, S, H, Dv)


def _hgrn2_direction(q, f_logit, v, lb):
    f = lb + (1.0 - lb) * jax.nn.sigmoid(f_logit.astype(jnp.float32))
    return _gla_chunkwise(q.astype(jnp.float32), 1.0 - f, v.astype(jnp.float32), jnp.log(f))


def _lower_bound(lb_logits, layer):
    p = jax.nn.softmax(lb_logits.astype(jnp.float32), axis=0)
    return jnp.cumsum(p, axis=0)[layer].reshape(HGRN_HEADS, HGRN_KEY_DIM)


def _hgrn2_bidir(hq, f_fwd, f_bwd, hi, hg, lb_fwd, lb_bwd, norm_g):
    B, S, _ = hq.shape
    q = hq.reshape(B, S, HGRN_HEADS, HGRN_KEY_DIM)
    ff = f_fwd.reshape(B, S, HGRN_HEADS, HGRN_KEY_DIM)
    fb = f_bwd.reshape(B, S, HGRN_HEADS, HGRN_KEY_DIM)
    v = hi.reshape(B, S, HGRN_HEADS, HGRN_VAL_DIM)
    o_fwd = _hgrn2_direction(q, ff, v, lb_fwd)
    o_bwd = jnp.flip(_hgrn2_direction(jnp.flip(q, 1), jnp.flip(fb, 1), jnp.flip(v, 1), lb_bwd), 1)
    o = _rmsnorm(o_fwd + o_bwd, norm_g).reshape(B, S, HGRN_WIDTH)
    return (o * jax.nn.silu(hg.astype(jnp.float32))).astype(hq.dtype)


def setup_inputs(seed: int = 0) -> dict:
    key = jax.random.key(seed)
    ks = jax.random.split(key, 24)
    f32 = jnp.float32

    def nrm(k, shape, fan_in, s=1.0):
        return jax.random.normal(k, shape, f32) * (s * fan_in ** -0.5)

    def gain(k, shape):
        return 1.0 + 0.05 * jax.random.normal(k, shape, f32)

    offs = jax.random.randint(ks[2], (BATCH, 1), 0, 1024, dtype=jnp.int32)
    positions = (jnp.arange(SEQ, dtype=jnp.int32)[None, :] + offs).astype(jnp.int32)
    return {
        "x": jax.random.normal(ks[0], (BATCH, SEQ, D_MODEL), f32),
        "c": jax.random.normal(ks[1], (BATCH, D_MODEL), f32),
        "positions": positions,
        "w_mod": nrm(ks[3], (DEPTH, D_MODEL, N_MOD * D_MODEL), D_MODEL, 0.5),
        "b_mod": 0.01 * jax.random.normal(ks[4], (DEPTH, N_MOD * D_MODEL), f32),
        "pre_mix_g": gain(ks[5], (DEPTH, D_MODEL)),
        "post_mix_g": gain(ks[6], (DEPTH, D_MODEL)),
        "pre_mlp_g": gain(ks[7], (DEPTH, D_MODEL)),
        "post_mlp_g": gain(ks[8], (DEPTH, D_MODEL)),
        "w_in": nrm(ks[9], (DEPTH, D_MODEL, IN_WIDTH), D_MODEL),
        "q_norm_g": gain(ks[10], (DEPTH, MLA_Q_RANK)),
        "kv_norm_g": gain(ks[11], (DEPTH, MLA_KV_RANK)),
        "w_uq": nrm(ks[12], (DEPTH, MLA_Q_RANK, MLA_HEADS * (MLA_NOPE_DIM + MLA_ROPE_DIM)), MLA_Q_RANK),
        "w_ukv": nrm(ks[13], (DEPTH, MLA_KV_RANK, MLA_HEADS * (MLA_NOPE_DIM + MLA_V_DIM)), MLA_KV_RANK),
        "hgrn_norm_g": gain(ks[14], (DEPTH, HGRN_VAL_DIM)),
        "hgrn_lb_logits_fwd": 0.1 * jax.random.normal(ks[15], (DEPTH + 1, HGRN_QK_WIDTH), f32),
        "hgrn_lb_logits_bwd": 0.1 * jax.random.normal(ks[16], (DEPTH + 1, HGRN_QK_WIDTH), f32),
        "w_out": nrm(ks[17], (DEPTH, MIX_WIDTH, D_MODEL), MIX_WIDTH),
        "w_up": nrm(ks[18], (DEPTH, D_MODEL, D_FF), D_MODEL),
        "w_down": nrm(ks[19], (DEPTH, D_FF, D_MODEL), D_FF),
    }


def reference(x, c, positions, w_mod, b_mod, pre_mix_g, post_mix_g, pre_mlp_g, post_mlp_g,
              w_in, q_norm_g, kv_norm_g, w_uq, w_ukv, hgrn_norm_g, hgrn_lb_logits_fwd,
              hgrn_lb_logits_bwd, w_out, w_up, w_down):
    cos, sin = _rope_angles(positions)
    cond = jax.nn.silu(c)
    offsets = _in_proj_offsets()
    for layer in range(DEPTH):
        mod = (cond @ w_mod[layer] + b_mod[layer]).astype(x.dtype)
        shift_a, scale_a, gate_a, shift_m, scale_m, gate_m = [
            m[:, None, :] for m in jnp.split(mod, N_MOD, axis=-1)]

        h = _rmsnorm(x, pre_mix_g[layer]) * (1.0 + scale_a) + shift_a
        proj = h @ w_in[layer]
        q_lat, kv_lat, k_rope, hq, f_fwd, f_bwd, hi, hg = jnp.split(proj, offsets, axis=-1)
        attn_out = _mla(q_lat, kv_lat, k_rope, cos, sin, q_norm_g[layer], kv_norm_g[layer],
                        w_uq[layer], w_ukv[layer])
        rec_out = _hgrn2_bidir(hq, f_fwd, f_bwd, hi, hg,
                               _lower_bound(hgrn_lb_logits_fwd, layer),
                               _lower_bound(hgrn_lb_logits_bwd, layer), hgrn_norm_g[layer])
        mix = jnp.concatenate([attn_out, rec_out], axis=-1) @ w_out[layer]
        x = x + gate_a * _rmsnorm(mix, post_mix_g[layer])

        h = _rmsnorm(x, pre_mlp_g[layer]) * (1.0 + scale_m) + shift_m
        y = jnp.square(jax.nn.relu(h @ w_up[layer])) @ w_down[layer]
        x = x + gate_m * _rmsnorm(y, post_mlp_g[layer])
    return x
```

```python
import math
import contextlib
import numpy as np
import concourse.bass as bass
import concourse.mybir as mybir
from concourse.bass_utils import run_bass_kernel_spmd

F32 = mybir.dt.float32
BF16 = mybir.dt.bfloat16
I32 = mybir.dt.int32
AF = mybir.ActivationFunctionType
ALU = mybir.AluOpType

ENGS = ("pe", "act", "dve", "pool", "sp")


class Res:
    __slots__ = ("name", "w", "rs")

    def __init__(self, name):
        self.name = name
        self.w = None
        self.rs = []


class Op:
    __slots__ = ("eng", "fn", "deps", "idx", "signal", "sigval", "dma_lane", "dma_val")

    def __init__(self, eng, fn):
        self.eng = eng
        self.fn = fn
        self.deps = []
        self.idx = None
        self.signal = False
        self.sigval = None
        self.dma_lane = None
        self.dma_val = None


class FW:
    def __init__(self, nc):
        self.nc = nc
        self.ops = {e: [] for e in ENGS}
        self.eh = {"pe": nc.tensor, "act": nc.scalar, "dve": nc.vector, "pool": nc.gpsimd, "sp": nc.sync}
        self.lanes = {}
        self.lane_sems = {}
        self.eng_sems = {}
        self.waited = {e: {} for e in ENGS}
        self.stack = contextlib.ExitStack()

    def res(self, name):
        return Res(name)

    def _tok_needed(self, eng, tok):
        key = (tok[0], tok[1])
        cur = self.waited[eng].get(key, -1)
        if tok[2] <= cur:
            return False
        self.waited[eng][key] = tok[2]
        return True

    def _add(self, eng, fn, reads, writes, dma_lane=None):
        op = Op(eng, fn)
        op.idx = len(self.ops[eng])
        toks = []
        for r in reads:
            if r.w is not None:
                toks.append(r.w)
        for w in writes:
            if w.w is not None:
                toks.append(w.w)
            toks.extend(w.rs)
        for tok in toks:
            if tok[0] == "e" and tok[1] == eng and eng == "pe":
                continue
            if tok[0] == "e" and tok[1] == eng and tok[2] >= op.idx:
                continue
            if self._tok_needed(eng, tok):
                op.deps.append(tok)
        if dma_lane is not None:
            self.lanes[dma_lane] = self.lanes.get(dma_lane, 0) + 1
            op.dma_lane = dma_lane
            op.dma_val = self.lanes[dma_lane] * 16
            mytok = ("d", dma_lane, op.dma_val)
        else:
            mytok = ("e", eng, op.idx)
        for r in reads:
            r.rs.append(mytok)
        for w in writes:
            w.w = mytok
            w.rs = []
        self.ops[eng].append(op)
        return op

    def op(self, eng, fn, reads=(), writes=()):
        return self._add(eng, fn, list(reads), list(writes))

    def dma(self, q, lane, out, in_, reads=(), writes=(), **kw):
        def fn(e):
            return e.dma_start(out=out, in_=in_, **kw)
        return self._add(q, fn, list(reads), list(writes), dma_lane=lane)

    def barrier(self, engs=ENGS):
        toks = []
        for e in ENGS:
            if self.ops[e]:
                toks.append(("e", e, len(self.ops[e]) - 1))
        for lane, cnt in self.lanes.items():
            toks.append(("d", lane, cnt * 16))
        for e in engs:
            op = Op(e, None)
            op.idx = len(self.ops[e])
            for tok in toks:
                if tok[0] == "e" and tok[1] == e:
                    continue
                if self._tok_needed(e, tok):
                    op.deps.append(tok)
            self.ops[e].append(op)

    def emit(self):
        nc = self.nc
        for e in ENGS:
            for op in self.ops[e]:
                for tok in op.deps:
                    if tok[0] == "e":
                        self.ops[tok[1]][tok[2]].signal = True
        for e in ENGS:
            cnt = 0
            ops = self.ops[e]
            for i, op in enumerate(ops):
                if op.signal and (op.fn is None or op.dma_lane is not None):
                    j = i - 1
                    while j >= 0 and (ops[j].fn is None or ops[j].dma_lane is not None):
                        j -= 1
                    op.signal = False
                    if j >= 0:
                        ops[j].signal = True
                        op.sigval = ("alias", j)
                    else:
                        op.sigval = ("zero",)
            for op in ops:
                if op.signal:
                    cnt += 1
                    op.sigval = cnt
            for op in ops:
                if isinstance(op.sigval, tuple):
                    op.sigval = ops[op.sigval[1]].sigval if op.sigval[0] == "alias" else 0
        for e in ENGS:
            self.eng_sems[e] = self.stack.enter_context(nc.semaphore("s_" + e))
        for lane in self.lanes:
            self.lane_sems[lane] = self.stack.enter_context(nc.semaphore("l_" + lane))
        block = self.stack.enter_context(nc.Block())

        def run(e, eh):
            for op in self.ops[e]:
                for tok in op.deps:
                    if tok[0] == "e":
                        v = self.ops[tok[1]][tok[2]].sigval
                        if v:
                            eh.wait_ge(self.eng_sems[tok[1]], v)
                    else:
                        eh.wait_ge(self.lane_sems[tok[1]], tok[2])
                if op.fn is None:
                    continue
                ins = op.fn(eh)
                if op.dma_lane is not None:
                    ins.then_inc(self.lane_sems[op.dma_lane], 16)
                elif op.signal:
                    ins.then_inc(self.eng_sems[e], 1)

        @block.tensor
        def _(eh):
            run("pe", eh)

        @block.scalar
        def _(eh):
            run("act", eh)

        @block.vector
        def _(eh):
            run("dve", eh)

        @block.gpsimd
        def _(eh):
            run("pool", eh)

        @block.sync
        def _(eh):
            run("sp", eh)

    def close(self):
        self.stack.close()


D = 2048
TOWN = 2048
TALL = 4096
SCALE = 192 ** -0.5
EPS = 1e-6
TWO_PI = 2.0 * math.pi


class T:
    def __init__(self, ap, r):
        self.ap = ap
        self.r = r

    def __getitem__(self, k):
        return self.ap[k]


def _rs(lst):
    out = []
    for x in lst:
        if x is None:
            continue
        out.append(x.r if isinstance(x, T) else x)
    return out


class KB:
    def __init__(self, stage=99):
        self.stage = stage
        nc = self.nc = bass.Bass("TRN2", target_bir_lowering=False)
        self.fw = FW(nc)
        n32 = (nc.sbuf_bytes_remaining - 512) // 4
        self.arena = self.fw.stack.enter_context(nc.sbuf_tensor("arena", [128, n32], F32))
        self.n32 = n32
        self.top = 0
        self.ps = [self.fw.stack.enter_context(nc.psum_tensor("ps%d" % i, [128, 512], F32))[:, :] for i in range(8)]
        self.psr = [self.fw.res("ps%d" % i) for i in range(8)]
        self.rot = 0

    def alloc(self, name, shape, dt=F32):
        n = 1
        for s in shape[1:]:
            n *= s
        words = n if dt != BF16 else (n + 1) // 2
        words = (words + 7) // 8 * 8
        assert self.top + words <= self.n32, ("SBUF overflow", name, self.top, words, self.n32)
        ap = self.arena[:, self.top:self.top + words]
        if dt == BF16:
            ap = ap.bitcast(BF16)[:, 0:n]
        elif dt == I32:
            ap = ap.bitcast(I32)[:, 0:n]
        else:
            ap = ap[:, 0:n]
        self.top += words
        if len(shape) == 3:
            ap = ap.rearrange("p (a b) -> p a b", a=shape[1])
        if shape[0] != 128:
            ap = ap[0:shape[0]]
        return T(ap, self.fw.res(name))

    def op(self, eng, fn, reads, writes):
        self.fw.op(eng, fn, _rs(reads), _rs(writes))

    def act(self, out, in_, func, reads, writes, scale=1.0, bias=0.0, accum=None):
        self.op("act", lambda e: e.activation(out=out, in_=in_, func=func, bias=bias, scale=scale, accum_out=accum), reads, writes)

    def ts(self, out, in0, s1, s2, op0, op1, reads, writes, eng="dve"):
        if op1 is None:
            self.op(eng, lambda e: e.tensor_scalar(out=out, in0=in0, scalar1=s1, scalar2=None, op0=op0), reads, writes)
        else:
            self.op(eng, lambda e: e.tensor_scalar(out=out, in0=in0, scalar1=s1, scalar2=s2, op0=op0, op1=op1), reads, writes)

    def tt(self, out, in0, in1, op, reads, writes, eng="dve"):
        self.op(eng, lambda e: e.tensor_tensor(out=out, in0=in0, in1=in1, op=op), reads, writes)

    def stt(self, out, in0, scalar, in1, op0, op1, reads, writes):
        self.op("dve", lambda e: e.scalar_tensor_tensor(out=out, in0=in0, scalar=scalar, in1=in1, op0=op0, op1=op1), reads, writes)

    def cp(self, eng, out, in_, reads, writes):
        if eng == "act":
            self.op("act", lambda e: e.copy(out=out, in_=in_), reads, writes)
        else:
            self.op(eng, lambda e: e.tensor_copy(out=out, in_=in_), reads, writes)

    def mm(self, out, lhsT, rhs, start, stop, reads, writes):
        self.op("pe", lambda e: e.matmul(out, lhsT=lhsT, rhs=rhs, start=start, stop=stop), reads, writes)

    def tr(self, out, in_, reads, writes):
        ident = self.ident
        self.op("pe", lambda e: e.transpose(out=out, in_=in_, identity=ident.ap), list(reads) + [ident], writes)

    def dma(self, q, lane, out, in_, reads, writes):
        self.fw.dma(q, lane, out, in_, _rs(reads), _rs(writes))

    def bank(self, lo=0, hi=8):
        n = hi - lo
        i = lo + (self.rot % n)
        self.rot += 1
        return T(self.ps[i], self.psr[i])

    def rsqrt_mean(self, t, n, reads_writes):
        self.ts(t, t, 1.0 / n, EPS, ALU.mult, ALU.add, reads_writes, reads_writes)
        self.act(t, t, AF.Ln, reads_writes, reads_writes)
        self.act(t, t, AF.Exp, reads_writes, reads_writes, scale=-0.5)


def build_program(stage=99):
    kb = KB(stage)
    nc, fw = kb.nc, kb.fw

    def din(name, shape, dt=F32):
        return nc.dram_tensor(name, shape, dt, kind="ExternalInput").ap()

    def dscr(name, shape, dt):
        return nc.dram_tensor(name, shape, dt, kind="Internal").ap()

    x_d = din("x", [TALL, D])
    pos_d = din("pos", [128, TALL], I32)
    vecs_d = din("vecs", [128, 128])
    vecs2_d = din("vecs2", [128, 128])
    wmod_d = din("w_mod", [D, 6 * D])
    win_d = din("w_in", [D, 6144])
    wuq_d = din("w_uq", [512, 2048])
    wukv_d = din("w_ukv", [256, 2048])
    wout_d = din("w_out", [D, D])
    wup_d = din("w_up", [D, 4 * D])
    wdn_d = din("w_down", [4 * D, D])
    out_d = nc.dram_tensor("out", [TOWN, D], F32, kind="ExternalOutput").ap()

    hq_s = dscr("hq_s", [8, 128, TOWN], BF16)
    z1_s = dscr("z1_s", [8, 128, TOWN], F32)
    z2_s = dscr("z2_s", [8, 128, TALL], F32)
    v_s = dscr("v_s", [TALL, 1024], BF16)
    sg_s = dscr("sg_s", [8, 128, TOWN], BF16)
    cat_s = dscr("cat_s", [16, 128, TOWN], BF16)
    R_hq = [fw.res("hq%d" % h) for h in range(8)]
    R_z1 = [fw.res("z1%d" % h) for h in range(8)]
    R_z2 = [fw.res("z2%d" % h) for h in range(8)]
    R_v = fw.res("v_s")
    R_sg = [fw.res("sg%d" % h) for h in range(8)]
    R_cat = [fw.res("cat%d" % c) for c in range(16)]
    R_out = fw.res("out")

    ident = kb.alloc("ident", [128, 128]); kb.ident = ident
    ones32 = kb.alloc("ones32", [128, 128])
    ones16 = kb.alloc("ones16", [128, 128], BF16)
    maskF = kb.alloc("maskF", [128, 128])
    maskB = kb.alloc("maskB", [128, 128])
    vT = kb.alloc("vT", [128, 128])
    v2T = kb.alloc("v2T", [128, 128])
    modT = kb.alloc("modT", [128, 96])
    A1 = kb.alloc("A1", [128, 16]); G1 = kb.alloc("G1", [128, 16]); A2 = kb.alloc("A2", [128, 16]); G2 = kb.alloc("G2", [128, 16])
    lbT = kb.alloc("lbT", [128, 16]); omlT = kb.alloc("omlT", [128, 16])
    cond = kb.alloc("cond", [128, 16], BF16)
    ring = [kb.alloc("ring%d" % i, [128, 16, 512], BF16) for i in range(3)]
    ringn = [0]

    def load_unit(src_ap):
        i = ringn[0] % 3
        ringn[0] += 1
        kb.dma("pool", "ring%d" % i, ring[i].ap, src_ap.rearrange("(k p) n -> p k n", p=128), [], [ring[i]])
        return ring[i]

    kb.op("pool", lambda e: e.memset(ident.ap, 0.0), [], [ident])
    kb.op("pool", lambda e: e.affine_select(out=ident.ap, in_=ident.ap, compare_op=ALU.not_equal, fill=1.0, base=0,
                                            pattern=[[-1, 128]], channel_multiplier=1), [ident], [ident])
    kb.op("pool", lambda e: e.memset(ones32.ap, 1.0), [], [ones32])
    kb.op("pool", lambda e: e.memset(ones16.ap, 1.0), [], [ones16])
    kb.op("pool", lambda e: e.memset(maskF.ap, 1.0), [], [maskF])
    kb.op("pool", lambda e: e.affine_select(out=maskF.ap, in_=maskF.ap, compare_op=ALU.is_ge, fill=0.0, base=0,
                                            pattern=[[1, 128]], channel_multiplier=-1), [maskF], [maskF])
    kb.op("pool", lambda e: e.memset(maskF[0:64, 64:128], 0.0), [maskF], [maskF])
    kb.op("pool", lambda e: e.memset(maskB.ap, 1.0), [], [maskB])
    kb.op("pool", lambda e: e.affine_select(out=maskB.ap, in_=maskB.ap, compare_op=ALU.is_ge, fill=0.0, base=0,
                                            pattern=[[-1, 128]], channel_multiplier=1), [maskB], [maskB])
    kb.op("pool", lambda e: e.memset(maskB[64:128, 0:64], 0.0), [maskB], [maskB])

    mark_global = kb.top

    tmpv = kb.alloc("tmpv", [128, 128])
    for src, dst in ((vecs_d, vT), (vecs2_d, v2T)):
        kb.dma("sp", "misc", tmpv.ap, src, [], [tmpv])
        b = kb.bank()
        kb.tr(b[:, 0:128], tmpv.ap, [tmpv], [b])
        kb.cp("dve", dst.ap, b[:, 0:128], [b], [dst])
    e16 = kb.alloc("e16", [128, 16])
    cT = v2T[:, 32:48]
    kb.act(e16.ap, cT, AF.Exp, [v2T], [e16], scale=-1.0)
    kb.ts(e16.ap, e16.ap, 1.0, None, ALU.add, None, [e16], [e16])
    kb.op("dve", lambda e: e.reciprocal(out=e16.ap, in_=e16.ap), [e16], [e16])
    kb.tt(cond.ap, cT, e16.ap, ALU.mult, [v2T, e16], [cond])
    pm = kb.bank()
    NU = 24
    units = [None] * NU
    units[0] = load_unit(wmod_d[:, 0:512])
    units[1] = load_unit(wmod_d[:, 512:1024])
    for u in range(NU):
        if u + 2 < NU:
            units[u + 2] = load_unit(wmod_d[:, (u + 2) * 512:(u + 3) * 512])
        un = units[u]
        for cc in range(4):
            j = u * 4 + cc
            for k in range(16):
                kb.mm(pm[:, j:j + 1], un[:, k, cc * 128:(cc + 1) * 128], cond[:, k:k + 1], k == 0, k == 15, [un, cond], [pm])
    kb.tt(modT.ap, pm[:, 0:96], vT[:, 0:96], ALU.add, [pm, vT], [modT])
    shA = modT[:, 0:16]; shM = modT[:, 48:64]
    kb.stt(A1.ap, modT[:, 16:32], 1.0, vT[:, 96:112], ALU.add, ALU.mult, [modT, vT], [A1])
    kb.tt(G1.ap, modT[:, 32:48], vT[:, 112:128], ALU.mult, [modT, vT], [G1])
    kb.stt(A2.ap, modT[:, 64:80], 1.0, v2T[:, 0:16], ALU.add, ALU.mult, [modT, v2T], [A2])
    kb.tt(G2.ap, modT[:, 80:96], v2T[:, 16:32], ALU.mult, [modT, v2T], [G2])
    for dI in range(2):
        l0 = v2T[:, 48 + 16 * dI:56 + 16 * dI]; l1 = v2T[:, 56 + 16 * dI:64 + 16 * dI]
        o = lbT[:, 8 * dI:8 * dI + 8]
        kb.tt(o, l1, l0, ALU.subtract, [v2T], [lbT])
        kb.act(o, o, AF.Exp, [lbT], [lbT])
        kb.ts(o, o, 1.0, None, ALU.add, None, [lbT], [lbT])
        kb.op("dve", lambda e, o=o: e.reciprocal(out=o, in_=o), [lbT], [lbT])
    kb.ts(omlT.ap, lbT.ap, -1.0, 1.0, ALU.mult, ALU.add, [lbT], [omlT])
    qg = v2T[:, 80:84]; kvg = v2T[:, 84:86]; hgn = v2T[:, 86:87]; invf = v2T[:, 87:88]

    kb.top = mark_global + 0
    tmpv = None
    qnT = kb.alloc("qnT", [128, 4, TOWN], BF16)
    kvnT = kb.alloc("kvnT", [128, 2, TALL], BF16)
    krT = kb.alloc("krT", [128, TALL], BF16)
    CS = kb.alloc("CS", [128, TOWN])
    mark_persist = kb.top
    hT = kb.alloc("hT", [128, 16, 1024], BF16)
    xb = [kb.alloc("xb%d" % i, [128, D]) for i in range(2)]
    junk = kb.alloc("junk", [128, D], BF16)
    ss1 = kb.alloc("ss1", [128, 1])
    posi = kb.alloc("posi", [128, 1024], I32)
    ang = kb.alloc("ang", [128, 1024])
    kf = kb.alloc("kf", [128, 1024])
    kiT = kb.alloc("kiT", [128, 1024], I32)
    cosT = kb.alloc("cosT", [128, 1024]); sinT = kb.alloc("sinT", [128, 1024])
    sq = [kb.alloc("sq%d" % i, [128, 512]) for i in range(2)]
    rsb = kb.alloc("rsb", [128, 512])
    tA = kb.alloc("tA", [128, 512]); tB = kb.alloc("tB", [128, 512])
    stg32 = [kb.alloc("stg32_%d" % i, [128, 1024]) for i in range(2)]
    stg16 = [kb.alloc("stg16_%d" % i, [128, 1024], BF16) for i in range(2)]
    stgv = [kb.alloc("stgv%d" % i, [128, 4, 512], BF16) for i in range(2)]
    cnt = {"s32": 0, "s16": 0, "sv": 0, "sq": 0}

    def sincos(dst, shift):
        kb.ts(kf.ap, ang.ap, shift, 1.0 / TWO_PI, ALU.add, ALU.mult, [ang], [kf])
        kb.cp("dve", kiT.ap, kf.ap, [kf], [kiT])
        kb.cp("dve", kf.ap, kiT.ap, [kiT], [kf])
        kb.stt(dst.ap, kf.ap, -TWO_PI, ang.ap, ALU.mult, ALU.add, [kf, ang], [dst])
        kb.ts(dst.ap, dst.ap, shift, None, ALU.add, None, [dst], [dst])
        kb.ts(kf.ap, dst.ap, math.pi, TWO_PI, ALU.is_gt, ALU.mult, [dst], [kf])
        kb.tt(dst.ap, dst.ap, kf.ap, ALU.subtract, [dst, kf], [dst])
        kb.ts(kf.ap, dst.ap, -math.pi, TWO_PI, ALU.is_lt, ALU.mult, [dst], [kf])
        kb.tt(dst.ap, dst.ap, kf.ap, ALU.add, [dst, kf], [dst])
        kb.act(dst.ap, dst.ap, AF.Sin, [dst], [dst])

    def stats_norm(banks, ncc, nfeat, gcols, dstT, t0):
        sb = kb.bank(4, 8)
        for cc in range(ncc):
            s = sq[cnt["sq"] % 2]; cnt["sq"] += 1
            kb.act(s.ap, banks[cc].ap, AF.Square, [banks[cc]], [s])
            kb.mm(sb.ap, ones32.ap, s.ap, cc == 0, cc == ncc - 1, [ones32, s], [sb])
        kb.cp("dve", rsb.ap, sb.ap, [sb], [rsb])
        kb.rsqrt_mean(rsb.ap, nfeat, [rsb])
        for cc in range(ncc):
            kb.stt(dstT[:, cc, t0:t0 + 512], banks[cc].ap, gcols[:, cc:cc + 1], rsb.ap, ALU.mult, ALU.mult,
                   [banks[cc], rsb, v2T], [dstT])

    for g in range(4):
        own = g < 2
        tok0 = g * 1024
        kb.dma("sp", "misc", posi.ap, pos_d[:, tok0:tok0 + 1024], [], [posi])
        kb.cp("dve", ang.ap, posi.ap, [posi], [ang])
        kb.ts(ang.ap, ang.ap, invf, None, ALU.mult, None, [ang, v2T], [ang])
        sincos(sinT, 0.0)
        sincos(cosT, math.pi / 2)
        if own:
            kb.cp("act", CS[0:64, tok0:tok0 + 1024], cosT[0:64, :], [cosT], [CS])
            kb.cp("act", CS[64:128, tok0:tok0 + 1024], sinT[64:128, :], [sinT], [CS])
        for i in range(8):
            xt = xb[i % 2]
            r0 = tok0 + i * 128
            kb.dma("sp", "x%d" % (i % 2), xt.ap, x_d[r0:r0 + 128, :], [], [xt])
            kb.act(junk.ap, xt.ap, AF.Square, [xt], [junk, ss1], accum=ss1.ap)
            kb.rsqrt_mean(ss1.ap, D, [ss1])
            kb.ts(xt.ap, xt.ap, ss1[:, 0:1], None, ALU.mult, None, [xt, ss1], [xt])
            for c4 in range(4):
                b = kb.bank(0, 4)
                for j in range(4):
                    c = c4 * 4 + j
                    kb.tr(b[:, j * 128:(j + 1) * 128], xt[:, c * 128:(c + 1) * 128], [xt], [b])
                for j in range(4):
                    c = c4 * 4 + j
                    o = hT[:, c, i * 128:(i + 1) * 128]
                    if j % 2 == 0:
                        kb.act(o, b[:, j * 128:(j + 1) * 128], AF.Identity, [b, A1, modT], [hT], scale=A1[:, c:c + 1], bias=shA[:, c:c + 1])
                    else:
                        kb.ts(o, b[:, j * 128:(j + 1) * 128], A1[:, c:c + 1], shA[:, c:c + 1], ALU.mult, ALU.add, [b, A1, modT], [hT])
        ulist = list(range(12)) if own else [1, 6, 7, 8, 9]
        units = {}
        for n_, u in enumerate(ulist[:2]):
            units[u] = load_unit(win_d[:, u * 512:(u + 1) * 512])
        for n_, u in enumerate(ulist):
            if n_ + 2 < len(ulist):
                u2 = ulist[n_ + 2]
                units[u2] = load_unit(win_d[:, u2 * 512:(u2 + 1) * 512])
            un = units[u]

            def fm(cc, tb, b):
                for k in range(16):
                    kb.mm(b.ap, un[:, k, cc * 128:(cc + 1) * 128], hT[:, k, tb * 512:(tb + 1) * 512], k == 0, k == 15, [un, hT], [b])

            if u == 1:
                for o0 in (384, 448):
                    kb.ts(un[:, :, o0:o0 + 32], un[:, :, 288:320], -1.0, None, ALU.mult, None, [un], [un])
                    kb.cp("dve", un[:, :, o0 + 32:o0 + 64], un[:, :, 256:288], [un], [un])
            if u in (0, 1):
                for tb in range(2):
                    t0 = tok0 + tb * 512
                    banks = []
                    for cc in range(4):
                        b = kb.bank(0, 4)
                        fm(cc, tb, b)
                        banks.append(b)
                    if u == 0:
                        stats_norm(banks, 4, 512, qg, qnT, t0)
                    else:
                        stats_norm(banks[0:2], 2, 256, kvg, kvnT, t0)
                        kb.tt(tA.ap, banks[2].ap, cosT[:, tb * 512:(tb + 1) * 512], ALU.mult, [banks[2], cosT], [tA])
                        kb.tt(tB.ap, banks[3].ap, sinT[:, tb * 512:(tb + 1) * 512], ALU.mult, [banks[3], sinT], [tB])
                        kb.tt(krT[:, t0:t0 + 512], tA.ap, tB.ap, ALU.add, [tA, tB], [krT])
            elif u in (2, 3, 10, 11):
                for cc in range(4):
                    head = (u % 2) * 4 + cc
                    s = stg16[cnt["s16"] % 2]; lane = "s16_%d" % (cnt["s16"] % 2); cnt["s16"] += 1
                    for tb in range(2):
                        b = kb.bank(0, 4)
                        fm(cc, tb, b)
                        o = s[:, tb * 512:(tb + 1) * 512]
                        if u < 4:
                            kb.cp("act", o, b.ap, [b], [s])
                        else:
                            kb.act(tA.ap, b.ap, AF.Exp, [b], [tA], scale=-1.0)
                            kb.ts(tA.ap, tA.ap, 1.0, None, ALU.add, None, [tA], [tA])
                            kb.op("dve", lambda e: e.reciprocal(out=tA.ap, in_=tA.ap), [tA], [tA])
                            kb.tt(o, b.ap, tA.ap, ALU.mult, [b, tA], [s])
                    if u < 4:
                        kb.dma("sp", lane, hq_s[head][:, tok0:tok0 + 1024], s.ap, [s], [R_hq[head]])
                    else:
                        kb.dma("sp", lane, sg_s[head][:, tok0:tok0 + 1024], s.ap, [s], [R_sg[head]])
            elif u in (4, 5, 6, 7):
                for cc in range(4):
                    head = (u % 2) * 4 + cc
                    s = stg32[cnt["s32"] % 2]; lane = "s32_%d" % (cnt["s32"] % 2); cnt["s32"] += 1
                    for tb in range(2):
                        b = kb.bank(0, 4)
                        fm(cc, tb, b)
                        kb.cp("act" if tb == 0 else "dve", s[:, tb * 512:(tb + 1) * 512], b.ap, [b], [s])
                    if u < 6:
                        kb.dma("sp", lane, z1_s[head][:, tok0:tok0 + 1024], s.ap, [s], [R_z1[head]])
                    else:
                        kb.dma("sp", lane, z2_s[head][:, tok0:tok0 + 1024], s.ap, [s], [R_z2[head]])
            else:
                for i4 in range(2):
                    s = stgv[cnt["sv"] % 2]; lane = "sv_%d" % (cnt["sv"] % 2); cnt["sv"] += 1
                    for ii in range(4):
                        i = i4 * 4 + ii
                        b = kb.bank(0, 4)
                        for k in range(16):
                            kb.mm(b.ap, hT[:, k, i * 128:(i + 1) * 128], un[:, k, :], k == 0, k == 15, [un, hT], [b])
                        kb.cp("act" if ii % 2 == 0 else "dve", s[:, ii, :], b.ap, [b], [s])
                    r0 = tok0 + i4 * 512
                    kb.dma("sp", lane, v_s[r0:r0 + 512, (u - 8) * 512:(u - 7) * 512].rearrange("(t p) c -> p t c", p=128), s.ap, [s], [R_v])
    fw.barrier()
    if stage <= 1:
        return kb, dict(qnT=qnT, kvnT=kvnT, krT=krT, CS=CS)

    kb.top = mark_persist
    wq = kb.alloc("wq", [128, 4, 2048], BF16)
    wkv = kb.alloc("wkv", [128, 2, 2048], BF16)
    kb.dma("pool", "wq", wq.ap, wuq_d.rearrange("(k p) n -> p k n", p=128), [], [wq])
    kb.dma("pool", "wkv", wkv.ap, wukv_d.rearrange("(k p) n -> p k n", p=128), [], [wkv])
    wq4 = wq.ap.rearrange("p k (h c) -> p k h c", c=256)
    kb.ts(wq4[:, :, :, 192:224], wq4[:, :, :, 160:192], -1.0, None, ALU.mult, None, [wq], [wq])
    kb.cp("dve", wq4[:, :, :, 224:256], wq4[:, :, :, 128:160], [wq], [wq])
    Kt = kb.alloc("Kt", [128, TALL], BF16)
    Vh = kb.alloc("Vh", [128, 32, 128], BF16)
    Qn = kb.alloc("Qn", [128, TOWN], BF16)
    Qr = kb.alloc("Qr", [128, TOWN], BF16)
    Pt = [kb.alloc("Pt%d" % i, [128, 512], BF16) for i in range(4)]
    rl = kb.alloc("rl", [128, 512])
    cstg = [kb.alloc("cstg%d" % i, [128, TOWN], BF16) for i in range(2)]
    for h in range(8):
        c0 = h * 256
        for kg in range(8):
            b = kb.bank(0, 3)
            for c in range(2):
                kb.mm(b.ap, wkv[:, c, c0:c0 + 128], kvnT[:, c, kg * 512:(kg + 1) * 512], c == 0, c == 1, [wkv, kvnT], [b])
            kb.cp("act" if kg % 2 == 0 else "dve", Kt[:, kg * 512:(kg + 1) * 512], b.ap, [b], [Kt])
        for k4 in range(8):
            b = kb.bank(0, 3)
            for j in range(4):
                kbk = k4 * 4 + j
                for c in range(2):
                    kb.mm(b[:, j * 128:(j + 1) * 128], kvnT[:, c, kbk * 128:(kbk + 1) * 128], wkv[:, c, c0 + 128:c0 + 256], c == 0, c == 1, [wkv, kvnT], [b])
            kb.cp("act" if k4 % 2 == 0 else "dve", Vh[:, k4 * 4:(k4 + 1) * 4, :].rearrange("p a b -> p (a b)"), b.ap, [b], [Vh])
        for tb in range(4):
            b = kb.bank(0, 3)
            for c in range(4):
                kb.mm(b.ap, wq[:, c, c0:c0 + 128], qnT[:, c, tb * 512:(tb + 1) * 512], c == 0, c == 3, [wq, qnT], [b])
            kb.cp("act", Qn[:, tb * 512:(tb + 1) * 512], b.ap, [b], [Qn])
            b = kb.bank(0, 3)
            for c in range(4):
                kb.mm(b.ap, wq[:, c, c0 + 128:c0 + 256], qnT[:, c, tb * 512:(tb + 1) * 512], c == 0, c == 3, [wq, qnT], [b])
            kb.tt(Qr[:, tb * 512:(tb + 1) * 512], b.ap, CS[:, tb * 512:(tb + 1) * 512], ALU.mult, [b, CS], [Qr])
        cs_ = cstg[h % 2]
        for qgp in range(4):
            qs = slice(qgp * 512, (qgp + 1) * 512)
            Ob = T(kb.ps[3 + qgp % 2], kb.psr[3 + qgp % 2])
            Lb = T(kb.ps[5 + qgp % 2], kb.psr[5 + qgp % 2])

            def score(kbk):
                b = kb.bank(0, 3)
                kb.mm(b.ap, Kt[:, kbk * 128:(kbk + 1) * 128], Qn[:, qs], True, False, [Kt, Qn], [b])
                kb.mm(b.ap, krT[:, kbk * 128:(kbk + 1) * 128], Qr[:, qs], False, True, [krT, Qr], [b])
                return b
            sb_next = score(0)
            for kbk in range(32):
                sbk = sb_next
                if kbk + 1 < 32:
                    sb_next = score(kbk + 1)
                p = Pt[kbk % 4]
                kb.act(p.ap, sbk.ap, AF.Exp, [sbk], [p], scale=SCALE)
                kb.mm(Ob.ap, Vh[:, kbk, :], p.ap, kbk == 0, kbk == 31, [Vh, p], [Ob])
                kb.mm(Lb.ap, ones16.ap, p.ap, kbk == 0, kbk == 31, [ones16, p], [Lb])
            kb.op("dve", lambda e, Lb=Lb: e.reciprocal(out=rl.ap, in_=Lb.ap), [Lb], [rl])
            kb.tt(cs_[:, qs], Ob.ap, rl.ap, ALU.mult, [Ob, rl], [cs_])
        kb.dma("sp", "cstg%d" % (h % 2), cat_s[h], cs_.ap, [cs_], [R_cat[h]])
    fw.barrier()
    if stage <= 2:
        return kb, {}

    kb.top = mark_global
    segF = kb.alloc("segF", [128, TOWN]); segB = kb.alloc("segB", [128, TOWN])
    kb.op("pool", lambda e: e.memset(segF.ap, 1.0), [], [segF])
    kb.op("pool", lambda e: e.memset(segF.ap.rearrange("p (c t) -> p c t", t=64)[:, :, 0:1], 0.0), [segF], [segF])
    kb.op("pool", lambda e: e.memset(segB.ap, 1.0), [], [segB])
    kb.op("pool", lambda e: e.memset(segB.ap.rearrange("p (c t) -> p c t", t=64)[:, :, 63:64], 0.0), [segB], [segB])
    Z = [kb.alloc("Z%d" % i, [128, TOWN]) for i in range(3)]
    W2 = kb.alloc("W2", [128, TOWN]); W3 = kb.alloc("W3", [128, TOWN])
    hqt = kb.alloc("hqt", [128, TOWN], BF16)
    sgt = kb.alloc("sgt", [128, TOWN], BF16)
    vt = kb.alloc("vt", [128, 32, 128], BF16)
    qtl = [kb.alloc("qtl%d" % i, [128, TOWN], BF16) for i in range(2)]
    ktl = [kb.alloc("ktl%d" % i, [128, TOWN], BF16) for i in range(2)]
    ebl = [kb.alloc("ebl%d" % i, [128, 32]) for i in range(3)]
    Sall = [kb.alloc("Sall%d" % i, [128, 32, 128], BF16) for i in range(2)]
    Sst = [kb.alloc("Sst%d" % i, [128, 128]) for i in range(2)]
    khtok = [kb.alloc("khtok%d" % i, [128, 128], BF16) for i in range(2)]
    Am = [kb.alloc("Am%d" % i, [128, 128], BF16) for i in range(4)]
    rstg = [kb.alloc("rstg%d" % i, [128, TOWN], BF16) for i in range(2)]
    otmp = kb.alloc("otmp", [128, 512])
    sqD = kb.alloc("sqD", [128, 512]); rsbD = kb.alloc("rsbD", [128, 512])
    kcnt = [0]; acnt = [0]

    def chain(zt, dI, h, fwd, with_q):
        lb = lbT[:, 8 * dI + h:8 * dI + h + 1]; oml = omlT[:, 8 * dI + h:8 * dI + h + 1]
        seg = segF if fwd else segB
        kb.act(zt.ap, zt.ap, AF.Exp, [zt], [zt], scale=-1.0)
        kb.ts(zt.ap, zt.ap, 1.0, None, ALU.add, None, [zt], [zt])
        kb.op("dve", lambda e: e.reciprocal(out=zt.ap, in_=zt.ap), [zt], [zt])
        kb.ts(zt.ap, zt.ap, oml, lb, ALU.mult, ALU.add, [zt, lbT, omlT], [zt])
        kb.act(W2.ap, zt.ap, AF.Ln, [zt], [W2])
        if fwd:
            kb.op("dve", lambda e: e.tensor_tensor_scan(out=W3.ap, data0=seg.ap, data1=W2.ap, initial=0.0, op0=ALU.mult, op1=ALU.add), [seg, W2], [W3])
        else:
            kb.op("dve", lambda e: e.tensor_tensor_scan(out=W3[:, ::-1], data0=seg[:, ::-1], data1=W2[:, ::-1], initial=0.0, op0=ALU.mult, op1=ALU.add), [seg, W2], [W3])
        kb.ts(zt.ap, zt.ap, -1.0, 1.0, ALU.mult, ALU.add, [zt], [zt])
        kb.act(W2.ap, W3.ap, AF.Exp, [W3], [W2])
        kb.act(W3.ap, W3.ap, AF.Exp, [W3], [W3], scale=-1.0)
        return lb

    for h in range(8):
        kb.dma("sp", "dz0", Z[0].ap, z1_s[h], [R_z1[h]], [Z[0]])
        kb.dma("sp", "dz1", Z[1].ap, z2_s[h][:, 0:TOWN], [R_z2[h]], [Z[1]])
        kb.dma("sp", "dz2", Z[2].ap, z2_s[h][:, TOWN:TALL], [R_z2[h]], [Z[2]])
        kb.dma("sp", "dhq", hqt.ap, hq_s[h], [R_hq[h]], [hqt])
        kb.dma("sp", "dsg", sgt.ap, sg_s[h], [R_sg[h]], [sgt])
        kb.dma("sp", "dv", vt.ap, v_s[:, h * 128:(h + 1) * 128].rearrange("(t p) c -> p t c", p=128), [R_v], [vt])
        for zi, dI, fwd, with_q in ((0, 0, True, True), (1, 1, False, True), (2, 1, False, False)):
            zt = Z[zi]
            chain(zt, dI, h, fwd, with_q)
            e3 = W2.ap.rearrange("p (c t) -> p c t", t=64)
            col = 63 if fwd else 0
            kb.cp("dve", ebl[zi].ap, e3[:, :, col], [W2], [ebl[zi]])
            if with_q:
                kb.tt(qtl[dI].ap, hqt.ap, W2.ap, ALU.mult, [hqt, W2], [qtl[dI]])
            kb.tt(zt.ap, zt.ap, W3.ap, ALU.mult, [zt, W3], [zt])
            if with_q:
                kb.cp("act", ktl[dI].ap, zt.ap, [zt], [ktl[dI]])
            z3 = zt.ap.rearrange("p (c t) -> p c t", t=64)
            kb.tt(z3, z3, ebl[zi].ap.unsqueeze(2).broadcast_to([128, 32, 64]), ALU.mult, [zt, ebl[zi]], [zt])
        for dI, seq in ((0, [(0, i) for i in range(16)]), (1, [(2, i) for i in range(15, -1, -1)] + [(1, i) for i in range(15, -1, -1)])):
            fwd = dI == 0
            scur = Sst[0]; snext = Sst[1]
            kb.op("pool", lambda e, scur=scur: e.memset(scur.ap, 0.0), [], [scur])
            if fwd:
                kb.op("pool", lambda e: e.memset(Sall[0][:, 0, :], 0.0), [], [Sall[0]])
            for zi, i in seq:
                zt = Z[zi]
                voff = 16 if zi == 2 else 0
                b = kb.bank(0, 2)
                kb.tr(b[:, 0:128], zt[:, i * 128:(i + 1) * 128], [zt], [b])
                kt_ = khtok[kcnt[0] % 2]; kcnt[0] += 1
                kb.cp("act", kt_.ap, b[:, 0:128], [b], [kt_])
                for a in ((0, 1) if fwd else (1, 0)):
                    c = 2 * i + a
                    db = kb.bank(2, 4)
                    kb.mm(db[:, 0:128], kt_[a * 64:(a + 1) * 64, :], vt[a * 64:(a + 1) * 64, voff + i, :], True, True, [kt_, vt], [db])
                    kb.stt(snext.ap, scur.ap, ebl[zi][:, c:c + 1], db[:, 0:128], ALU.mult, ALU.add, [scur, ebl[zi], db], [snext])
                    scur, snext = snext, scur
                    if fwd:
                        if c + 1 < 32:
                            kb.cp("act", Sall[0][:, c + 1, :], scur.ap, [scur], [Sall[0]])
                    else:
                        if zi == 2 and c == 0:
                            kb.cp("act", Sall[1][:, 31, :], scur.ap, [scur], [Sall[1]])
                        elif zi == 1 and c >= 1:
                            kb.cp("act", Sall[1][:, c - 1, :], scur.ap, [scur], [Sall[1]])
        rs_ = rstg[h % 2]
        for i4 in range(4):
            ob = kb.bank(4, 6)
            for ii in range(4):
                i = i4 * 4 + ii
                ts_ = slice(i * 128, (i + 1) * 128)
                oc = ob[:, ii * 128:(ii + 1) * 128]
                ams = []
                for dI in range(2):
                    ab = kb.bank(0, 2)
                    kb.mm(ab[:, 0:128], ktl[dI][:, ts_], qtl[dI][:, ts_], True, True, [ktl[dI], qtl[dI]], [ab])
                    am = Am[acnt[0] % 4]; acnt[0] += 1
                    msk = maskF if dI == 0 else maskB
                    kb.tt(am.ap, ab[:, 0:128], msk.ap, ALU.mult, [ab, msk], [am])
                    ams.append(am)
                first = True
                for dI in range(2):
                    kb.mm(oc, vt[:, i, :], ams[dI].ap, first, False, [vt, ams[dI]], [ob])
                    first = False
                    for a in range(2):
                        c = 2 * i + a
                        last = (dI == 1 and a == 1)
                        kb.mm(ob[:, ii * 128 + a * 64:ii * 128 + (a + 1) * 64], Sall[dI][:, c, :], qtl[dI][:, i * 128 + a * 64:i * 128 + (a + 1) * 64],
                              False, last, [Sall[dI], qtl[dI]], [ob])
            s = sqD
            kb.act(s.ap, ob.ap, AF.Square, [ob], [s])
            sb = kb.bank(6, 8)
            kb.mm(sb.ap, ones32.ap, s.ap, True, True, [ones32, s], [sb])
            kb.cp("dve", rsbD.ap, sb.ap, [sb], [rsbD])
            kb.rsqrt_mean(rsbD.ap, 128, [rsbD])
            kb.stt(otmp.ap, ob.ap, hgn, rsbD.ap, ALU.mult, ALU.mult, [ob, rsbD, v2T], [otmp])
            kb.tt(rs_[:, i4 * 512:(i4 + 1) * 512], otmp.ap, sgt[:, i4 * 512:(i4 + 1) * 512], ALU.mult, [otmp, sgt], [rs_])
        kb.dma("sp", "rstg%d" % (h % 2), cat_s[8 + h], rs_.ap, [rs_], [R_cat[8 + h]])
    fw.barrier()
    if stage <= 3:
        return kb, {}

    kb.top = mark_global
    x1T = kb.alloc("x1T", [128, 16, 512])
    r32 = kb.alloc("r32", [128, 16, 512])
    r32b = r32.ap.rearrange("p a b -> p (a b)").bitcast(BF16)
    catg = T(r32b[:, 0:8192].rearrange("p (a b) -> p a b", a=16), r32.r)
    h2T = T(r32b[:, 8192:16384].rearrange("p (a b) -> p a b", a=16), r32.r)
    yT = r32
    uT = kb.alloc("uT", [128, 64, 512], BF16)
    mixT = T(uT.ap.rearrange("p a b -> p (a b)")[:, 0:16384].bitcast(F32).rearrange("p (a b) -> p a b", a=16), uT.r)
    xh = [kb.alloc("xh%d" % i, [128, 1024]) for i in range(2)]
    sq = [kb.alloc("sqE%d" % i, [128, 512]) for i in range(2)]
    rsE = kb.alloc("rsE", [128, 512])
    tE = kb.alloc("tE", [128, 512])
    rel = [kb.alloc("rel%d" % i, [128, 512], BF16) for i in range(2)]
    xcnt = [0]; scnt = [0]

    def stats(srcs_fn, n_chunks):
        sb = kb.bank(6, 8)
        for c in range(n_chunks):
            s = sq[scnt[0] % 2]; scnt[0] += 1
            src, rr = srcs_fn(c)
            kb.act(s.ap, src, AF.Square, rr, [s])
            kb.mm(sb.ap, ones32.ap, s.ap, c == 0, c == n_chunks - 1, [ones32, s], [sb])
        kb.cp("dve", rsE.ap, sb.ap, [sb], [rsE])
        kb.rsqrt_mean(rsE.ap, D, [rsE])

    for tg in range(4):
        t0 = tg * 512
        kb.dma("sp", "catg", catg.ap, cat_s[:, :, t0:t0 + 512].rearrange("c p t -> p c t"), R_cat, [catg])
        for j in range(4):
            for hf in range(2):
                xt = xh[xcnt[0] % 2]; lane = "xh%d" % (xcnt[0] % 2); xcnt[0] += 1
                r0 = t0 + j * 128
                kb.dma("sp", lane, xt.ap, x_d[r0:r0 + 128, hf * 1024:(hf + 1) * 1024], [], [xt])
                for c4 in range(2):
                    b = kb.bank(0, 4)
                    for jj in range(4):
                        kb.tr(b[:, jj * 128:(jj + 1) * 128], xt[:, (c4 * 4 + jj) * 128:(c4 * 4 + jj + 1) * 128], [xt], [b])
                    cb = hf * 8 + c4 * 4
                    kb.cp("act" if c4 == 0 else "dve", x1T[:, cb:cb + 4, j * 128:(j + 1) * 128],
                          b.ap.rearrange("p (a b) -> p a b", a=4), [b], [x1T])
        nxt = load_unit(wout_d[:, 0:512])
        for u in range(4):
            un = nxt
            if u + 1 < 4:
                nxt = load_unit(wout_d[:, (u + 1) * 512:(u + 2) * 512])
            for cc in range(4):
                c = u * 4 + cc
                b = kb.bank(0, 4)
                for k in range(16):
                    kb.mm(b.ap, un[:, k, cc * 128:(cc + 1) * 128], catg[:, k, :], k == 0, k == 15, [un, catg], [b])
                kb.cp("act", mixT[:, c, :], b.ap, [b], [mixT])
        stats(lambda c: (mixT[:, c, :], [mixT]), 16)
        for c in range(16):
            kb.stt(tE.ap, mixT[:, c, :], G1[:, c:c + 1], rsE.ap, ALU.mult, ALU.mult, [mixT, G1, rsE], [tE])
            kb.tt(x1T[:, c, :], x1T[:, c, :], tE.ap, ALU.add, [x1T, tE], [x1T])
        stats(lambda c: (x1T[:, c, :], [x1T]), 16)
        for c in range(16):
            kb.tt(tE.ap, x1T[:, c, :], rsE.ap, ALU.mult, [x1T, rsE], [tE])
            kb.act(h2T[:, c, :], tE.ap, AF.Identity, [tE, A2, modT], [h2T], scale=A2[:, c:c + 1], bias=shM[:, c:c + 1])
        nxt = load_unit(wup_d[:, 0:512])
        for u in range(16):
            un = nxt
            if u + 1 < 16:
                nxt = load_unit(wup_d[:, (u + 1) * 512:(u + 2) * 512])
            for cc in range(4):
                j = u * 4 + cc
                b = kb.bank(0, 4)
                for k in range(16):
                    kb.mm(b.ap, un[:, k, cc * 128:(cc + 1) * 128], h2T[:, k, :], k == 0, k == 15, [un, h2T], [b])
                r = rel[j % 2]
                kb.act(r.ap, b.ap, AF.Relu, [b], [r])
                kb.tt(uT[:, j, :], r.ap, r.ap, ALU.mult, [r], [uT])
        for cs in range(4):
            accs = [T(kb.ps[i], kb.psr[i]) for i in range(4)]
            nxt = load_unit(wdn_d[0:2048, cs * 512:(cs + 1) * 512])
            for rg in range(4):
                un = nxt
                if rg + 1 < 4:
                    nxt = load_unit(wdn_d[(rg + 1) * 2048:(rg + 2) * 2048, cs * 512:(cs + 1) * 512])
                for k in range(16):
                    j = rg * 16 + k
                    for cc in range(4):
                        kb.mm(accs[cc].ap, un[:, k, cc * 128:(cc + 1) * 128], uT[:, j, :], j == 0, j == 63, [un, uT], [accs[cc]])
            for cc in range(4):
                kb.cp("act" if cc % 2 == 0 else "dve", yT[:, cs * 4 + cc, :], accs[cc].ap, [accs[cc]], [yT])
        stats(lambda c: (yT[:, c, :], [yT]), 16)
        for c in range(16):
            kb.stt(tE.ap, yT[:, c, :], G2[:, c:c + 1], rsE.ap, ALU.mult, ALU.mult, [yT, G2, rsE], [tE])
            kb.tt(x1T[:, c, :], x1T[:, c, :], tE.ap, ALU.add, [x1T, tE], [x1T])
        for j in range(4):
            for hf in range(2):
                xt = xh[xcnt[0] % 2]; lane = "xh%d" % (xcnt[0] % 2); xcnt[0] += 1
                for c4 in range(2):
                    b = kb.bank(4, 6)
                    for jj in range(4):
                        c = hf * 8 + c4 * 4 + jj
                        kb.tr(b[:, jj * 128:(jj + 1) * 128], x1T[:, c, j * 128:(j + 1) * 128], [x1T], [b])
                    kb.cp("act" if c4 == 0 else "dve", xt[:, c4 * 512:(c4 + 1) * 512], b.ap, [b], [xt])
                r0 = t0 + j * 128
                kb.dma("sp", lane, out_d[r0:r0 + 128, hf * 1024:(hf + 1) * 1024], xt.ap, [xt], [R_out])
    fw.barrier()
    return kb, {}


_CACHE = {}


def _prep_inputs(inputs):
    f32 = np.float32
    x = np.asarray(inputs["x"], f32); c = np.asarray(inputs["c"], f32)
    positions = np.asarray(inputs["positions"]).astype(np.int32)
    w_in = np.asarray(inputs["w_in"], f32)[0]
    offs = [0, 512, 768, 832, 1856, 2880, 3904, 4928, 5952]
    seg = lambda i: w_in[:, offs[i]:offs[i + 1]]
    w_uq = np.asarray(inputs["w_uq"], f32)[0].reshape(512, 8, 192)
    w_uq_r = np.concatenate([w_uq[:, :, 0:128], w_uq[:, :, 128:192], w_uq[:, :, 128:192]], axis=2).reshape(512, 2048)
    invf = (10000.0 ** (-np.arange(32, dtype=f32) / 32)).astype(f32)
    lbf = np.asarray(inputs["hgrn_lb_logits_fwd"], f32); lbb = np.asarray(inputs["hgrn_lb_logits_bwd"], f32)
    vecs = np.concatenate([np.asarray(inputs["b_mod"], f32)[0].reshape(96, 128),
                           np.asarray(inputs["pre_mix_g"], f32)[0].reshape(16, 128),
                           np.asarray(inputs["post_mix_g"], f32)[0].reshape(16, 128)], axis=0)
    common = {
        "w_mod": np.ascontiguousarray(np.asarray(inputs["w_mod"], f32)[0]),
        "w_uq": np.ascontiguousarray(w_uq_r),
        "w_ukv": np.ascontiguousarray(np.asarray(inputs["w_ukv"], f32)[0]),
        "w_out": np.ascontiguousarray(np.asarray(inputs["w_out"], f32)[0]),
        "w_up": np.ascontiguousarray(np.asarray(inputs["w_up"], f32)[0]),
        "w_down": np.ascontiguousarray(np.asarray(inputs["w_down"], f32)[0]),
        "vecs": np.ascontiguousarray(vecs),
    }
    kr = seg(2)
    win_p = []
    for p in range(2):
        f1, f2 = (seg(4), seg(5)) if p == 0 else (seg(5), seg(4))
        win_p.append(np.ascontiguousarray(np.concatenate([seg(0), seg(1), kr, kr, kr, kr, seg(3), f1, f2, seg(6), seg(7)], axis=1)))
    maps = []
    for b in range(4):
        for p in range(2):
            l1, l2 = (lbf, lbb) if p == 0 else (lbb, lbf)
            v2 = np.zeros((128, 128), f32)
            v2[0:16] = np.asarray(inputs["pre_mlp_g"], f32)[0].reshape(16, 128)
            v2[16:32] = np.asarray(inputs["post_mlp_g"], f32)[0].reshape(16, 128)
            v2[32:48] = c[b].reshape(16, 128)
            v2[48:56] = l1[0].reshape(8, 128); v2[56:64] = l1[1].reshape(8, 128)
            v2[64:72] = l2[0].reshape(8, 128); v2[72:80] = l2[1].reshape(8, 128)
            v2[80:84] = np.asarray(inputs["q_norm_g"], f32)[0].reshape(4, 128)
            v2[84:86] = np.asarray(inputs["kv_norm_g"], f32)[0].reshape(2, 128)
            v2[86] = np.asarray(inputs["hgrn_norm_g"], f32)[0]
            v2[87] = np.tile(invf, 4)
            xs = x[b] if p == 0 else x[b][::-1]
            ps = positions[b] if p == 0 else positions[b][::-1]
            m = dict(common)
            m["x"] = np.ascontiguousarray(xs)
            m["pos"] = np.ascontiguousarray(np.broadcast_to(ps[None, :], (128, TALL))).astype(np.int32)
            m["vecs2"] = v2
            m["w_in"] = win_p[p]
            maps.append(m)
    return maps


def kernel(**inputs):
    if "nc" not in _CACHE:
        kb, _ = build_program()
        kb.fw.emit()
        kb.fw.close()
        _CACHE["nc"] = kb.nc
    nc = _CACHE["nc"]
    maps = _prep_inputs(inputs)
    res = run_bass_kernel_spmd(nc, maps, core_ids=list(range(8)))
    out = np.empty((4, TALL, D), np.float32)
    for b in range(4):
        for p in range(2):
            o = np.asarray(res.results[b * 2 + p]["out"], np.float32)
            if p == 0:
                out[b, 0:TOWN] = o
            else:
                out[b, TOWN:TALL] = o[::-1]
    return out
```

```python
import math
import contextlib
import numpy as np
import concourse.bass as bass
import concourse.mybir as mybir
from concourse.bass_utils import run_bass_kernel_spmd

F32 = mybir.dt.float32
BF16 = mybir.dt.bfloat16
I32 = mybir.dt.int32
AF = mybir.ActivationFunctionType
ALU = mybir.AluOpType

ENGS = ("pe", "act", "dve", "pool", "sp")


class Res:
    __slots__ = ("name", "w", "rs")

    def __init__(self, name):
        self.name = name
        self.w = None
        self.rs = []


class Op:
    __slots__ = ("eng", "fn", "deps", "idx", "signal", "sigval", "dma_lane", "dma_val")

    def __init__(self, eng, fn):
        self.eng = eng
        self.fn = fn
        self.deps = []
        self.idx = None
        self.signal = False
        self.sigval = None
        self.dma_lane = None
        self.dma_val = None


class FW:
    def __init__(self, nc):
        self.nc = nc
        self.ops = {e: [] for e in ENGS}
        self.eh = {"pe": nc.tensor, "act": nc.scalar, "dve": nc.vector, "pool": nc.gpsimd, "sp": nc.sync}
        self.lanes = {}
        self.lane_sems = {}
        self.eng_sems = {}
        self.waited = {e: {} for e in ENGS}
        self.stack = contextlib.ExitStack()

    def res(self, name):
        return Res(name)

    def _tok_needed(self, eng, tok):
        key = (tok[0], tok[1])
        cur = self.waited[eng].get(key, -1)
        if tok[2] <= cur:
            return False
        self.waited[eng][key] = tok[2]
        return True

    def _add(self, eng, fn, reads, writes, dma_lane=None):
        op = Op(eng, fn)
        op.idx = len(self.ops[eng])
        toks = []
        for r in reads:
            if r.w is not None:
                toks.append(r.w)
        for w in writes:
            if w.w is not None:
                toks.append(w.w)
            toks.extend(w.rs)
        for tok in toks:
            if tok[0] == "e" and tok[1] == eng and eng == "pe":
                continue
            if tok[0] == "e" and tok[1] == eng and tok[2] >= op.idx:
                continue
            if self._tok_needed(eng, tok):
                op.deps.append(tok)
        if dma_lane is not None:
            self.lanes[dma_lane] = self.lanes.get(dma_lane, 0) + 1
            op.dma_lane = dma_lane
            op.dma_val = self.lanes[dma_lane] * 16
            mytok = ("d", dma_lane, op.dma_val)
        else:
            mytok = ("e", eng, op.idx)
        for r in reads:
            r.rs.append(mytok)
        for w in writes:
            w.w = mytok
            w.rs = []
        self.ops[eng].append(op)
        return op

    def op(self, eng, fn, reads=(), writes=()):
        return self._add(eng, fn, list(reads), list(writes))

    def dma(self, q, lane, out, in_, reads=(), writes=(), **kw):
        def fn(e):
            return e.dma_start(out=out, in_=in_, **kw)
        return self._add(q, fn, list(reads), list(writes), dma_lane=lane)

    def barrier(self, engs=ENGS):
        toks = []
        for e in ENGS:
            if self.ops[e]:
                toks.append(("e", e, len(self.ops[e]) - 1))
        for lane, cnt in self.lanes.items():
            toks.append(("d", lane, cnt * 16))
        for e in engs:
            op = Op(e, None)
            op.idx = len(self.ops[e])
            for tok in toks:
                if tok[0] == "e" and tok[1] == e:
                    continue
                if self._tok_needed(e, tok):
                    op.deps.append(tok)
            self.ops[e].append(op)

    def emit(self):
        nc = self.nc
        for e in ENGS:
            for op in self.ops[e]:
                for tok in op.deps:
                    if tok[0] == "e":
                        self.ops[tok[1]][tok[2]].signal = True
        for e in ENGS:
            cnt = 0
            ops = self.ops[e]
            for i, op in enumerate(ops):
                if op.signal and (op.fn is None or op.dma_lane is not None):
                    j = i - 1
                    while j >= 0 and (ops[j].fn is None or ops[j].dma_lane is not None):
                        j -= 1
                    op.signal = False
                    if j >= 0:
                        ops[j].signal = True
                        op.sigval = ("alias", j)
                    else:
                        op.sigval = ("zero",)
            for op in ops:
                if op.signal:
                    cnt += 1
                    op.sigval = cnt
            for op in ops:
                if isinstance(op.sigval, tuple):
                    op.sigval = ops[op.sigval[1]].sigval if op.sigval[0] == "alias" else 0
        for e in ENGS:
            self.eng_sems[e] = self.stack.enter_context(nc.semaphore("s_" + e))
        for lane in self.lanes:
            self.lane_sems[lane] = self.stack.enter_context(nc.semaphore("l_" + lane))
        block = self.stack.enter_context(nc.Block())

        def run(e, eh):
            for op in self.ops[e]:
                for tok in op.deps:
                    if tok[0] == "e":
                        v = self.ops[tok[1]][tok[2]].sigval
                        if v:
                            eh.wait_ge(self.eng_sems[tok[1]], v)
                    else:
                        eh.wait_ge(self.lane_sems[tok[1]], tok[2])
                if op.fn is None:
                    continue
                ins = op.fn(eh)
                if op.dma_lane is not None:
                    ins.then_inc(self.lane_sems[op.dma_lane], 16)
                elif op.signal:
                    ins.then_inc(self.eng_sems[e], 1)

        @block.tensor
        def _(eh):
            run("pe", eh)

        @block.scalar
        def _(eh):
            run("act", eh)

        @block.vector
        def _(eh):
            run("dve", eh)

        @block.gpsimd
        def _(eh):
            run("pool", eh)

        @block.sync
        def _(eh):
            run("sp", eh)

    def close(self):
        self.stack.close()


D = 2048
TOWN = 2048
TALL = 4096
SCALE = 192 ** -0.5
EPS = 1e-6
TWO_PI = 2.0 * math.pi


class T:
    def __init__(self, ap, r):
        self.ap = ap
        self.r = r

    def __getitem__(self, k):
        return self.ap[k]


def _rs(lst):
    out = []
    for x in lst:
        if x is None:
            continue
        out.append(x.r if isinstance(x, T) else x)
    return out


class KB:
    def __init__(self, stage=99):
        self.stage = stage
        nc = self.nc = bass.Bass("TRN2", target_bir_lowering=False)
        self.fw = FW(nc)
        n32 = (nc.sbuf_bytes_remaining - 512) // 4
        self.arena = self.fw.stack.enter_context(nc.sbuf_tensor("arena", [128, n32], F32))
        self.n32 = n32
        self.top = 0
        self.ps = [self.fw.stack.enter_context(nc.psum_tensor("ps%d" % i, [128, 512], F32))[:, :] for i in range(8)]
        self.psr = [self.fw.res("ps%d" % i) for i in range(8)]
        self.rot = 0

    def alloc(self, name, shape, dt=F32):
        n = 1
        for s in shape[1:]:
            n *= s
        words = n if dt != BF16 else (n + 1) // 2
        words = (words + 7) // 8 * 8
        assert self.top + words <= self.n32, ("SBUF overflow", name, self.top, words, self.n32)
        ap = self.arena[:, self.top:self.top + words]
        if dt == BF16:
            ap = ap.bitcast(BF16)[:, 0:n]
        elif dt == I32:
            ap = ap.bitcast(I32)[:, 0:n]
        else:
            ap = ap[:, 0:n]
        self.top += words
        if len(shape) == 3:
            ap = ap.rearrange("p (a b) -> p a b", a=shape[1])
        if shape[0] != 128:
            ap = ap[0:shape[0]]
        return T(ap, self.fw.res(name))

    def op(self, eng, fn, reads, writes):
        self.fw.op(eng, fn, _rs(reads), _rs(writes))

    def act(self, out, in_, func, reads, writes, scale=1.0, bias=0.0, accum=None):
        self.op("act", lambda e: e.activation(out=out, in_=in_, func=func, bias=bias, scale=scale, accum_out=accum), reads, writes)

    def ts(self, out, in0, s1, s2, op0, op1, reads, writes, eng="dve"):
        if op1 is None:
            self.op(eng, lambda e: e.tensor_scalar(out=out, in0=in0, scalar1=s1, scalar2=None, op0=op0), reads, writes)
        else:
            self.op(eng, lambda e: e.tensor_scalar(out=out, in0=in0, scalar1=s1, scalar2=s2, op0=op0, op1=op1), reads, writes)

    def tt(self, out, in0, in1, op, reads, writes, eng="dve"):
        self.op(eng, lambda e: e.tensor_tensor(out=out, in0=in0, in1=in1, op=op), reads, writes)

    def stt(self, out, in0, scalar, in1, op0, op1, reads, writes):
        self.op("dve", lambda e: e.scalar_tensor_tensor(out=out, in0=in0, scalar=scalar, in1=in1, op0=op0, op1=op1), reads, writes)

    def cp(self, eng, out, in_, reads, writes):
        if eng == "act":
            self.op("act", lambda e: e.copy(out=out, in_=in_), reads, writes)
        else:
            self.op(eng, lambda e: e.tensor_copy(out=out, in_=in_), reads, writes)

    def mm(self, out, lhsT, rhs, start, stop, reads, writes):
        self.op("pe", lambda e: e.matmul(out, lhsT=lhsT, rhs=rhs, start=start, stop=stop), reads, writes)

    def tr(self, out, in_, reads, writes):
        ident = self.ident
        self.op("pe", lambda e: e.transpose(out=out, in_=in_, identity=ident.ap), list(reads) + [ident], writes)

    def dma(self, q, lane, out, in_, reads, writes):
        self.fw.dma(q, lane, out, in_, _rs(reads), _rs(writes))

    def bank(self, lo=0, hi=8):
        n = hi - lo
        i = lo + (self.rot % n)
        self.rot += 1
        return T(self.ps[i], self.psr[i])

    def rsqrt_mean(self, t, n, reads_writes):
        self.ts(t, t, 1.0 / n, EPS, ALU.mult, ALU.add, reads_writes, reads_writes)
        self.act(t, t, AF.Ln, reads_writes, reads_writes)
        self.act(t, t, AF.Exp, reads_writes, reads_writes, scale=-0.5)


def build_program(stage=99):
    kb = KB(stage)
    nc, fw = kb.nc, kb.fw

    def din(name, shape, dt=F32):
        return nc.dram_tensor(name, shape, dt, kind="ExternalInput").ap()

    def dscr(name, shape, dt):
        return nc.dram_tensor(name, shape, dt, kind="Internal").ap()

    x_d = din("x", [TALL, D])
    pos_d = din("pos", [128, TALL], I32)
    vecs_d = din("vecs", [128, 128])
    vecs2_d = din("vecs2", [128, 128])
    wmod_d = din("w_mod", [D, 6 * D])
    win_d = din("w_in", [D, 6144])
    wuq_d = din("w_uq", [512, 2048])
    wukv_d = din("w_ukv", [256, 2048])
    wout_d = din("w_out", [D, D])
    wup_d = din("w_up", [D, 4 * D])
    wdn_d = din("w_down", [4 * D, D])
    out_d = nc.dram_tensor("out", [TOWN, D], F32, kind="ExternalOutput").ap()

    hq_s = dscr("hq_s", [8, 128, TOWN], BF16)
    z1_s = dscr("z1_s", [8, 128, TOWN], F32)
    z2_s = dscr("z2_s", [8, 128, TALL], F32)
    v_s = dscr("v_s", [TALL, 1024], BF16)
    sg_s = dscr("sg_s", [8, 128, TOWN], BF16)
    cat_s = dscr("cat_s", [16, 128, TOWN], BF16)
    R_hq = [fw.res("hq%d" % h) for h in range(8)]
    R_z1 = [fw.res("z1%d" % h) for h in range(8)]
    R_z2 = [fw.res("z2%d" % h) for h in range(8)]
    R_v = fw.res("v_s")
    R_sg = [fw.res("sg%d" % h) for h in range(8)]
    R_cat = [fw.res("cat%d" % c) for c in range(16)]
    R_out = fw.res("out")

    ident = kb.alloc("ident", [128, 128]); kb.ident = ident
    ones32 = kb.alloc("ones32", [128, 128])
    ones16 = kb.alloc("ones16", [128, 128], BF16)
    maskF = kb.alloc("maskF", [128, 128])
    maskB = kb.alloc("maskB", [128, 128])
    vT = kb.alloc("vT", [128, 128])
    v2T = kb.alloc("v2T", [128, 128])
    modT = kb.alloc("modT", [128, 96])
    A1 = kb.alloc("A1", [128, 16]); G1 = kb.alloc("G1", [128, 16]); A2 = kb.alloc("A2", [128, 16]); G2 = kb.alloc("G2", [128, 16])
    lbT = kb.alloc("lbT", [128, 16]); omlT = kb.alloc("omlT", [128, 16])
    cond = kb.alloc("cond", [128, 16], BF16)
    mark_small = kb.top
    qnT = kb.alloc("qnT", [128, 4, TOWN], BF16)
    kvnT = kb.alloc("kvnT", [128, 2, TALL], BF16)
    krT = kb.alloc("krT", [128, TALL], BF16)
    CS = kb.alloc("CS", [128, TOWN])
    mark_cd = kb.top
    ring = [kb.alloc("ring%d" % i, [128, 16, 512], BF16) for i in range(3)]
    ringn = [0]

    def load_unit(src_ap):
        i = ringn[0] % 3
        ringn[0] += 1
        kb.dma("pool", "ring%d" % i, ring[i].ap, src_ap.rearrange("(k p) n -> p k n", p=128), [], [ring[i]])
        return ring[i]

    kb.op("pool", lambda e: e.memset(ident.ap, 0.0), [], [ident])
    kb.op("pool", lambda e: e.affine_select(out=ident.ap, in_=ident.ap, compare_op=ALU.not_equal, fill=1.0, base=0,
                                            pattern=[[-1, 128]], channel_multiplier=1), [ident], [ident])
    kb.op("pool", lambda e: e.memset(ones32.ap, 1.0), [], [ones32])
    kb.op("pool", lambda e: e.memset(ones16.ap, 1.0), [], [ones16])
    kb.op("pool", lambda e: e.memset(maskF.ap, 1.0), [], [maskF])
    kb.op("pool", lambda e: e.affine_select(out=maskF.ap, in_=maskF.ap, compare_op=ALU.is_ge, fill=0.0, base=0,
                                            pattern=[[1, 128]], channel_multiplier=-1), [maskF], [maskF])
    kb.op("pool", lambda e: e.memset(maskF[0:64, 64:128], 0.0), [maskF], [maskF])
    kb.op("pool", lambda e: e.memset(maskB.ap, 1.0), [], [maskB])
    kb.op("pool", lambda e: e.affine_select(out=maskB.ap, in_=maskB.ap, compare_op=ALU.is_ge, fill=0.0, base=0,
                                            pattern=[[-1, 128]], channel_multiplier=1), [maskB], [maskB])
    kb.op("pool", lambda e: e.memset(maskB[64:128, 0:64], 0.0), [maskB], [maskB])

    mark_global = kb.top

    tmpv = kb.alloc("tmpv", [128, 128])
    for src, dst in ((vecs_d, vT), (vecs2_d, v2T)):
        kb.dma("sp", "ltmpv", tmpv.ap, src, [], [tmpv])
        b = kb.bank()
        kb.tr(b[:, 0:128], tmpv.ap, [tmpv], [b])
        kb.cp("dve", dst.ap, b[:, 0:128], [b], [dst])
    e16 = kb.alloc("e16", [128, 16])
    cT = v2T[:, 32:48]
    kb.act(e16.ap, cT, AF.Exp, [v2T], [e16], scale=-1.0)
    kb.ts(e16.ap, e16.ap, 1.0, None, ALU.add, None, [e16], [e16])
    kb.op("dve", lambda e: e.reciprocal(out=e16.ap, in_=e16.ap), [e16], [e16])
    kb.tt(cond.ap, cT, e16.ap, ALU.mult, [v2T, e16], [cond])
    pm = kb.bank()
    NU = 8
    units = [None] * NU
    units[0] = load_unit(wmod_d[:, 0:512])
    units[1] = load_unit(wmod_d[:, 512:1024])
    for u in range(NU):
        if u + 2 < NU:
            units[u + 2] = load_unit(wmod_d[:, (u + 2) * 512:(u + 3) * 512])
        un = units[u]
        for cc in range(4):
            j = u * 4 + cc
            for k in range(16):
                kb.mm(pm[:, j:j + 1], un[:, k, cc * 128:(cc + 1) * 128], cond[:, k:k + 1], k == 0, k == 15, [un, cond], [pm])
    kb.tt(modT[:, 0:32], pm[:, 0:32], vT[:, 0:32], ALU.add, [pm, vT], [modT])
    shA = modT[:, 0:16]; shM = modT[:, 48:64]
    kb.stt(A1.ap, modT[:, 16:32], 1.0, vT[:, 96:112], ALU.add, ALU.mult, [modT, vT], [A1])
    for dI in range(2):
        l0 = v2T[:, 48 + 16 * dI:56 + 16 * dI]; l1 = v2T[:, 56 + 16 * dI:64 + 16 * dI]
        o = lbT[:, 8 * dI:8 * dI + 8]
        kb.tt(o, l1, l0, ALU.subtract, [v2T], [lbT])
        kb.act(o, o, AF.Exp, [lbT], [lbT])
        kb.ts(o, o, 1.0, None, ALU.add, None, [lbT], [lbT])
        kb.op("dve", lambda e, o=o: e.reciprocal(out=o, in_=o), [lbT], [lbT])
    kb.ts(omlT.ap, lbT.ap, -1.0, 1.0, ALU.mult, ALU.add, [lbT], [omlT])
    qg = v2T[:, 80:84]; kvg = v2T[:, 84:86]; hgn = v2T[:, 86:87]; invf = v2T[:, 87:88]

    kb.top = mark_global + 0
    tmpv = None
    hT = kb.alloc("hT", [128, 16, 1024], BF16)
    xb = [kb.alloc("xb%d" % i, [128, D]) for i in range(2)]
    junk = kb.alloc("junk", [128, D], BF16)
    ss1 = kb.alloc("ss1", [128, 1])
    posi = kb.alloc("posi", [128, 1024], I32)
    ang = kb.alloc("ang", [128, 1024])
    kf = kb.alloc("kf", [128, 1024])
    kiT = kb.alloc("kiT", [128, 1024], I32)
    cosT = kb.alloc("cosT", [128, 1024]); sinT = kb.alloc("sinT", [128, 1024])
    sq = [kb.alloc("sq%d" % i, [128, 512]) for i in range(2)]
    rsb = kb.alloc("rsb", [128, 512])
    tA = kb.alloc("tA", [128, 512]); tB = kb.alloc("tB", [128, 512])
    stg32 = [kb.alloc("stg32_%d" % i, [128, 1024]) for i in range(2)]
    stg16 = [kb.alloc("stg16_%d" % i, [128, 1024], BF16) for i in range(2)]
    stgv = [kb.alloc("stgv%d" % i, [128, 4, 512], BF16) for i in range(2)]
    cnt = {"s32": 0, "s16": 0, "sv": 0, "sq": 0}

    def sincos(dst, shift):
        kb.ts(kf.ap, ang.ap, shift, 1.0 / TWO_PI, ALU.add, ALU.mult, [ang], [kf])
        kb.cp("dve", kiT.ap, kf.ap, [kf], [kiT])
        kb.cp("dve", kf.ap, kiT.ap, [kiT], [kf])
        kb.stt(dst.ap, kf.ap, -TWO_PI, ang.ap, ALU.mult, ALU.add, [kf, ang], [dst])
        kb.ts(dst.ap, dst.ap, shift, None, ALU.add, None, [dst], [dst])
        kb.ts(kf.ap, dst.ap, math.pi, TWO_PI, ALU.is_gt, ALU.mult, [dst], [kf])
        kb.tt(dst.ap, dst.ap, kf.ap, ALU.subtract, [dst, kf], [dst])
        kb.ts(kf.ap, dst.ap, -math.pi, TWO_PI, ALU.is_lt, ALU.mult, [dst], [kf])
        kb.tt(dst.ap, dst.ap, kf.ap, ALU.add, [dst, kf], [dst])
        kb.act(dst.ap, dst.ap, AF.Sin, [dst], [dst])

    def stats_norm(banks, ncc, nfeat, gcols, dstT, t0):
        sb = kb.bank(4, 8)
        for cc in range(ncc):
            s = sq[cnt["sq"] % 2]; cnt["sq"] += 1
            kb.act(s.ap, banks[cc].ap, AF.Square, [banks[cc]], [s])
            kb.mm(sb.ap, ones32.ap, s.ap, cc == 0, cc == ncc - 1, [ones32, s], [sb])
        kb.cp("dve", rsb.ap, sb.ap, [sb], [rsb])
        kb.rsqrt_mean(rsb.ap, nfeat, [rsb])
        for cc in range(ncc):
            kb.stt(dstT[:, cc, t0:t0 + 512], banks[cc].ap, gcols[:, cc:cc + 1], rsb.ap, ALU.mult, ALU.mult,
                   [banks[cc], rsb, v2T], [dstT])

    for g in range(4):
        own = g < 2
        tok0 = g * 1024
        kb.dma("sp", "lposi", posi.ap, pos_d[:, tok0:tok0 + 1024], [], [posi])
        kb.cp("dve", ang.ap, posi.ap, [posi], [ang])
        kb.ts(ang.ap, ang.ap, invf, None, ALU.mult, None, [ang, v2T], [ang])
        sincos(sinT, 0.0)
        sincos(cosT, math.pi / 2)
        if own:
            kb.cp("act", CS[0:64, tok0:tok0 + 1024], cosT[0:64, :], [cosT], [CS])
            kb.cp("act", CS[64:128, tok0:tok0 + 1024], sinT[64:128, :], [sinT], [CS])
        for i in range(8):
            xt = xb[i % 2]
            r0 = tok0 + i * 128
            kb.dma("sp", "x%d" % (i % 2), xt.ap, x_d[r0:r0 + 128, :], [], [xt])
            kb.act(junk.ap, xt.ap, AF.Square, [xt], [junk, ss1], accum=ss1.ap)
            kb.rsqrt_mean(ss1.ap, D, [ss1])
            kb.ts(xt.ap, xt.ap, ss1[:, 0:1], None, ALU.mult, None, [xt, ss1], [xt])
            for c4 in range(4):
                b = kb.bank(0, 4)
                for j in range(4):
                    c = c4 * 4 + j
                    kb.tr(b[:, j * 128:(j + 1) * 128], xt[:, c * 128:(c + 1) * 128], [xt], [b])
                for j in range(4):
                    c = c4 * 4 + j
                    o = hT[:, c, i * 128:(i + 1) * 128]
                    if j % 2 == 0:
                        kb.act(o, b[:, j * 128:(j + 1) * 128], AF.Identity, [b, A1, modT], [hT], scale=A1[:, c:c + 1], bias=shA[:, c:c + 1])
                    else:
                        kb.ts(o, b[:, j * 128:(j + 1) * 128], A1[:, c:c + 1], shA[:, c:c + 1], ALU.mult, ALU.add, [b, A1, modT], [hT])
        ulist = list(range(12)) if own else [1, 6, 7, 8, 9]
        units = {}
        for n_, u in enumerate(ulist[:2]):
            units[u] = load_unit(win_d[:, u * 512:(u + 1) * 512])
        for n_, u in enumerate(ulist):
            if n_ + 2 < len(ulist):
                u2 = ulist[n_ + 2]
                units[u2] = load_unit(win_d[:, u2 * 512:(u2 + 1) * 512])
            un = units[u]

            def fm(cc, tb, b):
                for k in range(16):
                    kb.mm(b.ap, un[:, k, cc * 128:(cc + 1) * 128], hT[:, k, tb * 512:(tb + 1) * 512], k == 0, k == 15, [un, hT], [b])

            if u == 1:
                for o0 in (384, 448):
                    kb.ts(un[:, :, o0:o0 + 32], un[:, :, 288:320], -1.0, None, ALU.mult, None, [un], [un])
                    kb.cp("dve", un[:, :, o0 + 32:o0 + 64], un[:, :, 256:288], [un], [un])
            if u in (0, 1):
                for tb in range(2):
                    t0 = tok0 + tb * 512
                    banks = []
                    for cc in range(4):
                        b = kb.bank(0, 4)
                        fm(cc, tb, b)
                        banks.append(b)
                    if u == 0:
                        stats_norm(banks, 4, 512, qg, qnT, t0)
                    else:
                        stats_norm(banks[0:2], 2, 256, kvg, kvnT, t0)
                        kb.tt(tA.ap, banks[2].ap, cosT[:, tb * 512:(tb + 1) * 512], ALU.mult, [banks[2], cosT], [tA])
                        kb.tt(tB.ap, banks[3].ap, sinT[:, tb * 512:(tb + 1) * 512], ALU.mult, [banks[3], sinT], [tB])
                        kb.tt(krT[:, t0:t0 + 512], tA.ap, tB.ap, ALU.add, [tA, tB], [krT])
            elif u in (2, 3, 10, 11):
                for cc in range(4):
                    head = (u % 2) * 4 + cc
                    s = stg16[cnt["s16"] % 2]; lane = "s16_%d" % (cnt["s16"] % 2); cnt["s16"] += 1
                    for tb in range(2):
                        b = kb.bank(0, 4)
                        fm(cc, tb, b)
                        o = s[:, tb * 512:(tb + 1) * 512]
                        if u < 4:
                            kb.cp("act", o, b.ap, [b], [s])
                        else:
                            kb.act(tA.ap, b.ap, AF.Exp, [b], [tA], scale=-1.0)
                            kb.ts(tA.ap, tA.ap, 1.0, None, ALU.add, None, [tA], [tA])
                            kb.op("dve", lambda e: e.reciprocal(out=tA.ap, in_=tA.ap), [tA], [tA])
                            kb.tt(o, b.ap, tA.ap, ALU.mult, [b, tA], [s])
                    if u < 4:
                        kb.dma("sp", lane, hq_s[head][:, tok0:tok0 + 1024], s.ap, [s], [R_hq[head]])
                    else:
                        kb.dma("sp", lane, sg_s[head][:, tok0:tok0 + 1024], s.ap, [s], [R_sg[head]])
            elif u in (4, 5, 6, 7):
                for cc in range(4):
                    head = (u % 2) * 4 + cc
                    s = stg32[cnt["s32"] % 2]; lane = "s32_%d" % (cnt["s32"] % 2); cnt["s32"] += 1
                    for tb in range(2):
                        b = kb.bank(0, 4)
                        fm(cc, tb, b)
                        kb.cp("act" if tb == 0 else "dve", s[:, tb * 512:(tb + 1) * 512], b.ap, [b], [s])
                    if u < 6:
                        kb.dma("sp", lane, z1_s[head][:, tok0:tok0 + 1024], s.ap, [s], [R_z1[head]])
                    else:
                        kb.dma("sp", lane, z2_s[head][:, tok0:tok0 + 1024], s.ap, [s], [R_z2[head]])
            else:
                for i4 in range(2):
                    s = stgv[cnt["sv"] % 2]; lane = "sv_%d" % (cnt["sv"] % 2); cnt["sv"] += 1
                    for ii in range(4):
                        i = i4 * 4 + ii
                        b = kb.bank(0, 4)
                        for k in range(16):
                            kb.mm(b.ap, hT[:, k, i * 128:(i + 1) * 128], un[:, k, :], k == 0, k == 15, [un, hT], [b])
                        kb.cp("act" if ii % 2 == 0 else "dve", s[:, ii, :], b.ap, [b], [s])
                    r0 = tok0 + i4 * 512
                    kb.dma("sp", lane, v_s[r0:r0 + 512, (u - 8) * 512:(u - 7) * 512].rearrange("(t p) c -> p t c", p=128), s.ap, [s], [R_v])
    fw.barrier()
    if stage <= 1:
        return kb, dict(qnT=qnT, kvnT=kvnT, krT=krT, CS=CS)

    kb.top = mark_cd
    wqh = [kb.alloc("wqh%d" % i, [128, 4, 256], BF16) for i in range(2)]
    wkvh = [kb.alloc("wkvh%d" % i, [128, 2, 256], BF16) for i in range(2)]
    Kt = kb.alloc("Kt", [128, TALL], BF16)
    Vh = kb.alloc("Vh", [128, 32, 128], BF16)
    Qn = kb.alloc("Qn", [128, TOWN], BF16)
    Qr = kb.alloc("Qr", [128, TOWN], BF16)
    Pt = [kb.alloc("Pt%d" % i, [128, 512], BF16) for i in range(3)]
    rl = kb.alloc("rl", [128, 512])
    cst = [kb.alloc("cst%d" % i, [128, 512], BF16) for i in range(2)]
    mring = [kb.alloc("mring%d" % i, [128, 16, 128], BF16) for i in range(2)]
    modraw = kb.alloc("modraw", [128, 64])
    segF = kb.alloc("segF", [128, 1024]); segB = kb.alloc("segB", [128, 1024])
    Z = [kb.alloc("Z%d" % i, [128, TOWN]) for i in range(3)]
    W2 = kb.alloc("W2", [128, 1024]); W3 = kb.alloc("W3", [128, 1024])
    hqt = kb.alloc("hqt", [128, TOWN], BF16)
    sgt = kb.alloc("sgt", [128, TOWN], BF16)
    vt = kb.alloc("vt", [128, 32, 128], BF16)
    qtl = [kb.alloc("qtl%d" % i, [128, TOWN], BF16) for i in range(2)]
    ktl = [kb.alloc("ktl%d" % i, [128, TOWN], BF16) for i in range(2)]
    ebl = [kb.alloc("ebl%d" % i, [128, 32]) for i in range(3)]
    Sall = [kb.alloc("Sall%d" % i, [128, 32, 128], BF16) for i in range(2)]
    Sst = [kb.alloc("Sst%d" % i, [128, 128]) for i in range(2)]
    khtok = [kb.alloc("khtok%d" % i, [128, 128], BF16) for i in range(3)]
    Am = [kb.alloc("Am%d" % i, [128, 128], BF16) for i in range(6)]
    rst = [kb.alloc("rst%d" % i, [128, 512], BF16) for i in range(2)]
    otmp = kb.alloc("otmp", [128, 512])
    sqD = kb.alloc("sqD", [128, 512]); rsbD = kb.alloc("rsbD", [128, 512])
    kb.op("pool", lambda e: e.memset(segF.ap, 1.0), [], [segF])
    kb.op("pool", lambda e: e.memset(segF.ap.rearrange("p (c t) -> p c t", t=64)[:, :, 0:1], 0.0), [segF], [segF])
    kb.op("pool", lambda e: e.memset(segB.ap, 1.0), [], [segB])
    kb.op("pool", lambda e: e.memset(segB.ap.rearrange("p (c t) -> p c t", t=64)[:, :, 63:64], 0.0), [segB], [segB])
    qn_ = [0]
    HB = (4, 5, 7)

    def qbank(full=False):
        i = HB[qn_[0] % 3]
        qn_[0] += 1
        return T(kb.ps[i] if full else kb.ps[i][:, 0:128], kb.psr[i])

    def gen_attn():
        for h in range(8):
            wq_ = wqh[h % 2]; wk_ = wkvh[h % 2]
            kb.dma("pool", "wqh%d" % (h % 2), wq_.ap, wuq_d[:, h * 256:(h + 1) * 256].rearrange("(k p) n -> p k n", p=128), [], [wq_])
            kb.dma("pool", "wkvh%d" % (h % 2), wk_.ap, wukv_d[:, h * 256:(h + 1) * 256].rearrange("(k p) n -> p k n", p=128), [], [wk_])
            kb.ts(wq_[:, :, 192:224], wq_[:, :, 160:192], -1.0, None, ALU.mult, None, [wq_], [wq_])
            kb.cp("dve", wq_[:, :, 224:256], wq_[:, :, 128:160], [wq_], [wq_])
            yield
            for kg in range(8):
                b = kb.bank(0, 2)
                for c in range(2):
                    kb.mm(b.ap, wk_[:, c, 0:128], kvnT[:, c, kg * 512:(kg + 1) * 512], c == 0, c == 1, [wk_, kvnT], [b])
                kb.cp("act" if kg % 2 == 0 else "dve", Kt[:, kg * 512:(kg + 1) * 512], b.ap, [b], [Kt])
                yield
            for k4 in range(8):
                b = kb.bank(0, 2)
                for j in range(4):
                    kbk = k4 * 4 + j
                    for c in range(2):
                        kb.mm(b[:, j * 128:(j + 1) * 128], kvnT[:, c, kbk * 128:(kbk + 1) * 128], wk_[:, c, 128:256], c == 0, c == 1, [wk_, kvnT], [b])
                kb.cp("act" if k4 % 2 == 0 else "dve", Vh[:, k4 * 4:(k4 + 1) * 4, :].rearrange("p a b -> p (a b)"), b.ap, [b], [Vh])
                yield
            for tb in range(4):
                b = kb.bank(0, 2)
                for c in range(4):
                    kb.mm(b.ap, wq_[:, c, 0:128], qnT[:, c, tb * 512:(tb + 1) * 512], c == 0, c == 3, [wq_, qnT], [b])
                kb.cp("act", Qn[:, tb * 512:(tb + 1) * 512], b.ap, [b], [Qn])
                b = kb.bank(0, 2)
                for c in range(4):
                    kb.mm(b.ap, wq_[:, c, 128:256], qnT[:, c, tb * 512:(tb + 1) * 512], c == 0, c == 3, [wq_, qnT], [b])
                kb.tt(Qr[:, tb * 512:(tb + 1) * 512], b.ap, CS[:, tb * 512:(tb + 1) * 512], ALU.mult, [b, CS], [Qr])
                yield
            for qgp in range(4):
                qs = slice(qgp * 512, (qgp + 1) * 512)
                Ob = T(kb.ps[2], kb.psr[2])
                Lb = T(kb.ps[3], kb.psr[3])

                def score(kbk):
                    b = T(kb.ps[kbk % 2], kb.psr[kbk % 2])
                    kb.mm(b.ap, Kt[:, kbk * 128:(kbk + 1) * 128], Qn[:, qs], True, False, [Kt, Qn], [b])
                    kb.mm(b.ap, krT[:, kbk * 128:(kbk + 1) * 128], Qr[:, qs], False, True, [krT, Qr], [b])
                    return b
                sb_next = score(0)
                for kbk in range(32):
                    sbk = sb_next
                    if kbk + 1 < 32:
                        sb_next = score(kbk + 1)
                    p = Pt[kbk % 3]
                    kb.act(p.ap, sbk.ap, AF.Exp, [sbk], [p], scale=SCALE)
                    kb.mm(Ob.ap, Vh[:, kbk, :], p.ap, kbk == 0, kbk == 31, [Vh, p], [Ob])
                    kb.mm(Lb.ap, ones16.ap, p.ap, kbk == 0, kbk == 31, [ones16, p], [Lb])
                    yield
                cs_ = cst[qgp % 2]
                kb.op("dve", lambda e, Lb=Lb: e.reciprocal(out=rl.ap, in_=Lb.ap), [Lb], [rl])
                kb.tt(cs_.ap, Ob.ap, rl.ap, ALU.mult, [Ob, rl], [cs_])
                kb.dma("sp", "cst%d" % (qgp % 2), cat_s[h][:, qs], cs_.ap, [cs_], [R_cat[h]])
                yield

    def chain_half(zt, hf, zi, dI, h, fwd, with_q):
        sl = slice(hf * 1024, (hf + 1) * 1024)
        z = zt[:, sl]
        lb = lbT[:, 8 * dI + h:8 * dI + h + 1]; oml = omlT[:, 8 * dI + h:8 * dI + h + 1]
        seg = segF if fwd else segB
        kb.act(z, z, AF.Exp, [zt], [zt], scale=-1.0)
        kb.ts(z, z, 1.0, None, ALU.add, None, [zt], [zt])
        yield
        kb.op("dve", lambda e: e.reciprocal(out=z, in_=z), [zt], [zt])
        kb.ts(z, z, oml, lb, ALU.mult, ALU.add, [zt, lbT, omlT], [zt])
        yield
        kb.act(W2.ap, z, AF.Ln, [zt], [W2])
        if fwd:
            kb.op("dve", lambda e: e.tensor_tensor_scan(out=W3.ap, data0=seg.ap, data1=W2.ap, initial=0.0, op0=ALU.mult, op1=ALU.add), [seg, W2], [W3])
        else:
            kb.op("dve", lambda e: e.tensor_tensor_scan(out=W3[:, ::-1], data0=seg[:, ::-1], data1=W2[:, ::-1], initial=0.0, op0=ALU.mult, op1=ALU.add), [seg, W2], [W3])
        yield
        kb.ts(z, z, -1.0, 1.0, ALU.mult, ALU.add, [zt], [zt])
        kb.act(W2.ap, W3.ap, AF.Exp, [W3], [W2])
        kb.act(W3.ap, W3.ap, AF.Exp, [W3], [W3], scale=-1.0)
        yield
        e3 = W2.ap.rearrange("p (c t) -> p c t", t=64)
        eb_ = ebl[zi][:, hf * 16:(hf + 1) * 16]
        kb.cp("dve", eb_, e3[:, :, 63 if fwd else 0], [W2], [ebl[zi]])
        if with_q:
            kb.tt(qtl[dI][:, sl], hqt[:, sl], W2.ap, ALU.mult, [hqt, W2], [qtl[dI]])
        yield
        kb.tt(z, z, W3.ap, ALU.mult, [zt, W3], [zt])
        if with_q:
            kb.cp("act", ktl[dI][:, sl], z, [zt], [ktl[dI]])
        z3 = z.rearrange("p (c t) -> p c t", t=64)
        kb.tt(z3, z3, eb_.unsqueeze(2).broadcast_to([128, 16, 64]), ALU.mult, [zt, ebl[zi]], [zt])
        yield

    def gen_hgrn():
        kcnt = 0; acnt = 0
        for h in range(8):
            kb.dma("sp", "dz0", Z[0].ap, z1_s[h], [R_z1[h]], [Z[0]])
            kb.dma("sp", "dz1", Z[1].ap, z2_s[h][:, 0:TOWN], [R_z2[h]], [Z[1]])
            kb.dma("sp", "dz2", Z[2].ap, z2_s[h][:, TOWN:TALL], [R_z2[h]], [Z[2]])
            kb.dma("sp", "dhq", hqt.ap, hq_s[h], [R_hq[h]], [hqt])
            kb.dma("sp", "dsg", sgt.ap, sg_s[h], [R_sg[h]], [sgt])
            kb.dma("sp", "dv", vt.ap, v_s[:, h * 128:(h + 1) * 128].rearrange("(t p) c -> p t c", p=128), [R_v], [vt])
            yield
            for zi, dI, fwd, with_q in ((0, 0, True, True), (1, 1, False, True), (2, 1, False, False)):
                for hf in range(2):
                    for _ in chain_half(Z[zi], hf, zi, dI, h, fwd, with_q):
                        yield
            for dI, seq in ((0, [(0, i) for i in range(16)]), (1, [(2, i) for i in range(15, -1, -1)] + [(1, i) for i in range(15, -1, -1)])):
                fwd = dI == 0
                scur = Sst[0]; snext = Sst[1]
                kb.op("pool", lambda e, scur=scur: e.memset(scur.ap, 0.0), [], [scur])
                if fwd:
                    kb.op("pool", lambda e: e.memset(Sall[0][:, 0, :], 0.0), [], [Sall[0]])

                def xpose(zi, i):
                    nonlocal kcnt
                    b = qbank()
                    kb.tr(b.ap, Z[zi][:, i * 128:(i + 1) * 128], [Z[zi]], [b])
                    kt_ = khtok[kcnt % 3]; kcnt += 1
                    kb.cp("act", kt_.ap, b.ap, [b], [kt_])
                    return kt_
                kt_next = xpose(*seq[0])
                for n_, (zi, i) in enumerate(seq):
                    kt_ = kt_next
                    if n_ + 1 < len(seq):
                        kt_next = xpose(*seq[n_ + 1])
                    voff = 16 if zi == 2 else 0
                    for a in ((0, 1) if fwd else (1, 0)):
                        c = 2 * i + a
                        db = qbank()
                        kb.mm(db.ap, kt_[a * 64:(a + 1) * 64, :], vt[a * 64:(a + 1) * 64, voff + i, :], True, True, [kt_, vt], [db])
                        kb.stt(snext.ap, scur.ap, ebl[zi][:, c:c + 1], db.ap, ALU.mult, ALU.add, [scur, ebl[zi], db], [snext])
                        scur, snext = snext, scur
                        if fwd:
                            if c + 1 < 32:
                                kb.cp("act", Sall[0][:, c + 1, :], scur.ap, [scur], [Sall[0]])
                        else:
                            if zi == 2 and c == 0:
                                kb.cp("act", Sall[1][:, 31, :], scur.ap, [scur], [Sall[1]])
                            elif zi == 1 and c >= 1:
                                kb.cp("act", Sall[1][:, c - 1, :], scur.ap, [scur], [Sall[1]])
                    yield
            ob = T(kb.ps[6], kb.psr[6])

            def amats(i):
                nonlocal acnt
                ts_ = slice(i * 128, (i + 1) * 128)
                ams = []
                for dI in range(2):
                    ab = qbank()
                    kb.mm(ab.ap, ktl[dI][:, ts_], qtl[dI][:, ts_], True, True, [ktl[dI], qtl[dI]], [ab])
                    am = Am[acnt % 6]; acnt += 1
                    msk = maskF if dI == 0 else maskB
                    kb.tt(am.ap, ab.ap, msk.ap, ALU.mult, [ab, msk], [am])
                    ams.append(am)
                return ams
            ams_next = amats(0)
            for i in range(16):
                ii = i % 4; i4 = i // 4
                ams = ams_next
                if i + 1 < 16:
                    ams_next = amats(i + 1)
                oc = ob[:, ii * 128:(ii + 1) * 128]
                first = True
                for dI in range(2):
                    kb.mm(oc, vt[:, i, :], ams[dI].ap, first, False, [vt, ams[dI]], [ob])
                    first = False
                    for a in range(2):
                        c = 2 * i + a
                        last = (dI == 1 and a == 1)
                        kb.mm(ob[:, ii * 128 + a * 64:ii * 128 + (a + 1) * 64], Sall[dI][:, c, :], qtl[dI][:, i * 128 + a * 64:i * 128 + (a + 1) * 64],
                              False, last, [Sall[dI], qtl[dI]], [ob])
                yield
                if ii == 3:
                    s = sqD
                    kb.act(s.ap, ob.ap, AF.Square, [ob], [s])
                    sb = qbank(full=True)
                    kb.mm(sb.ap, ones32.ap, s.ap, True, True, [ones32, s], [sb])
                    kb.cp("dve", rsbD.ap, sb.ap, [sb], [rsbD])
                    kb.rsqrt_mean(rsbD.ap, 128, [rsbD])
                    kb.stt(otmp.ap, ob.ap, hgn, rsbD.ap, ALU.mult, ALU.mult, [ob, rsbD, v2T], [otmp])
                    rs_ = rst[i4 % 2]
                    kb.tt(rs_.ap, otmp.ap, sgt[:, i4 * 512:(i4 + 1) * 512], ALU.mult, [otmp, sgt], [rs_])
                    kb.dma("sp", "rst%d" % (i4 % 2), cat_s[8 + h][:, i4 * 512:(i4 + 1) * 512], rs_.ap, [rs_], [R_cat[8 + h]])
                    yield

    def gen_modrest():

        def ld(j):
            s = mring[j % 2]
            kb.dma("pool", "mr%d" % (j % 2), s.ap, wmod_d[:, j * 128:(j + 1) * 128].rearrange("(k p) n -> p k n", p=128), [], [s])
            return s
        nxt = ld(32)
        for j in range(32, 96):
            un = nxt
            if j + 1 < 96:
                nxt = ld(j + 1)
            pm2 = qbank(full=True)
            for k in range(16):
                kb.mm(pm2[:, 0:1], un[:, k, :], cond[:, k:k + 1], k == 0, k == 15, [un, cond], [pm2])
            kb.cp("act", modraw[:, j - 32:j - 31], pm2[:, 0:1], [pm2], [modraw])
            yield
        kb.tt(modT[:, 32:96], modraw.ap, vT[:, 32:96], ALU.add, [modraw, vT], [modT])
        kb.tt(G1.ap, modT[:, 32:48], vT[:, 112:128], ALU.mult, [modT, vT], [G1])
        kb.stt(A2.ap, modT[:, 64:80], 1.0, v2T[:, 0:16], ALU.add, ALU.mult, [modT, v2T], [A2])
        kb.tt(G2.ap, modT[:, 80:96], v2T[:, 16:32], ALU.mult, [modT, v2T], [G2])
        yield

    gens = [(gen_attn(), 1), (gen_hgrn(), 1), (gen_modrest(), 6)]
    live = [True] * len(gens)
    rnd = 0
    while any(live):
        for gi, (g_, stride) in enumerate(gens):
            if live[gi] and rnd % stride == 0:
                try:
                    next(g_)
                except StopIteration:
                    live[gi] = False
        rnd += 1
    fw.barrier()
    if stage <= 3:
        return kb, {}

    kb.top = mark_small
    ring[:] = [kb.alloc("ringE%d" % i, [128, 16, 512], BF16) for i in range(3)]
    x1T = kb.alloc("x1T", [128, 16, 512])
    r32 = kb.alloc("r32", [128, 16, 512])
    r32b = r32.ap.rearrange("p a b -> p (a b)").bitcast(BF16)
    catg = T(r32b[:, 0:8192].rearrange("p (a b) -> p a b", a=16), r32.r)
    h2T = T(r32b[:, 8192:16384].rearrange("p (a b) -> p a b", a=16), r32.r)
    yT = r32
    uT = kb.alloc("uT", [128, 64, 512], BF16)
    mixT = T(uT.ap.rearrange("p a b -> p (a b)")[:, 0:16384].bitcast(F32).rearrange("p (a b) -> p a b", a=16), uT.r)
    xh = [kb.alloc("xh%d" % i, [128, 1024]) for i in range(2)]
    sq = [kb.alloc("sqE%d" % i, [128, 512]) for i in range(2)]
    rsE = kb.alloc("rsE", [128, 512])
    tE = kb.alloc("tE", [128, 512])
    rel = [kb.alloc("rel%d" % i, [128, 512], BF16) for i in range(2)]
    xcnt = [0]; scnt = [0]

    def stats(srcs_fn, n_chunks):
        sb = kb.bank(6, 8)
        for c in range(n_chunks):
            s = sq[scnt[0] % 2]; scnt[0] += 1
            src, rr = srcs_fn(c)
            kb.act(s.ap, src, AF.Square, rr, [s])
            kb.mm(sb.ap, ones32.ap, s.ap, c == 0, c == n_chunks - 1, [ones32, s], [sb])
        kb.cp("dve", rsE.ap, sb.ap, [sb], [rsE])
        kb.rsqrt_mean(rsE.ap, D, [rsE])

    for tg in range(4):
        t0 = tg * 512
        kb.dma("sp", "catg", catg.ap, cat_s[:, :, t0:t0 + 512].rearrange("c p t -> p c t"), R_cat, [catg])
        for j in range(4):
            for hf in range(2):
                xt = xh[xcnt[0] % 2]; lane = "xh%d" % (xcnt[0] % 2); xcnt[0] += 1
                r0 = t0 + j * 128
                kb.dma("sp", lane, xt.ap, x_d[r0:r0 + 128, hf * 1024:(hf + 1) * 1024], [], [xt])
                for c4 in range(2):
                    b = kb.bank(0, 4)
                    for jj in range(4):
                        kb.tr(b[:, jj * 128:(jj + 1) * 128], xt[:, (c4 * 4 + jj) * 128:(c4 * 4 + jj + 1) * 128], [xt], [b])
                    cb = hf * 8 + c4 * 4
                    kb.cp("act" if c4 == 0 else "dve", x1T[:, cb:cb + 4, j * 128:(j + 1) * 128],
                          b.ap.rearrange("p (a b) -> p a b", a=4), [b], [x1T])
        nxt = load_unit(wout_d[:, 0:512])
        for u in range(4):
            un = nxt
            if u + 1 < 4:
                nxt = load_unit(wout_d[:, (u + 1) * 512:(u + 2) * 512])
            for cc in range(4):
                c = u * 4 + cc
                b = kb.bank(0, 4)
                for k in range(16):
                    kb.mm(b.ap, un[:, k, cc * 128:(cc + 1) * 128], catg[:, k, :], k == 0, k == 15, [un, catg], [b])
                kb.cp("act", mixT[:, c, :], b.ap, [b], [mixT])
        stats(lambda c: (mixT[:, c, :], [mixT]), 16)
        for c in range(16):
            kb.stt(tE.ap, mixT[:, c, :], G1[:, c:c + 1], rsE.ap, ALU.mult, ALU.mult, [mixT, G1, rsE], [tE])
            kb.tt(x1T[:, c, :], x1T[:, c, :], tE.ap, ALU.add, [x1T, tE], [x1T])
        stats(lambda c: (x1T[:, c, :], [x1T]), 16)
        for c in range(16):
            kb.tt(tE.ap, x1T[:, c, :], rsE.ap, ALU.mult, [x1T, rsE], [tE])
            kb.act(h2T[:, c, :], tE.ap, AF.Identity, [tE, A2, modT], [h2T], scale=A2[:, c:c + 1], bias=shM[:, c:c + 1])
        nxt = load_unit(wup_d[:, 0:512])
        for u in range(16):
            un = nxt
            if u + 1 < 16:
                nxt = load_unit(wup_d[:, (u + 1) * 512:(u + 2) * 512])
            for cc in range(4):
                j = u * 4 + cc
                b = kb.bank(0, 4)
                for k in range(16):
                    kb.mm(b.ap, un[:, k, cc * 128:(cc + 1) * 128], h2T[:, k, :], k == 0, k == 15, [un, h2T], [b])
                r = rel[j % 2]
                kb.act(r.ap, b.ap, AF.Relu, [b], [r])
                kb.tt(uT[:, j, :], r.ap, r.ap, ALU.mult, [r], [uT])
        for cs in range(4):
            accs = [T(kb.ps[i], kb.psr[i]) for i in range(4)]
            nxt = load_unit(wdn_d[0:2048, cs * 512:(cs + 1) * 512])
            for rg in range(4):
                un = nxt
                if rg + 1 < 4:
                    nxt = load_unit(wdn_d[(rg + 1) * 2048:(rg + 2) * 2048, cs * 512:(cs + 1) * 512])
                for k in range(16):
                    j = rg * 16 + k
                    for cc in range(4):
                        kb.mm(accs[cc].ap, un[:, k, cc * 128:(cc + 1) * 128], uT[:, j, :], j == 0, j == 63, [un, uT], [accs[cc]])
            for cc in range(4):
                kb.cp("act" if cc % 2 == 0 else "dve", yT[:, cs * 4 + cc, :], accs[cc].ap, [accs[cc]], [yT])
        stats(lambda c: (yT[:, c, :], [yT]), 16)
        for c in range(16):
            kb.stt(tE.ap, yT[:, c, :], G2[:, c:c + 1], rsE.ap, ALU.mult, ALU.mult, [yT, G2, rsE], [tE])
            kb.tt(x1T[:, c, :], x1T[:, c, :], tE.ap, ALU.add, [x1T, tE], [x1T])
        for j in range(4):
            for hf in range(2):
                xt = xh[xcnt[0] % 2]; lane = "xh%d" % (xcnt[0] % 2); xcnt[0] += 1
                for c4 in range(2):
                    b = kb.bank(4, 6)
                    for jj in range(4):
                        c = hf * 8 + c4 * 4 + jj
                        kb.tr(b[:, jj * 128:(jj + 1) * 128], x1T[:, c, j * 128:(j + 1) * 128], [x1T], [b])
                    kb.cp("act" if c4 == 0 else "dve", xt[:, c4 * 512:(c4 + 1) * 512], b.ap, [b], [xt])
                r0 = t0 + j * 128
                kb.dma("sp", lane, out_d[r0:r0 + 128, hf * 1024:(hf + 1) * 1024], xt.ap, [xt], [R_out])
    fw.barrier()
    return kb, {}


_CACHE = {}


def _prep_inputs(inputs):
    f32 = np.float32
    x = np.asarray(inputs["x"], f32); c = np.asarray(inputs["c"], f32)
    positions = np.asarray(inputs["positions"]).astype(np.int32)
    w_in = np.asarray(inputs["w_in"], f32)[0]
    offs = [0, 512, 768, 832, 1856, 2880, 3904, 4928, 5952]
    seg = lambda i: w_in[:, offs[i]:offs[i + 1]]
    w_uq = np.asarray(inputs["w_uq"], f32)[0].reshape(512, 8, 192)
    w_uq_r = np.concatenate([w_uq[:, :, 0:128], w_uq[:, :, 128:192], w_uq[:, :, 128:192]], axis=2).reshape(512, 2048)
    invf = (10000.0 ** (-np.arange(32, dtype=f32) / 32)).astype(f32)
    lbf = np.asarray(inputs["hgrn_lb_logits_fwd"], f32); lbb = np.asarray(inputs["hgrn_lb_logits_bwd"], f32)
    vecs = np.concatenate([np.asarray(inputs["b_mod"], f32)[0].reshape(96, 128),
                           np.asarray(inputs["pre_mix_g"], f32)[0].reshape(16, 128),
                           np.asarray(inputs["post_mix_g"], f32)[0].reshape(16, 128)], axis=0)
    common = {
        "w_mod": np.ascontiguousarray(np.asarray(inputs["w_mod"], f32)[0]),
        "w_uq": np.ascontiguousarray(w_uq_r),
        "w_ukv": np.ascontiguousarray(np.asarray(inputs["w_ukv"], f32)[0]),
        "w_out": np.ascontiguousarray(np.asarray(inputs["w_out"], f32)[0]),
        "w_up": np.ascontiguousarray(np.asarray(inputs["w_up"], f32)[0]),
        "w_down": np.ascontiguousarray(np.asarray(inputs["w_down"], f32)[0]),
        "vecs": np.ascontiguousarray(vecs),
    }
    kr = seg(2)
    win_p = []
    for p in range(2):
        f1, f2 = (seg(4), seg(5)) if p == 0 else (seg(5), seg(4))
        win_p.append(np.ascontiguousarray(np.concatenate([seg(0), seg(1), kr, kr, kr, kr, seg(3), f1, f2, seg(6), seg(7)], axis=1)))
    maps = []
    for b in range(4):
        for p in range(2):
            l1, l2 = (lbf, lbb) if p == 0 else (lbb, lbf)
            v2 = np.zeros((128, 128), f32)
            v2[0:16] = np.asarray(inputs["pre_mlp_g"], f32)[0].reshape(16, 128)
            v2[16:32] = np.asarray(inputs["post_mlp_g"], f32)[0].reshape(16, 128)
            v2[32:48] = c[b].reshape(16, 128)
            v2[48:56] = l1[0].reshape(8, 128); v2[56:64] = l1[1].reshape(8, 128)
            v2[64:72] = l2[0].reshape(8, 128); v2[72:80] = l2[1].reshape(8, 128)
            v2[80:84] = np.asarray(inputs["q_norm_g"], f32)[0].reshape(4, 128)
            v2[84:86] = np.asarray(inputs["kv_norm_g"], f32)[0].reshape(2, 128)
            v2[86] = np.asarray(inputs["hgrn_norm_g"], f32)[0]
            v2[87] = np.tile(invf, 4)
            xs = x[b] if p == 0 else x[b][::-1]
            ps = positions[b] if p == 0 else positions[b][::-1]
            m = dict(common)
            m["x"] = np.ascontiguousarray(xs)
            m["pos"] = np.ascontiguousarray(np.broadcast_to(ps[None, :], (128, TALL))).astype(np.int32)
            m["vecs2"] = v2
            m["w_in"] = win_p[p]
            maps.append(m)
    return maps


def kernel(**inputs):
    if "nc" not in _CACHE:
        kb, _ = build_program()
        kb.fw.emit()
        kb.fw.close()
        _CACHE["nc"] = kb.nc
    nc = _CACHE["nc"]
    maps = _prep_inputs(inputs)
    res = run_bass_kernel_spmd(nc, maps, core_ids=list(range(8)))
    out = np.empty((4, TALL, D), np.float32)
    for b in range(4):
        for p in range(2):
            o = np.asarray(res.results[b * 2 + p]["out"], np.float32)
            if p == 0:
                out[b, 0:TOWN] = o
            else:
                out[b, TOWN:TALL] = o[::-1]
    return out
```

```python
import math
import contextlib
import numpy as np
import concourse.bass as bass
import concourse.mybir as mybir
from concourse.bass_utils import run_bass_kernel_spmd

F32 = mybir.dt.float32
BF16 = mybir.dt.bfloat16
I32 = mybir.dt.int32
AF = mybir.ActivationFunctionType
ALU = mybir.AluOpType

ENGS = ("pe", "act", "dve", "pool", "sp")


class Res:
    __slots__ = ("name", "w", "rs")

    def __init__(self, name):
        self.name = name
        self.w = None
        self.rs = []


class Op:
    __slots__ = ("eng", "fn", "deps", "idx", "signal", "sigval", "dma_lane", "dma_val")

    def __init__(self, eng, fn):
        self.eng = eng
        self.fn = fn
        self.deps = []
        self.idx = None
        self.signal = False
        self.sigval = None
        self.dma_lane = None
        self.dma_val = None


class FW:
    def __init__(self, nc):
        self.nc = nc
        self.ops = {e: [] for e in ENGS}
        self.eh = {"pe": nc.tensor, "act": nc.scalar, "dve": nc.vector, "pool": nc.gpsimd, "sp": nc.sync}
        self.lanes = {}
        self.lane_sems = {}
        self.eng_sems = {}
        self.waited = {e: {} for e in ENGS}
        self.stack = contextlib.ExitStack()

    def res(self, name):
        return Res(name)

    def _tok_needed(self, eng, tok):
        key = (tok[0], tok[1])
        cur = self.waited[eng].get(key, -1)
        if tok[2] <= cur:
            return False
        self.waited[eng][key] = tok[2]
        return True

    def _add(self, eng, fn, reads, writes, dma_lane=None):
        op = Op(eng, fn)
        op.idx = len(self.ops[eng])
        toks = []
        for r in reads:
            if r.w is not None:
                toks.append(r.w)
        for w in writes:
            if w.w is not None:
                toks.append(w.w)
            toks.extend(w.rs)
        for tok in toks:
            if tok[0] == "e" and tok[1] == eng and eng == "pe":
                continue
            if tok[0] == "e" and tok[1] == eng and tok[2] >= op.idx:
                continue
            if self._tok_needed(eng, tok):
                op.deps.append(tok)
        if dma_lane is not None:
            self.lanes[dma_lane] = self.lanes.get(dma_lane, 0) + 1
            op.dma_lane = dma_lane
            op.dma_val = self.lanes[dma_lane] * 16
            mytok = ("d", dma_lane, op.dma_val)
        else:
            mytok = ("e", eng, op.idx)
        for r in reads:
            r.rs.append(mytok)
        for w in writes:
            w.w = mytok
            w.rs = []
        self.ops[eng].append(op)
        return op

    def op(self, eng, fn, reads=(), writes=()):
        return self._add(eng, fn, list(reads), list(writes))

    def dma(self, q, lane, out, in_, reads=(), writes=(), **kw):
        def fn(e):
            return e.dma_start(out=out, in_=in_, **kw)
        return self._add(q, fn, list(reads), list(writes), dma_lane=lane)

    def barrier(self, engs=ENGS):
        toks = []
        for e in ENGS:
            if self.ops[e]:
                toks.append(("e", e, len(self.ops[e]) - 1))
        for lane, cnt in self.lanes.items():
            toks.append(("d", lane, cnt * 16))
        for e in engs:
            op = Op(e, None)
            op.idx = len(self.ops[e])
            for tok in toks:
                if tok[0] == "e" and tok[1] == e:
                    continue
                if self._tok_needed(e, tok):
                    op.deps.append(tok)
            self.ops[e].append(op)

    def emit(self):
        nc = self.nc
        for e in ENGS:
            for op in self.ops[e]:
                for tok in op.deps:
                    if tok[0] == "e":
                        self.ops[tok[1]][tok[2]].signal = True
        for e in ENGS:
            cnt = 0
            ops = self.ops[e]
            for i, op in enumerate(ops):
                if op.signal and (op.fn is None or op.dma_lane is not None):
                    j = i - 1
                    while j >= 0 and (ops[j].fn is None or ops[j].dma_lane is not None):
                        j -= 1
                    op.signal = False
                    if j >= 0:
                        ops[j].signal = True
                        op.sigval = ("alias", j)
                    else:
                        op.sigval = ("zero",)
            for op in ops:
                if op.signal:
                    cnt += 1
                    op.sigval = cnt
            for op in ops:
                if isinstance(op.sigval, tuple):
                    op.sigval = ops[op.sigval[1]].sigval if op.sigval[0] == "alias" else 0
        for e in ENGS:
            self.eng_sems[e] = self.stack.enter_context(nc.semaphore("s_" + e))
        for lane in self.lanes:
            self.lane_sems[lane] = self.stack.enter_context(nc.semaphore("l_" + lane))
        block = self.stack.enter_context(nc.Block())

        def run(e, eh):
            for op in self.ops[e]:
                for tok in op.deps:
                    if tok[0] == "e":
                        v = self.ops[tok[1]][tok[2]].sigval
                        if v:
                            eh.wait_ge(self.eng_sems[tok[1]], v)
                    else:
                        eh.wait_ge(self.lane_sems[tok[1]], tok[2])
                if op.fn is None:
                    continue
                ins = op.fn(eh)
                if op.dma_lane is not None:
                    ins.then_inc(self.lane_sems[op.dma_lane], 16)
                elif op.signal:
                    ins.then_inc(self.eng_sems[e], 1)

        @block.tensor
        def _(eh):
            run("pe", eh)

        @block.scalar
        def _(eh):
            run("act", eh)

        @block.vector
        def _(eh):
            run("dve", eh)

        @block.gpsimd
        def _(eh):
            run("pool", eh)

        @block.sync
        def _(eh):
            run("sp", eh)

    def close(self):
        self.stack.close()


D = 2048
TOWN = 2048
TALL = 4096
SCALE = 192 ** -0.5
EPS = 1e-6
TWO_PI = 2.0 * math.pi
F32R = mybir.dt.float32r


class T:
    def __init__(self, ap, r):
        self.ap = ap
        self.r = r

    def __getitem__(self, k):
        return self.ap[k]


def _rs(lst):
    out = []
    for x in lst:
        if x is None:
            continue
        out.append(x.r if isinstance(x, T) else x)
    return out


class KB:
    def __init__(self, stage=99):
        self.stage = stage
        nc = self.nc = bass.Bass("TRN2", target_bir_lowering=False)
        self.fw = FW(nc)
        n32 = (nc.sbuf_bytes_remaining - 512) // 4
        self.arena = self.fw.stack.enter_context(nc.sbuf_tensor("arena", [128, n32], F32))
        self.n32 = n32
        self.top = 0
        self.ps = [self.fw.stack.enter_context(nc.psum_tensor("ps%d" % i, [128, 512], F32))[:, :] for i in range(8)]
        self.psr = [self.fw.res("ps%d" % i) for i in range(8)]
        self.rot = 0

    def alloc(self, name, shape, dt=F32):
        n = 1
        for s in shape[1:]:
            n *= s
        words = n if dt != BF16 else (n + 1) // 2
        words = (words + 7) // 8 * 8
        assert self.top + words <= self.n32, ("SBUF overflow", name, self.top, words, self.n32)
        ap = self.arena[:, self.top:self.top + words]
        if dt == BF16:
            ap = ap.bitcast(BF16)[:, 0:n]
        elif dt == I32:
            ap = ap.bitcast(I32)[:, 0:n]
        else:
            ap = ap[:, 0:n]
        self.top += words
        if len(shape) == 3:
            ap = ap.rearrange("p (a b) -> p a b", a=shape[1])
        if shape[0] != 128:
            ap = ap[0:shape[0]]
        return T(ap, self.fw.res(name))

    def op(self, eng, fn, reads, writes):
        self.fw.op(eng, fn, _rs(reads), _rs(writes))

    def act(self, out, in_, func, reads, writes, scale=1.0, bias=0.0, accum=None):
        self.op("act", lambda e: e.activation(out=out, in_=in_, func=func, bias=bias, scale=scale, accum_out=accum), reads, writes)

    def ts(self, out, in0, s1, s2, op0, op1, reads, writes, eng="dve"):
        if op1 is None:
            self.op(eng, lambda e: e.tensor_scalar(out=out, in0=in0, scalar1=s1, scalar2=None, op0=op0), reads, writes)
        else:
            self.op(eng, lambda e: e.tensor_scalar(out=out, in0=in0, scalar1=s1, scalar2=s2, op0=op0, op1=op1), reads, writes)

    def tt(self, out, in0, in1, op, reads, writes, eng="dve"):
        self.op(eng, lambda e: e.tensor_tensor(out=out, in0=in0, in1=in1, op=op), reads, writes)

    def stt(self, out, in0, scalar, in1, op0, op1, reads, writes):
        self.op("dve", lambda e: e.scalar_tensor_tensor(out=out, in0=in0, scalar=scalar, in1=in1, op0=op0, op1=op1), reads, writes)

    def cp(self, eng, out, in_, reads, writes):
        if eng == "act":
            self.op("act", lambda e: e.copy(out=out, in_=in_), reads, writes)
        else:
            self.op(eng, lambda e: e.tensor_copy(out=out, in_=in_), reads, writes)

    def mm(self, out, lhsT, rhs, start, stop, reads, writes):
        self.op("pe", lambda e: e.matmul(out, lhsT=lhsT, rhs=rhs, start=start, stop=stop), reads, writes)

    def tr(self, out, in_, reads, writes):
        ident = self.ident
        self.op("pe", lambda e: e.transpose(out=out, in_=in_, identity=ident.ap), list(reads) + [ident], writes)

    def dma(self, q, lane, out, in_, reads, writes):
        self.fw.dma(q, lane, out, in_, _rs(reads), _rs(writes))

    def bank(self, lo=0, hi=8):
        n = hi - lo
        i = lo + (self.rot % n)
        self.rot += 1
        return T(self.ps[i], self.psr[i])

    def rsqrt_mean(self, t, n, reads_writes):
        self.ts(t, t, 1.0 / n, EPS, ALU.mult, ALU.add, reads_writes, reads_writes)
        self.act(t, t, AF.Ln, reads_writes, reads_writes)
        self.act(t, t, AF.Exp, reads_writes, reads_writes, scale=-0.5)


def build_program(stage=99):
    kb = KB(stage)
    nc, fw = kb.nc, kb.fw

    def din(name, shape, dt=F32):
        return nc.dram_tensor(name, shape, dt, kind="ExternalInput").ap()

    def dscr(name, shape, dt):
        return nc.dram_tensor(name, shape, dt, kind="Internal").ap()

    x_d = din("x", [TALL, D])
    pos_d = din("pos", [128, TALL], I32)
    vecs_d = din("vecs", [128, 128])
    vecs2_d = din("vecs2", [128, 128])
    wmod_d = din("w_mod", [D, 6 * D])
    win_d = din("w_in", [D, 6144])
    wuq_d = din("w_uq", [512, 2048])
    wukv_d = din("w_ukv", [256, 2048])
    wout_d = din("w_out", [D, D])
    wup_d = din("w_up", [D, 4 * D])
    wdn_d = din("w_down", [4 * D, D])
    out_d = nc.dram_tensor("out", [TOWN, D], F32, kind="ExternalOutput").ap()

    hq_s = dscr("hq_s", [8, 128, TOWN], BF16)
    z1_s = dscr("z1_s", [8, 128, TOWN], F32)
    z2_s = dscr("z2_s", [8, 128, TALL], F32)
    v_s = dscr("v_s", [TALL, 1024], BF16)
    sg_s = dscr("sg_s", [8, 128, TOWN], BF16)
    cat_s = dscr("cat_s", [16, 128, TOWN], BF16)
    R_hq = [fw.res("hq%d" % h) for h in range(8)]
    R_z1 = [fw.res("z1%d" % h) for h in range(8)]
    R_z2 = [fw.res("z2%d" % h) for h in range(8)]
    R_v = fw.res("v_s")
    R_sg = [fw.res("sg%d" % h) for h in range(8)]
    R_cat = [fw.res("cat%d" % c) for c in range(16)]
    R_out = fw.res("out")

    ident = kb.alloc("ident", [128, 128]); kb.ident = ident
    ones32 = kb.alloc("ones32", [128, 128])
    ones16 = kb.alloc("ones16", [128, 128], BF16)
    ones32r = kb.alloc("ones32r", [128, 128])
    maskF = kb.alloc("maskF", [128, 128])
    maskB = kb.alloc("maskB", [128, 128])
    vT = kb.alloc("vT", [128, 128])
    v2T = kb.alloc("v2T", [128, 128])
    modT = kb.alloc("modT", [128, 96])
    A1 = kb.alloc("A1", [128, 16]); G1 = kb.alloc("G1", [128, 16]); A2 = kb.alloc("A2", [128, 16]); G2 = kb.alloc("G2", [128, 16])
    lbT = kb.alloc("lbT", [128, 16]); omlT = kb.alloc("omlT", [128, 16])
    cond = kb.alloc("cond", [128, 16], BF16)
    mark_small = kb.top
    qnT = kb.alloc("qnT", [128, 4, TOWN], BF16)
    kvnT = kb.alloc("kvnT", [128, 2, TALL], BF16)
    krT = kb.alloc("krT", [128, TALL], BF16)
    CS = kb.alloc("CS", [128, TOWN])
    mark_cd = kb.top
    ring = [kb.alloc("ring%d" % i, [128, 16, 512], BF16) for i in range(3)]
    ringn = [0]

    def load_unit(src_ap):
        i = ringn[0] % 3
        ringn[0] += 1
        kb.dma("pool", "ring%d" % i, ring[i].ap, src_ap.rearrange("(k p) n -> p k n", p=128), [], [ring[i]])
        return ring[i]

    class UnitStream:
        def __init__(self, srcs):
            self.srcs = srcs
            self.tiles = []
            self.pos = 0
            self._fill()

        def _fill(self):
            while len(self.tiles) < len(self.srcs) and len(self.tiles) < self.pos + 2:
                self.tiles.append(load_unit(self.srcs[len(self.tiles)]))

        def get(self):
            self.pos += 1
            self._fill()
            return self.tiles[self.pos - 1]

    kb.op("pool", lambda e: e.memset(ident.ap, 0.0), [], [ident])
    kb.op("pool", lambda e: e.affine_select(out=ident.ap, in_=ident.ap, compare_op=ALU.not_equal, fill=1.0, base=0,
                                            pattern=[[-1, 128]], channel_multiplier=1), [ident], [ident])
    kb.op("pool", lambda e: e.memset(ones32.ap, 1.0), [], [ones32])
    kb.op("pool", lambda e: e.memset(ones16.ap, 1.0), [], [ones16])
    kb.op("pool", lambda e: e.memset(maskF.ap, 1.0), [], [maskF])
    kb.op("pool", lambda e: e.affine_select(out=maskF.ap, in_=maskF.ap, compare_op=ALU.is_ge, fill=0.0, base=0,
                                            pattern=[[1, 128]], channel_multiplier=-1), [maskF], [maskF])
    kb.op("pool", lambda e: e.memset(maskF[0:64, 64:128], 0.0), [maskF], [maskF])
    kb.op("pool", lambda e: e.memset(maskB.ap, 1.0), [], [maskB])
    kb.op("pool", lambda e: e.affine_select(out=maskB.ap, in_=maskB.ap, compare_op=ALU.is_ge, fill=0.0, base=0,
                                            pattern=[[-1, 128]], channel_multiplier=1), [maskB], [maskB])
    kb.op("pool", lambda e: e.memset(maskB[64:128, 0:64], 0.0), [maskB], [maskB])

    mark_global = kb.top

    tmpv = kb.alloc("tmpv", [128, 128])
    for src, dst in ((vecs_d, vT), (vecs2_d, v2T)):
        kb.dma("sp", "ltmpv", tmpv.ap, src, [], [tmpv])
        b = kb.bank()
        kb.tr(b[:, 0:128], tmpv.ap, [tmpv], [b])
        kb.cp("dve", dst.ap, b[:, 0:128], [b], [dst])
    e16 = kb.alloc("e16", [128, 16])
    cT = v2T[:, 32:48]
    kb.act(e16.ap, cT, AF.Exp, [v2T], [e16], scale=-1.0)
    kb.ts(e16.ap, e16.ap, 1.0, None, ALU.add, None, [e16], [e16])
    kb.op("dve", lambda e: e.reciprocal(out=e16.ap, in_=e16.ap), [e16], [e16])
    kb.tt(cond.ap, cT, e16.ap, ALU.mult, [v2T, e16], [cond])
    pm = kb.bank()
    NU = 8
    units = [None] * NU
    units[0] = load_unit(wmod_d[:, 0:512])
    units[1] = load_unit(wmod_d[:, 512:1024])
    for u in range(NU):
        if u + 2 < NU:
            units[u + 2] = load_unit(wmod_d[:, (u + 2) * 512:(u + 3) * 512])
        un = units[u]
        for cc in range(4):
            j = u * 4 + cc
            for k in range(16):
                kb.mm(pm[:, j:j + 1], un[:, k, cc * 128:(cc + 1) * 128], cond[:, k:k + 1], k == 0, k == 15, [un, cond], [pm])
    kb.tt(modT[:, 0:32], pm[:, 0:32], vT[:, 0:32], ALU.add, [pm, vT], [modT])
    shA = modT[:, 0:16]; shM = modT[:, 48:64]
    kb.stt(A1.ap, modT[:, 16:32], 1.0, vT[:, 96:112], ALU.add, ALU.mult, [modT, vT], [A1])
    for dI in range(2):
        l0 = v2T[:, 48 + 16 * dI:56 + 16 * dI]; l1 = v2T[:, 56 + 16 * dI:64 + 16 * dI]
        o = lbT[:, 8 * dI:8 * dI + 8]
        kb.tt(o, l1, l0, ALU.subtract, [v2T], [lbT])
        kb.act(o, o, AF.Exp, [lbT], [lbT])
        kb.ts(o, o, 1.0, None, ALU.add, None, [lbT], [lbT])
        kb.op("dve", lambda e, o=o: e.reciprocal(out=o, in_=o), [lbT], [lbT])
    kb.ts(omlT.ap, lbT.ap, -1.0, 1.0, ALU.mult, ALU.add, [lbT], [omlT])
    qg = v2T[:, 80:84]; kvg = v2T[:, 84:86]; hgn = v2T[:, 86:87]; invf = v2T[:, 87:88]

    kb.top = mark_global + 0
    tmpv = None
    hT = kb.alloc("hT", [128, 16, 1024], BF16)
    xb = [kb.alloc("xb%d" % i, [128, D]) for i in range(2)]
    junk = kb.alloc("junk", [128, D], BF16)
    ss1 = kb.alloc("ss1", [128, 1])
    posi = kb.alloc("posi", [128, 1024], I32)
    ang = kb.alloc("ang", [128, 1024])
    kf = kb.alloc("kf", [128, 1024])
    kiT = kb.alloc("kiT", [128, 1024], I32)
    cosT = kb.alloc("cosT", [128, 1024]); sinT = kb.alloc("sinT", [128, 1024])
    sq = [kb.alloc("sq%d" % i, [128, 512]) for i in range(2)]
    rsb = kb.alloc("rsb", [128, 512])
    tA = kb.alloc("tA", [128, 512]); tB = kb.alloc("tB", [128, 512])
    stg32 = [kb.alloc("stg32_%d" % i, [128, 1024]) for i in range(2)]
    stg16 = [kb.alloc("stg16_%d" % i, [128, 1024], BF16) for i in range(2)]
    stgv = [kb.alloc("stgv%d" % i, [128, 4, 512], BF16) for i in range(2)]
    cnt = {"s32": 0, "s16": 0, "sv": 0, "sq": 0}

    def sincos(dst, shift):
        kb.ts(kf.ap, ang.ap, shift, 1.0 / TWO_PI, ALU.add, ALU.mult, [ang], [kf])
        kb.cp("dve", kiT.ap, kf.ap, [kf], [kiT])
        kb.cp("dve", kf.ap, kiT.ap, [kiT], [kf])
        kb.stt(dst.ap, kf.ap, -TWO_PI, ang.ap, ALU.mult, ALU.add, [kf, ang], [dst])
        kb.ts(dst.ap, dst.ap, shift, None, ALU.add, None, [dst], [dst])
        kb.ts(kf.ap, dst.ap, math.pi, TWO_PI, ALU.is_gt, ALU.mult, [dst], [kf])
        kb.tt(dst.ap, dst.ap, kf.ap, ALU.subtract, [dst, kf], [dst])
        kb.ts(kf.ap, dst.ap, -math.pi, TWO_PI, ALU.is_lt, ALU.mult, [dst], [kf])
        kb.tt(dst.ap, dst.ap, kf.ap, ALU.add, [dst, kf], [dst])
        kb.act(dst.ap, dst.ap, AF.Sin, [dst], [dst])

    def stats_norm(banks, ncc, nfeat, gcols, dstT, t0):
        sb = kb.bank(4, 8)
        for cc in range(ncc):
            s = sq[cnt["sq"] % 2]; cnt["sq"] += 1
            kb.act(s.ap, banks[cc].ap, AF.Square, [banks[cc]], [s])
            kb.mm(sb.ap, ones32.ap, s.ap, cc == 0, cc == ncc - 1, [ones32, s], [sb])
        kb.cp("dve", rsb.ap, sb.ap, [sb], [rsb])
        kb.rsqrt_mean(rsb.ap, nfeat, [rsb])
        for cc in range(ncc):
            kb.stt(dstT[:, cc, t0:t0 + 512], banks[cc].ap, gcols[:, cc:cc + 1], rsb.ap, ALU.mult, ALU.mult,
                   [banks[cc], rsb, v2T], [dstT])

    srcsB = []
    for g in range(4):
        for u in (list(range(12)) if g < 2 else [1, 6, 7, 8, 9]):
            srcsB.append(win_d[:, u * 512:(u + 1) * 512])
    streamB = UnitStream(srcsB)
    for g in range(4):
        own = g < 2
        tok0 = g * 1024
        kb.dma("sp", "lposi", posi.ap, pos_d[:, tok0:tok0 + 1024], [], [posi])
        kb.cp("dve", ang.ap, posi.ap, [posi], [ang])
        kb.ts(ang.ap, ang.ap, invf, None, ALU.mult, None, [ang, v2T], [ang])
        sincos(sinT, 0.0)
        sincos(cosT, math.pi / 2)
        if own:
            kb.cp("act", CS[0:64, tok0:tok0 + 1024], cosT[0:64, :], [cosT], [CS])
            kb.cp("act", CS[64:128, tok0:tok0 + 1024], sinT[64:128, :], [sinT], [CS])
        for i in range(8):
            xt = xb[i % 2]
            r0 = tok0 + i * 128
            kb.dma("sp", "x%d" % (i % 2), xt.ap, x_d[r0:r0 + 128, :], [], [xt])
            kb.act(junk.ap, xt.ap, AF.Square, [xt], [junk, ss1], accum=ss1.ap)
            kb.rsqrt_mean(ss1.ap, D, [ss1])
            kb.ts(xt.ap, xt.ap, ss1[:, 0:1], None, ALU.mult, None, [xt, ss1], [xt])
            for c4 in range(4):
                b = kb.bank(0, 4)
                for j in range(4):
                    c = c4 * 4 + j
                    kb.tr(b[:, j * 128:(j + 1) * 128], xt[:, c * 128:(c + 1) * 128], [xt], [b])
                for j in range(4):
                    c = c4 * 4 + j
                    o = hT[:, c, i * 128:(i + 1) * 128]
                    if j % 2 == 0:
                        kb.act(o, b[:, j * 128:(j + 1) * 128], AF.Identity, [b, A1, modT], [hT], scale=A1[:, c:c + 1], bias=shA[:, c:c + 1])
                    else:
                        kb.ts(o, b[:, j * 128:(j + 1) * 128], A1[:, c:c + 1], shA[:, c:c + 1], ALU.mult, ALU.add, [b, A1, modT], [hT])
        ulist = list(range(12)) if own else [1, 6, 7, 8, 9]
        for n_, u in enumerate(ulist):
            un = streamB.get()

            def fm(cc, tb, b):
                for k in range(16):
                    kb.mm(b.ap, un[:, k, cc * 128:(cc + 1) * 128], hT[:, k, tb * 512:(tb + 1) * 512], k == 0, k == 15, [un, hT], [b])

            if u == 1:
                for o0 in (384, 448):
                    kb.ts(un[:, :, o0:o0 + 32], un[:, :, 288:320], -1.0, None, ALU.mult, None, [un], [un])
                    kb.cp("dve", un[:, :, o0 + 32:o0 + 64], un[:, :, 256:288], [un], [un])
            if u in (0, 1):
                for tb in range(2):
                    t0 = tok0 + tb * 512
                    banks = []
                    for cc in range(4):
                        b = kb.bank(0, 4)
                        fm(cc, tb, b)
                        banks.append(b)
                    if u == 0:
                        stats_norm(banks, 4, 512, qg, qnT, t0)
                    else:
                        stats_norm(banks[0:2], 2, 256, kvg, kvnT, t0)
                        kb.tt(tA.ap, banks[2].ap, cosT[:, tb * 512:(tb + 1) * 512], ALU.mult, [banks[2], cosT], [tA])
                        kb.tt(tB.ap, banks[3].ap, sinT[:, tb * 512:(tb + 1) * 512], ALU.mult, [banks[3], sinT], [tB])
                        kb.tt(krT[:, t0:t0 + 512], tA.ap, tB.ap, ALU.add, [tA, tB], [krT])
            elif u in (2, 3, 10, 11):
                for cc in range(4):
                    head = (u % 2) * 4 + cc
                    s = stg16[cnt["s16"] % 2]; lane = "s16_%d" % (cnt["s16"] % 2); cnt["s16"] += 1
                    for tb in range(2):
                        b = kb.bank(0, 4)
                        fm(cc, tb, b)
                        o = s[:, tb * 512:(tb + 1) * 512]
                        if u < 4:
                            kb.cp("act", o, b.ap, [b], [s])
                        else:
                            kb.act(tA.ap, b.ap, AF.Exp, [b], [tA], scale=-1.0)
                            kb.act(tA.ap, tA.ap, AF.Ln, [tA, ones32], [tA], bias=ones32[:, 0:1])
                            kb.act(tA.ap, tA.ap, AF.Exp, [tA], [tA], scale=-1.0)
                            kb.tt(o, b.ap, tA.ap, ALU.mult, [b, tA], [s])
                    if u < 4:
                        kb.dma("sp", lane, hq_s[head][:, tok0:tok0 + 1024], s.ap, [s], [R_hq[head]])
                    else:
                        kb.dma("sp", lane, sg_s[head][:, tok0:tok0 + 1024], s.ap, [s], [R_sg[head]])
            elif u in (4, 5, 6, 7):
                for cc in range(4):
                    head = (u % 2) * 4 + cc
                    s = stg32[cnt["s32"] % 2]; lane = "s32_%d" % (cnt["s32"] % 2); cnt["s32"] += 1
                    for tb in range(2):
                        b = kb.bank(0, 4)
                        fm(cc, tb, b)
                        kb.cp("act" if tb == 0 else "dve", s[:, tb * 512:(tb + 1) * 512], b.ap, [b], [s])
                    if u < 6:
                        kb.dma("sp", lane, z1_s[head][:, tok0:tok0 + 1024], s.ap, [s], [R_z1[head]])
                    else:
                        kb.dma("sp", lane, z2_s[head][:, tok0:tok0 + 1024], s.ap, [s], [R_z2[head]])
            else:
                for i4 in range(2):
                    s = stgv[cnt["sv"] % 2]; lane = "sv_%d" % (cnt["sv"] % 2); cnt["sv"] += 1
                    for ii in range(4):
                        i = i4 * 4 + ii
                        b = kb.bank(0, 4)
                        for k in range(16):
                            kb.mm(b.ap, hT[:, k, i * 128:(i + 1) * 128], un[:, k, :], k == 0, k == 15, [un, hT], [b])
                        kb.cp("act" if ii % 2 == 0 else "dve", s[:, ii, :], b.ap, [b], [s])
                    r0 = tok0 + i4 * 512
                    kb.dma("sp", lane, v_s[r0:r0 + 512, (u - 8) * 512:(u - 7) * 512].rearrange("(t p) c -> p t c", p=128), s.ap, [s], [R_v])
    fw.barrier()
    if stage <= 1:
        return kb, dict(qnT=qnT, kvnT=kvnT, krT=krT, CS=CS)

    kb.top = mark_cd
    wqh = [kb.alloc("wqh%d" % i, [128, 4, 256], BF16) for i in range(2)]
    wkvh = [kb.alloc("wkvh%d" % i, [128, 2, 256], BF16) for i in range(2)]
    Kt = kb.alloc("Kt", [128, TALL], BF16)
    Vh = kb.alloc("Vh", [128, 32, 128], BF16)
    Qn = kb.alloc("Qn", [128, TOWN], BF16)
    Qr = kb.alloc("Qr", [128, TOWN], BF16)
    Pt = [kb.alloc("Pt%d" % i, [128, 512], BF16) for i in range(3)]
    rl = kb.alloc("rl", [128, 512])
    cst = [kb.alloc("cst%d" % i, [128, 512], BF16) for i in range(2)]
    mring = [kb.alloc("mring%d" % i, [128, 16, 128], BF16) for i in range(2)]
    modraw = kb.alloc("modraw", [128, 64])
    segF = kb.alloc("segF", [128, 1024]); segB = kb.alloc("segB", [128, 1024])
    Z = [kb.alloc("Z%d" % i, [128, TOWN]) for i in range(3)]
    W2 = kb.alloc("W2", [128, 1024]); W3 = kb.alloc("W3", [128, 1024])
    hqt = kb.alloc("hqt", [128, TOWN], BF16)
    sgt = kb.alloc("sgt", [128, TOWN], BF16)
    vt = kb.alloc("vt", [128, 32, 128], BF16)
    qtl = [kb.alloc("qtl%d" % i, [128, TOWN], BF16) for i in range(2)]
    ktl = [kb.alloc("ktl%d" % i, [128, TOWN], BF16) for i in range(2)]
    ebl = [kb.alloc("ebl%d" % i, [128, 32]) for i in range(3)]
    Sall = [kb.alloc("Sall%d" % i, [128, 32, 128], BF16) for i in range(2)]
    Sst = [kb.alloc("Sst%d" % i, [128, 128]) for i in range(2)]
    khtok = [kb.alloc("khtok%d" % i, [128, 128], BF16) for i in range(3)]
    Am = [kb.alloc("Am%d" % i, [128, 128], BF16) for i in range(6)]
    rst = [kb.alloc("rst%d" % i, [128, 512], BF16) for i in range(2)]
    otmp = kb.alloc("otmp", [128, 512])
    sqD = kb.alloc("sqD", [128, 512]); rsbD = kb.alloc("rsbD", [128, 512])
    kb.op("pool", lambda e: e.memset(segF.ap, 1.0), [], [segF])
    kb.op("pool", lambda e: e.memset(segF.ap.rearrange("p (c t) -> p c t", t=64)[:, :, 0:1], 0.0), [segF], [segF])
    kb.op("pool", lambda e: e.memset(segB.ap, 1.0), [], [segB])
    kb.op("pool", lambda e: e.memset(segB.ap.rearrange("p (c t) -> p c t", t=64)[:, :, 63:64], 0.0), [segB], [segB])
    qn_ = [0]
    HB = (4, 5, 7)

    def qbank(full=False):
        i = HB[qn_[0] % 3]
        qn_[0] += 1
        return T(kb.ps[i] if full else kb.ps[i][:, 0:128], kb.psr[i])

    def gen_attn():
        for h in range(8):
            wq_ = wqh[h % 2]; wk_ = wkvh[h % 2]
            kb.dma("pool", "wqh%d" % (h % 2), wq_.ap, wuq_d[:, h * 256:(h + 1) * 256].rearrange("(k p) n -> p k n", p=128), [], [wq_])
            kb.dma("pool", "wkvh%d" % (h % 2), wk_.ap, wukv_d[:, h * 256:(h + 1) * 256].rearrange("(k p) n -> p k n", p=128), [], [wk_])
            kb.ts(wq_[:, :, 192:224], wq_[:, :, 160:192], -1.0, None, ALU.mult, None, [wq_], [wq_])
            kb.cp("dve", wq_[:, :, 224:256], wq_[:, :, 128:160], [wq_], [wq_])
            yield
            for kg in range(8):
                b = kb.bank(0, 2)
                for c in range(2):
                    kb.mm(b.ap, wk_[:, c, 0:128], kvnT[:, c, kg * 512:(kg + 1) * 512], c == 0, c == 1, [wk_, kvnT], [b])
                kb.cp("act" if kg % 2 == 0 else "dve", Kt[:, kg * 512:(kg + 1) * 512], b.ap, [b], [Kt])
                yield
            for k4 in range(8):
                b = kb.bank(0, 2)
                for j in range(4):
                    kbk = k4 * 4 + j
                    for c in range(2):
                        kb.mm(b[:, j * 128:(j + 1) * 128], kvnT[:, c, kbk * 128:(kbk + 1) * 128], wk_[:, c, 128:256], c == 0, c == 1, [wk_, kvnT], [b])
                kb.cp("act" if k4 % 2 == 0 else "dve", Vh[:, k4 * 4:(k4 + 1) * 4, :].rearrange("p a b -> p (a b)"), b.ap, [b], [Vh])
                yield
            for tb in range(4):
                b = kb.bank(0, 2)
                for c in range(4):
                    kb.mm(b.ap, wq_[:, c, 0:128], qnT[:, c, tb * 512:(tb + 1) * 512], c == 0, c == 3, [wq_, qnT], [b])
                kb.cp("act", Qn[:, tb * 512:(tb + 1) * 512], b.ap, [b], [Qn])
                b = kb.bank(0, 2)
                for c in range(4):
                    kb.mm(b.ap, wq_[:, c, 128:256], qnT[:, c, tb * 512:(tb + 1) * 512], c == 0, c == 3, [wq_, qnT], [b])
                kb.tt(Qr[:, tb * 512:(tb + 1) * 512], b.ap, CS[:, tb * 512:(tb + 1) * 512], ALU.mult, [b, CS], [Qr])
                yield
            for qgp in range(4):
                qs = slice(qgp * 512, (qgp + 1) * 512)
                Ob = T(kb.ps[2], kb.psr[2])
                Lb = T(kb.ps[3], kb.psr[3])

                def score(kbk):
                    b = T(kb.ps[kbk % 2], kb.psr[kbk % 2])
                    kb.mm(b.ap, Kt[:, kbk * 128:(kbk + 1) * 128], Qn[:, qs], True, False, [Kt, Qn], [b])
                    kb.mm(b.ap, krT[:, kbk * 128:(kbk + 1) * 128], Qr[:, qs], False, True, [krT, Qr], [b])
                    return b
                sb_next = score(0)
                for kbk in range(32):
                    sbk = sb_next
                    if kbk + 1 < 32:
                        sb_next = score(kbk + 1)
                    p = Pt[kbk % 3]
                    kb.act(p.ap, sbk.ap, AF.Exp, [sbk], [p], scale=SCALE)
                    kb.mm(Ob.ap, Vh[:, kbk, :], p.ap, kbk == 0, kbk == 31, [Vh, p], [Ob])
                    kb.mm(Lb.ap, ones16.ap, p.ap, kbk == 0, kbk == 31, [ones16, p], [Lb])
                    yield
                cs_ = cst[qgp % 2]
                kb.act(rl.ap, Lb.ap, AF.Ln, [Lb], [rl])
                kb.act(rl.ap, rl.ap, AF.Exp, [rl], [rl], scale=-1.0)
                kb.tt(cs_.ap, Ob.ap, rl.ap, ALU.mult, [Ob, rl], [cs_])
                kb.dma("sp", "cst%d" % (qgp % 2), cat_s[h][:, qs], cs_.ap, [cs_], [R_cat[h]])
                yield

    def chain_half(zt, hf, zi, dI, h, fwd, with_q):
        sl = slice(hf * 1024, (hf + 1) * 1024)
        z = zt[:, sl]
        lb = lbT[:, 8 * dI + h:8 * dI + h + 1]; oml = omlT[:, 8 * dI + h:8 * dI + h + 1]
        seg = segF if fwd else segB
        kb.act(z, z, AF.Exp, [zt], [zt], scale=-1.0)
        kb.act(z, z, AF.Ln, [zt, ones32], [zt], bias=ones32[:, 0:1])
        yield
        kb.act(z, z, AF.Exp, [zt], [zt], scale=-1.0)
        kb.ts(z, z, oml, lb, ALU.mult, ALU.add, [zt, lbT, omlT], [zt])
        yield
        kb.act(W2.ap, z, AF.Ln, [zt], [W2])
        if fwd:
            kb.op("dve", lambda e: e.tensor_tensor_scan(out=W3.ap, data0=seg.ap, data1=W2.ap, initial=0.0, op0=ALU.mult, op1=ALU.add), [seg, W2], [W3])
        else:
            kb.op("dve", lambda e: e.tensor_tensor_scan(out=W3[:, ::-1], data0=seg[:, ::-1], data1=W2[:, ::-1], initial=0.0, op0=ALU.mult, op1=ALU.add), [seg, W2], [W3])
        yield
        kb.ts(z, z, -1.0, 1.0, ALU.mult, ALU.add, [zt], [zt])
        kb.act(W2.ap, W3.ap, AF.Exp, [W3], [W2])
        kb.act(W3.ap, W3.ap, AF.Exp, [W3], [W3], scale=-1.0)
        yield
        e3 = W2.ap.rearrange("p (c t) -> p c t", t=64)
        eb_ = ebl[zi][:, hf * 16:(hf + 1) * 16]
        kb.cp("dve", eb_, e3[:, :, 63 if fwd else 0], [W2], [ebl[zi]])
        if with_q:
            kb.tt(qtl[dI][:, sl], hqt[:, sl], W2.ap, ALU.mult, [hqt, W2], [qtl[dI]])
        yield
        kb.tt(z, z, W3.ap, ALU.mult, [zt, W3], [zt])
        if with_q:
            kb.cp("act", ktl[dI][:, sl], z, [zt], [ktl[dI]])
        z3 = z.rearrange("p (c t) -> p c t", t=64)
        kb.tt(z3, z3, eb_.unsqueeze(2).broadcast_to([128, 16, 64]), ALU.mult, [zt, ebl[zi]], [zt])
        yield

    def gen_hgrn():
        kcnt = 0; acnt = 0
        for h in range(8):
            kb.dma("sp", "dz0", Z[0].ap, z1_s[h], [R_z1[h]], [Z[0]])
            kb.dma("sp", "dz1", Z[1].ap, z2_s[h][:, 0:TOWN], [R_z2[h]], [Z[1]])
            kb.dma("sp", "dz2", Z[2].ap, z2_s[h][:, TOWN:TALL], [R_z2[h]], [Z[2]])
            kb.dma("sp", "dhq", hqt.ap, hq_s[h], [R_hq[h]], [hqt])
            kb.dma("sp", "dsg", sgt.ap, sg_s[h], [R_sg[h]], [sgt])
            kb.dma("sp", "dv", vt.ap, v_s[:, h * 128:(h + 1) * 128].rearrange("(t p) c -> p t c", p=128), [R_v], [vt])
            yield
            for zi, dI, fwd, with_q in ((0, 0, True, True), (1, 1, False, True), (2, 1, False, False)):
                for hf in range(2):
                    for _ in chain_half(Z[zi], hf, zi, dI, h, fwd, with_q):
                        yield
            for dI, seq in ((0, [(0, i) for i in range(16)]), (1, [(2, i) for i in range(15, -1, -1)] + [(1, i) for i in range(15, -1, -1)])):
                fwd = dI == 0
                scur = Sst[0]; snext = Sst[1]
                kb.op("pool", lambda e, scur=scur: e.memset(scur.ap, 0.0), [], [scur])
                if fwd:
                    kb.op("pool", lambda e: e.memset(Sall[0][:, 0, :], 0.0), [], [Sall[0]])

                def xpose(zi, i):
                    nonlocal kcnt
                    b = qbank()
                    kb.tr(b.ap, Z[zi][:, i * 128:(i + 1) * 128], [Z[zi]], [b])
                    kt_ = khtok[kcnt % 3]; kcnt += 1
                    kb.cp("dve", kt_.ap, b.ap, [b], [kt_])
                    return kt_
                kt_next = xpose(*seq[0])
                for n_, (zi, i) in enumerate(seq):
                    kt_ = kt_next
                    if n_ + 1 < len(seq):
                        kt_next = xpose(*seq[n_ + 1])
                    voff = 16 if zi == 2 else 0
                    for a in ((0, 1) if fwd else (1, 0)):
                        c = 2 * i + a
                        db = qbank()
                        kb.mm(db.ap, kt_[a * 64:(a + 1) * 64, :], vt[a * 64:(a + 1) * 64, voff + i, :], True, True, [kt_, vt], [db])
                        kb.stt(snext.ap, scur.ap, ebl[zi][:, c:c + 1], db.ap, ALU.mult, ALU.add, [scur, ebl[zi], db], [snext])
                        scur, snext = snext, scur
                        if fwd:
                            if c + 1 < 32:
                                kb.cp("pool", Sall[0][:, c + 1, :], scur.ap, [scur], [Sall[0]])
                        else:
                            if zi == 2 and c == 0:
                                kb.cp("pool", Sall[1][:, 31, :], scur.ap, [scur], [Sall[1]])
                            elif zi == 1 and c >= 1:
                                kb.cp("pool", Sall[1][:, c - 1, :], scur.ap, [scur], [Sall[1]])
                    yield
            ob = T(kb.ps[6], kb.psr[6])

            def amats(i):
                nonlocal acnt
                ts_ = slice(i * 128, (i + 1) * 128)
                ams = []
                for dI in range(2):
                    ab = qbank()
                    kb.mm(ab.ap, ktl[dI][:, ts_], qtl[dI][:, ts_], True, True, [ktl[dI], qtl[dI]], [ab])
                    am = Am[acnt % 6]; acnt += 1
                    msk = maskF if dI == 0 else maskB
                    kb.tt(am.ap, ab.ap, msk.ap, ALU.mult, [ab, msk], [am])
                    ams.append(am)
                return ams
            ams_next = amats(0)
            for i in range(16):
                ii = i % 4; i4 = i // 4
                ams = ams_next
                if i + 1 < 16:
                    ams_next = amats(i + 1)
                oc = ob[:, ii * 128:(ii + 1) * 128]
                first = True
                for dI in range(2):
                    kb.mm(oc, vt[:, i, :], ams[dI].ap, first, False, [vt, ams[dI]], [ob])
                    first = False
                    for a in range(2):
                        c = 2 * i + a
                        last = (dI == 1 and a == 1)
                        kb.mm(ob[:, ii * 128 + a * 64:ii * 128 + (a + 1) * 64], Sall[dI][:, c, :], qtl[dI][:, i * 128 + a * 64:i * 128 + (a + 1) * 64],
                              False, last, [Sall[dI], qtl[dI]], [ob])
                yield
                if ii == 3:
                    s = sqD
                    kb.act(s.ap, ob.ap, AF.Square, [ob], [s])
                    sb = qbank(full=True)
                    kb.mm(sb.ap, ones32.ap, s.ap, True, True, [ones32, s], [sb])
                    kb.cp("dve", rsbD.ap, sb.ap, [sb], [rsbD])
                    kb.rsqrt_mean(rsbD.ap, 128, [rsbD])
                    kb.stt(otmp.ap, ob.ap, hgn, rsbD.ap, ALU.mult, ALU.mult, [ob, rsbD, v2T], [otmp])
                    rs_ = rst[i4 % 2]
                    kb.tt(rs_.ap, otmp.ap, sgt[:, i4 * 512:(i4 + 1) * 512], ALU.mult, [otmp, sgt], [rs_])
                    kb.dma("sp", "rst%d" % (i4 % 2), cat_s[8 + h][:, i4 * 512:(i4 + 1) * 512], rs_.ap, [rs_], [R_cat[8 + h]])
                    yield

    def gen_modrest():

        def ld(j):
            s = mring[j % 2]
            kb.dma("pool", "mr%d" % (j % 2), s.ap, wmod_d[:, j * 128:(j + 1) * 128].rearrange("(k p) n -> p k n", p=128), [], [s])
            return s
        nxt = ld(32)
        for j in range(32, 96):
            un = nxt
            if j + 1 < 96:
                nxt = ld(j + 1)
            pm2 = qbank(full=True)
            for k in range(16):
                kb.mm(pm2[:, 0:1], un[:, k, :], cond[:, k:k + 1], k == 0, k == 15, [un, cond], [pm2])
            kb.cp("act", modraw[:, j - 32:j - 31], pm2[:, 0:1], [pm2], [modraw])
            yield
        kb.tt(modT[:, 32:96], modraw.ap, vT[:, 32:96], ALU.add, [modraw, vT], [modT])
        kb.tt(G1.ap, modT[:, 32:48], vT[:, 112:128], ALU.mult, [modT, vT], [G1])
        kb.stt(A2.ap, modT[:, 64:80], 1.0, v2T[:, 0:16], ALU.add, ALU.mult, [modT, v2T], [A2])
        kb.tt(G2.ap, modT[:, 80:96], v2T[:, 16:32], ALU.mult, [modT, v2T], [G2])
        yield

    gens = [(gen_attn(), 1), (gen_hgrn(), 1), (gen_modrest(), 6)]
    live = [True] * len(gens)
    rnd = 0
    while any(live):
        for gi, (g_, stride) in enumerate(gens):
            if live[gi] and rnd % stride == 0:
                try:
                    next(g_)
                except StopIteration:
                    live[gi] = False
        rnd += 1
    fw.barrier()
    if stage <= 3:
        return kb, {}

    kb.top = mark_small
    ring[:] = [kb.alloc("ringE%d" % i, [128, 16, 512], BF16) for i in range(3)]
    x1T = kb.alloc("x1T", [128, 16, 512])
    r32 = kb.alloc("r32", [128, 16, 512])
    r32b = r32.ap.rearrange("p a b -> p (a b)").bitcast(BF16)
    catg = T(r32b[:, 0:8192].rearrange("p (a b) -> p a b", a=16), r32.r)
    h2T = T(r32b[:, 8192:16384].rearrange("p (a b) -> p a b", a=16), r32.r)
    yT = r32
    uT = kb.alloc("uT", [128, 64, 512], BF16)
    mixT = T(uT.ap.rearrange("p a b -> p (a b)")[:, 0:16384].bitcast(F32).rearrange("p (a b) -> p a b", a=16), uT.r)
    xh = [kb.alloc("xh%d" % i, [128, 1024]) for i in range(2)]
    sq = [kb.alloc("sqE%d" % i, [128, 512]) for i in range(2)]
    rsE = kb.alloc("rsE", [128, 512])
    tE = kb.alloc("tE", [128, 512])
    rel = [kb.alloc("rel%d" % i, [128, 512], BF16) for i in range(2)]
    xcnt = [0]; scnt = [0]

    def stats(srcs_fn, n_chunks):
        sb = kb.bank(6, 8)
        for c in range(n_chunks):
            s = sq[scnt[0] % 2]; scnt[0] += 1
            src, rr = srcs_fn(c)
            kb.act(s.ap, src, AF.Square, rr, [s])
            kb.mm(sb.ap, ones32.ap, s.ap, c == 0, c == n_chunks - 1, [ones32, s], [sb])
        kb.cp("dve", rsE.ap, sb.ap, [sb], [rsE])
        kb.rsqrt_mean(rsE.ap, D, [rsE])

    srcsE = []
    for tg in range(4):
        srcsE += [wout_d[:, u * 512:(u + 1) * 512] for u in range(4)]
        srcsE += [wup_d[:, u * 512:(u + 1) * 512] for u in range(16)]
        srcsE += [wdn_d[rg * 2048:(rg + 1) * 2048, cs * 512:(cs + 1) * 512] for cs in range(4) for rg in range(4)]
    streamE = UnitStream(srcsE)
    for tg in range(4):
        t0 = tg * 512
        kb.dma("sp", "catg", catg.ap, cat_s[:, :, t0:t0 + 512].rearrange("c p t -> p c t"), R_cat, [catg])
        for j in range(4):
            for hf in range(2):
                xt = xh[xcnt[0] % 2]; lane = "xh%d" % (xcnt[0] % 2); xcnt[0] += 1
                r0 = t0 + j * 128
                kb.dma("sp", lane, xt.ap, x_d[r0:r0 + 128, hf * 1024:(hf + 1) * 1024], [], [xt])
                for c4 in range(2):
                    b = kb.bank(0, 4)
                    for jj in range(4):
                        kb.tr(b[:, jj * 128:(jj + 1) * 128], xt[:, (c4 * 4 + jj) * 128:(c4 * 4 + jj + 1) * 128], [xt], [b])
                    cb = hf * 8 + c4 * 4
                    kb.cp("act" if c4 == 0 else "dve", x1T[:, cb:cb + 4, j * 128:(j + 1) * 128],
                          b.ap.rearrange("p (a b) -> p a b", a=4), [b], [x1T])
        for u in range(4):
            un = streamE.get()
            for cc in range(4):
                c = u * 4 + cc
                b = kb.bank(0, 4)
                for k in range(16):
                    kb.mm(b.ap, un[:, k, cc * 128:(cc + 1) * 128], catg[:, k, :], k == 0, k == 15, [un, catg], [b])
                kb.cp("act", mixT[:, c, :], b.ap, [b], [mixT])
        stats(lambda c: (mixT[:, c, :], [mixT]), 16)
        for c in range(16):
            kb.stt(tE.ap, mixT[:, c, :], G1[:, c:c + 1], rsE.ap, ALU.mult, ALU.mult, [mixT, G1, rsE], [tE])
            kb.tt(x1T[:, c, :], x1T[:, c, :], tE.ap, ALU.add, [x1T, tE], [x1T])
        stats(lambda c: (x1T[:, c, :], [x1T]), 16)
        for c in range(16):
            kb.tt(tE.ap, x1T[:, c, :], rsE.ap, ALU.mult, [x1T, rsE], [tE])
            kb.act(h2T[:, c, :], tE.ap, AF.Identity, [tE, A2, modT], [h2T], scale=A2[:, c:c + 1], bias=shM[:, c:c + 1])
        for u in range(16):
            un = streamE.get()
            for cc in range(4):
                j = u * 4 + cc
                b = kb.bank(0, 4)
                for k in range(16):
                    kb.mm(b.ap, un[:, k, cc * 128:(cc + 1) * 128], h2T[:, k, :], k == 0, k == 15, [un, h2T], [b])
                r = rel[j % 2]
                kb.act(r.ap, b.ap, AF.Relu, [b], [r])
                kb.tt(uT[:, j, :], r.ap, r.ap, ALU.mult, [r], [uT])
        for cs in range(4):
            accs = [T(kb.ps[i], kb.psr[i]) for i in range(4)]
            for rg in range(4):
                un = streamE.get()
                for k in range(16):
                    j = rg * 16 + k
                    for cc in range(4):
                        kb.mm(accs[cc].ap, un[:, k, cc * 128:(cc + 1) * 128], uT[:, j, :], j == 0, j == 63, [un, uT], [accs[cc]])
            for cc in range(4):
                kb.cp("act" if cc % 2 == 0 else "dve", yT[:, cs * 4 + cc, :], accs[cc].ap, [accs[cc]], [yT])
        stats(lambda c: (yT[:, c, :], [yT]), 16)
        for c in range(16):
            kb.stt(tE.ap, yT[:, c, :], G2[:, c:c + 1], rsE.ap, ALU.mult, ALU.mult, [yT, G2, rsE], [tE])
            kb.tt(x1T[:, c, :], x1T[:, c, :], tE.ap, ALU.add, [x1T, tE], [x1T])
        for j in range(4):
            for hf in range(2):
                xt = xh[xcnt[0] % 2]; lane = "xh%d" % (xcnt[0] % 2); xcnt[0] += 1
                for c4 in range(2):
                    b = kb.bank(4, 6)
                    for jj in range(4):
                        c = hf * 8 + c4 * 4 + jj
                        kb.tr(b[:, jj * 128:(jj + 1) * 128], x1T[:, c, j * 128:(j + 1) * 128], [x1T], [b])
                    kb.cp("act" if c4 == 0 else "dve", xt[:, c4 * 512:(c4 + 1) * 512], b.ap, [b], [xt])
                r0 = t0 + j * 128
                kb.dma("sp", lane, out_d[r0:r0 + 128, hf * 1024:(hf + 1) * 1024], xt.ap, [xt], [R_out])
    fw.barrier()
    return kb, {}


_CACHE = {}


def _prep_inputs(inputs):
    f32 = np.float32
    x = np.asarray(inputs["x"], f32); c = np.asarray(inputs["c"], f32)
    positions = np.asarray(inputs["positions"]).astype(np.int32)
    w_in = np.asarray(inputs["w_in"], f32)[0]
    offs = [0, 512, 768, 832, 1856, 2880, 3904, 4928, 5952]
    seg = lambda i: w_in[:, offs[i]:offs[i + 1]]
    w_uq = np.asarray(inputs["w_uq"], f32)[0].reshape(512, 8, 192)
    w_uq_r = np.concatenate([w_uq[:, :, 0:128], w_uq[:, :, 128:192], w_uq[:, :, 128:192]], axis=2).reshape(512, 2048)
    invf = (10000.0 ** (-np.arange(32, dtype=f32) / 32)).astype(f32)
    lbf = np.asarray(inputs["hgrn_lb_logits_fwd"], f32); lbb = np.asarray(inputs["hgrn_lb_logits_bwd"], f32)
    vecs = np.concatenate([np.asarray(inputs["b_mod"], f32)[0].reshape(96, 128),
                           np.asarray(inputs["pre_mix_g"], f32)[0].reshape(16, 128),
                           np.asarray(inputs["post_mix_g"], f32)[0].reshape(16, 128)], axis=0)
    common = {
        "w_mod": np.ascontiguousarray(np.asarray(inputs["w_mod"], f32)[0]),
        "w_uq": np.ascontiguousarray(w_uq_r),
        "w_ukv": np.ascontiguousarray(np.asarray(inputs["w_ukv"], f32)[0]),
        "w_out": np.ascontiguousarray(np.asarray(inputs["w_out"], f32)[0]),
        "w_up": np.ascontiguousarray(np.asarray(inputs["w_up"], f32)[0]),
        "w_down": np.ascontiguousarray(np.asarray(inputs["w_down"], f32)[0]),
        "vecs": np.ascontiguousarray(vecs),
    }
    kr = seg(2)
    win_p = []
    for p in range(2):
        f1, f2 = (seg(4), seg(5)) if p == 0 else (seg(5), seg(4))
        win_p.append(np.ascontiguousarray(np.concatenate([seg(0), seg(1), kr, kr, kr, kr, seg(3), f1, f2, seg(6), seg(7)], axis=1)))
    maps = []
    for b in range(4):
        for p in range(2):
            l1, l2 = (lbf, lbb) if p == 0 else (lbb, lbf)
            v2 = np.zeros((128, 128), f32)
            v2[0:16] = np.asarray(inputs["pre_mlp_g"], f32)[0].reshape(16, 128)
            v2[16:32] = np.asarray(inputs["post_mlp_g"], f32)[0].reshape(16, 128)
            v2[32:48] = c[b].reshape(16, 128)
            v2[48:56] = l1[0].reshape(8, 128); v2[56:64] = l1[1].reshape(8, 128)
            v2[64:72] = l2[0].reshape(8, 128); v2[72:80] = l2[1].reshape(8, 128)
            v2[80:84] = np.asarray(inputs["q_norm_g"], f32)[0].reshape(4, 128)
            v2[84:86] = np.asarray(inputs["kv_norm_g"], f32)[0].reshape(2, 128)
            v2[86] = np.asarray(inputs["hgrn_norm_g"], f32)[0]
            v2[87] = np.tile(invf, 4)
            xs = x[b] if p == 0 else x[b][::-1]
            ps = positions[b] if p == 0 else positions[b][::-1]
            m = dict(common)
            m["x"] = np.ascontiguousarray(xs)
            m["pos"] = np.ascontiguousarray(np.broadcast_to(ps[None, :], (128, TALL))).astype(np.int32)
            m["vecs2"] = v2
            m["w_in"] = win_p[p]
            maps.append(m)
    return maps


def kernel(**inputs):
    if "nc" not in _CACHE:
        kb, _ = build_program()
        kb.fw.emit()
        kb.fw.close()
        _CACHE["nc"] = kb.nc
    nc = _CACHE["nc"]
    maps = _prep_inputs(inputs)
    res = run_bass_kernel_spmd(nc, maps, core_ids=list(range(8)))
    out = np.empty((4, TALL, D), np.float32)
    for b in range(4):
        for p in range(2):
            o = np.asarray(res.results[b * 2 + p]["out"], np.float32)
            if p == 0:
                out[b, 0:TOWN] = o
            else:
                out[b, TOWN:TALL] = o[::-1]
    return out
```

```python
import math
import contextlib
import numpy as np
import concourse.bass as bass
import concourse.mybir as mybir
from concourse.bass_utils import run_bass_kernel_spmd

F32 = mybir.dt.float32
BF16 = mybir.dt.bfloat16
I32 = mybir.dt.int32
AF = mybir.ActivationFunctionType
ALU = mybir.AluOpType

ENGS = ("pe", "act", "dve", "pool", "sp")


class Res:
    __slots__ = ("name", "w", "rs")

    def __init__(self, name):
        self.name = name
        self.w = None
        self.rs = []


class Op:
    __slots__ = ("eng", "fn", "deps", "idx", "signal", "sigval", "dma_lane", "dma_val")

    def __init__(self, eng, fn):
        self.eng = eng
        self.fn = fn
        self.deps = []
        self.idx = None
        self.signal = False
        self.sigval = None
        self.dma_lane = None
        self.dma_val = None


class FW:
    def __init__(self, nc):
        self.nc = nc
        self.ops = {e: [] for e in ENGS}
        self.eh = {"pe": nc.tensor, "act": nc.scalar, "dve": nc.vector, "pool": nc.gpsimd, "sp": nc.sync}
        self.lanes = {}
        self.lane_sems = {}
        self.eng_sems = {}
        self.waited = {e: {} for e in ENGS}
        self.stack = contextlib.ExitStack()

    def res(self, name):
        return Res(name)

    def _tok_needed(self, eng, tok):
        key = (tok[0], tok[1])
        cur = self.waited[eng].get(key, -1)
        if tok[2] <= cur:
            return False
        self.waited[eng][key] = tok[2]
        return True

    def _add(self, eng, fn, reads, writes, dma_lane=None):
        op = Op(eng, fn)
        op.idx = len(self.ops[eng])
        toks = []
        for r in reads:
            if r.w is not None:
                toks.append(r.w)
        for w in writes:
            if w.w is not None:
                toks.append(w.w)
            toks.extend(w.rs)
        for tok in toks:
            if tok[0] == "e" and tok[1] == eng and eng == "pe":
                continue
            if tok[0] == "e" and tok[1] == eng and tok[2] >= op.idx:
                continue
            if self._tok_needed(eng, tok):
                op.deps.append(tok)
        if dma_lane is not None:
            self.lanes[dma_lane] = self.lanes.get(dma_lane, 0) + 1
            op.dma_lane = dma_lane
            op.dma_val = self.lanes[dma_lane] * 16
            mytok = ("d", dma_lane, op.dma_val)
        else:
            mytok = ("e", eng, op.idx)
        for r in reads:
            r.rs.append(mytok)
        for w in writes:
            w.w = mytok
            w.rs = []
        self.ops[eng].append(op)
        return op

    def op(self, eng, fn, reads=(), writes=()):
        return self._add(eng, fn, list(reads), list(writes))

    def dma(self, q, lane, out, in_, reads=(), writes=(), **kw):
        def fn(e):
            return e.dma_start(out=out, in_=in_, **kw)
        return self._add(q, fn, list(reads), list(writes), dma_lane=lane)

    def barrier(self, engs=ENGS):
        toks = []
        for e in ENGS:
            if self.ops[e]:
                toks.append(("e", e, len(self.ops[e]) - 1))
        for lane, cnt in self.lanes.items():
            toks.append(("d", lane, cnt * 16))
        for e in engs:
            op = Op(e, None)
            op.idx = len(self.ops[e])
            for tok in toks:
                if tok[0] == "e" and tok[1] == e:
                    continue
                if self._tok_needed(e, tok):
                    op.deps.append(tok)
            self.ops[e].append(op)

    def emit(self):
        nc = self.nc
        for e in ENGS:
            for op in self.ops[e]:
                for tok in op.deps:
                    if tok[0] == "e":
                        self.ops[tok[1]][tok[2]].signal = True
        for e in ENGS:
            cnt = 0
            ops = self.ops[e]
            for i, op in enumerate(ops):
                if op.signal and (op.fn is None or op.dma_lane is not None):
                    j = i - 1
                    while j >= 0 and (ops[j].fn is None or ops[j].dma_lane is not None):
                        j -= 1
                    op.signal = False
                    if j >= 0:
                        ops[j].signal = True
                        op.sigval = ("alias", j)
                    else:
                        op.sigval = ("zero",)
            for op in ops:
                if op.signal:
                    cnt += 1
                    op.sigval = cnt
            for op in ops:
                if isinstance(op.sigval, tuple):
                    op.sigval = ops[op.sigval[1]].sigval if op.sigval[0] == "alias" else 0
        for e in ENGS:
            self.eng_sems[e] = self.stack.enter_context(nc.semaphore("s_" + e))
        for lane in self.lanes:
            self.lane_sems[lane] = self.stack.enter_context(nc.semaphore("l_" + lane))
        block = self.stack.enter_context(nc.Block())

        def run(e, eh):
            for op in self.ops[e]:
                for tok in op.deps:
                    if tok[0] == "e":
                        v = self.ops[tok[1]][tok[2]].sigval
                        if v:
                            eh.wait_ge(self.eng_sems[tok[1]], v)
                    else:
                        eh.wait_ge(self.lane_sems[tok[1]], tok[2])
                if op.fn is None:
                    continue
                ins = op.fn(eh)
                if op.dma_lane is not None:
                    ins.then_inc(self.lane_sems[op.dma_lane], 16)
                elif op.signal:
                    ins.then_inc(self.eng_sems[e], 1)

        @block.tensor
        def _(eh):
            run("pe", eh)

        @block.scalar
        def _(eh):
            run("act", eh)

        @block.vector
        def _(eh):
            run("dve", eh)

        @block.gpsimd
        def _(eh):
            run("pool", eh)

        @block.sync
        def _(eh):
            run("sp", eh)

    def close(self):
        self.stack.close()


D = 2048
TOWN = 2048
TALL = 4096
SCALE = 192 ** -0.5
EPS = 1e-6
TWO_PI = 2.0 * math.pi
F32R = mybir.dt.float32r


class T:
    def __init__(self, ap, r):
        self.ap = ap
        self.r = r

    def __getitem__(self, k):
        return self.ap[k]


def _rs(lst):
    out = []
    for x in lst:
        if x is None:
            continue
        out.append(x.r if isinstance(x, T) else x)
    return out


class KB:
    def __init__(self, stage=99):
        self.stage = stage
        nc = self.nc = bass.Bass("TRN2", target_bir_lowering=False)
        self.fw = FW(nc)
        n32 = (nc.sbuf_bytes_remaining - 512) // 4
        self.arena = self.fw.stack.enter_context(nc.sbuf_tensor("arena", [128, n32], F32))
        self.n32 = n32
        self.top = 0
        self.ps = [self.fw.stack.enter_context(nc.psum_tensor("ps%d" % i, [128, 512], F32))[:, :] for i in range(8)]
        self.psr = [self.fw.res("ps%d" % i) for i in range(8)]
        self.rot = 0

    def alloc(self, name, shape, dt=F32):
        n = 1
        for s in shape[1:]:
            n *= s
        words = n if dt != BF16 else (n + 1) // 2
        words = (words + 7) // 8 * 8
        assert self.top + words <= self.n32, ("SBUF overflow", name, self.top, words, self.n32)
        ap = self.arena[:, self.top:self.top + words]
        if dt == BF16:
            ap = ap.bitcast(BF16)[:, 0:n]
        elif dt == I32:
            ap = ap.bitcast(I32)[:, 0:n]
        else:
            ap = ap[:, 0:n]
        self.top += words
        if len(shape) == 3:
            ap = ap.rearrange("p (a b) -> p a b", a=shape[1])
        if shape[0] != 128:
            ap = ap[0:shape[0]]
        return T(ap, self.fw.res(name))

    def op(self, eng, fn, reads, writes):
        self.fw.op(eng, fn, _rs(reads), _rs(writes))

    def act(self, out, in_, func, reads, writes, scale=1.0, bias=0.0, accum=None):
        self.op("act", lambda e: e.activation(out=out, in_=in_, func=func, bias=bias, scale=scale, accum_out=accum), reads, writes)

    def ts(self, out, in0, s1, s2, op0, op1, reads, writes, eng="dve"):
        if op1 is None:
            self.op(eng, lambda e: e.tensor_scalar(out=out, in0=in0, scalar1=s1, scalar2=None, op0=op0), reads, writes)
        else:
            self.op(eng, lambda e: e.tensor_scalar(out=out, in0=in0, scalar1=s1, scalar2=s2, op0=op0, op1=op1), reads, writes)

    def tt(self, out, in0, in1, op, reads, writes, eng="dve"):
        self.op(eng, lambda e: e.tensor_tensor(out=out, in0=in0, in1=in1, op=op), reads, writes)

    def stt(self, out, in0, scalar, in1, op0, op1, reads, writes):
        self.op("dve", lambda e: e.scalar_tensor_tensor(out=out, in0=in0, scalar=scalar, in1=in1, op0=op0, op1=op1), reads, writes)

    def cp(self, eng, out, in_, reads, writes):
        if eng == "act":
            self.op("act", lambda e: e.copy(out=out, in_=in_), reads, writes)
        else:
            self.op(eng, lambda e: e.tensor_copy(out=out, in_=in_), reads, writes)

    def mm(self, out, lhsT, rhs, start, stop, reads, writes):
        self.op("pe", lambda e: e.matmul(out, lhsT=lhsT, rhs=rhs, start=start, stop=stop), reads, writes)

    def tr(self, out, in_, reads, writes):
        ident = self.ident
        self.op("pe", lambda e: e.transpose(out=out, in_=in_, identity=ident.ap), list(reads) + [ident], writes)

    def dma(self, q, lane, out, in_, reads, writes):
        self.fw.dma(q, lane, out, in_, _rs(reads), _rs(writes))

    def bank(self, lo=0, hi=8):
        n = hi - lo
        i = lo + (self.rot % n)
        self.rot += 1
        return T(self.ps[i], self.psr[i])

    def rsqrt_mean(self, t, n, reads_writes):
        self.ts(t, t, 1.0 / n, EPS, ALU.mult, ALU.add, reads_writes, reads_writes)
        self.act(t, t, AF.Ln, reads_writes, reads_writes)
        self.act(t, t, AF.Exp, reads_writes, reads_writes, scale=-0.5)


def build_program(stage=99):
    kb = KB(stage)
    nc, fw = kb.nc, kb.fw

    def din(name, shape, dt=F32):
        return nc.dram_tensor(name, shape, dt, kind="ExternalInput").ap()

    def dscr(name, shape, dt):
        return nc.dram_tensor(name, shape, dt, kind="Internal").ap()

    x_d = din("x", [TALL, D])
    pos_d = din("pos", [128, TALL], I32)
    vecs_d = din("vecs", [128, 128])
    vecs2_d = din("vecs2", [128, 128])
    wmod_d = din("w_mod", [D, 6 * D])
    win_d = din("w_in", [D, 6144])
    wuq_d = din("w_uq", [512, 2048])
    wukv_d = din("w_ukv", [256, 2048])
    wout_d = din("w_out", [D, D])
    wup_d = din("w_up", [D, 4 * D])
    wdn_d = din("w_down", [4 * D, D])
    out_d = nc.dram_tensor("out", [TOWN, D], F32, kind="ExternalOutput").ap()

    hq_s = dscr("hq_s", [8, 128, TOWN], BF16)
    z1_s = dscr("z1_s", [8, 128, TOWN], F32)
    z2_s = dscr("z2_s", [8, 128, TALL], F32)
    v_s = dscr("v_s", [TALL, 1024], BF16)
    sg_s = dscr("sg_s", [8, 128, TOWN], BF16)
    cat_s = dscr("cat_s", [16, 128, TOWN], BF16)
    R_hq = [fw.res("hq%d" % h) for h in range(8)]
    R_z1 = [fw.res("z1%d" % h) for h in range(8)]
    R_z2 = [fw.res("z2%d" % h) for h in range(8)]
    R_v = fw.res("v_s")
    R_sg = [fw.res("sg%d" % h) for h in range(8)]
    R_cat = [fw.res("cat%d" % c) for c in range(16)]
    R_out = fw.res("out")

    ident = kb.alloc("ident", [128, 128]); kb.ident = ident
    ones32 = kb.alloc("ones32", [128, 128])
    ones16 = kb.alloc("ones16", [128, 128], BF16)
    ones32r = kb.alloc("ones32r", [128, 128])
    maskF = kb.alloc("maskF", [128, 128])
    maskB = kb.alloc("maskB", [128, 128])
    vT = kb.alloc("vT", [128, 128])
    v2T = kb.alloc("v2T", [128, 128])
    modT = kb.alloc("modT", [128, 96])
    A1 = kb.alloc("A1", [128, 16]); G1 = kb.alloc("G1", [128, 16]); A2 = kb.alloc("A2", [128, 16]); G2 = kb.alloc("G2", [128, 16])
    lbT = kb.alloc("lbT", [128, 16]); omlT = kb.alloc("omlT", [128, 16])
    cond = kb.alloc("cond", [128, 16], BF16)
    mark_small = kb.top
    qnT = kb.alloc("qnT", [128, 4, TOWN], BF16)
    kvnT = kb.alloc("kvnT", [128, 2, TALL], BF16)
    krT = kb.alloc("krT", [128, TALL], BF16)
    CS = kb.alloc("CS", [128, TOWN])
    mark_cd = kb.top
    ring = [kb.alloc("ring%d" % i, [128, 16, 512], BF16) for i in range(3)]
    ringn = [0]

    def load_unit(src_ap):
        i = ringn[0] % 3
        ringn[0] += 1
        kb.dma("pool", "ring%d" % i, ring[i].ap, src_ap.rearrange("(k p) n -> p k n", p=128), [], [ring[i]])
        return ring[i]

    class UnitStream:
        def __init__(self, srcs):
            self.srcs = srcs
            self.tiles = []
            self.pos = 0
            self._fill()

        def _fill(self):
            while len(self.tiles) < len(self.srcs) and len(self.tiles) < self.pos + 2:
                self.tiles.append(load_unit(self.srcs[len(self.tiles)]))

        def get(self):
            self.pos += 1
            self._fill()
            return self.tiles[self.pos - 1]

    kb.op("pool", lambda e: e.memset(ident.ap, 0.0), [], [ident])
    kb.op("pool", lambda e: e.affine_select(out=ident.ap, in_=ident.ap, compare_op=ALU.not_equal, fill=1.0, base=0,
                                            pattern=[[-1, 128]], channel_multiplier=1), [ident], [ident])
    kb.op("pool", lambda e: e.memset(ones32.ap, 1.0), [], [ones32])
    kb.op("pool", lambda e: e.memset(ones16.ap, 1.0), [], [ones16])
    kb.op("pool", lambda e: e.memset(maskF.ap, 1.0), [], [maskF])
    kb.op("pool", lambda e: e.affine_select(out=maskF.ap, in_=maskF.ap, compare_op=ALU.is_ge, fill=0.0, base=0,
                                            pattern=[[1, 128]], channel_multiplier=-1), [maskF], [maskF])
    kb.op("pool", lambda e: e.memset(maskF[0:64, 64:128], 0.0), [maskF], [maskF])
    kb.op("pool", lambda e: e.memset(maskB.ap, 1.0), [], [maskB])
    kb.op("pool", lambda e: e.affine_select(out=maskB.ap, in_=maskB.ap, compare_op=ALU.is_ge, fill=0.0, base=0,
                                            pattern=[[-1, 128]], channel_multiplier=1), [maskB], [maskB])
    kb.op("pool", lambda e: e.memset(maskB[64:128, 0:64], 0.0), [maskB], [maskB])

    mark_global = kb.top

    tmpv = kb.alloc("tmpv", [128, 128])
    for src, dst in ((vecs_d, vT), (vecs2_d, v2T)):
        kb.dma("sp", "ltmpv", tmpv.ap, src, [], [tmpv])
        b = kb.bank()
        kb.tr(b[:, 0:128], tmpv.ap, [tmpv], [b])
        kb.cp("dve", dst.ap, b[:, 0:128], [b], [dst])
    e16 = kb.alloc("e16", [128, 16])
    cT = v2T[:, 32:48]
    kb.act(e16.ap, cT, AF.Exp, [v2T], [e16], scale=-1.0)
    kb.ts(e16.ap, e16.ap, 1.0, None, ALU.add, None, [e16], [e16])
    kb.op("dve", lambda e: e.reciprocal(out=e16.ap, in_=e16.ap), [e16], [e16])
    kb.tt(cond.ap, cT, e16.ap, ALU.mult, [v2T, e16], [cond])
    pm = kb.bank()
    NU = 8
    units = [None] * NU
    units[0] = load_unit(wmod_d[:, 0:512])
    units[1] = load_unit(wmod_d[:, 512:1024])
    for u in range(NU):
        if u + 2 < NU:
            units[u + 2] = load_unit(wmod_d[:, (u + 2) * 512:(u + 3) * 512])
        un = units[u]
        for cc in range(4):
            j = u * 4 + cc
            for k in range(16):
                kb.mm(pm[:, j:j + 1], un[:, k, cc * 128:(cc + 1) * 128], cond[:, k:k + 1], k == 0, k == 15, [un, cond], [pm])
    kb.tt(modT[:, 0:32], pm[:, 0:32], vT[:, 0:32], ALU.add, [pm, vT], [modT])
    shA = modT[:, 0:16]; shM = modT[:, 48:64]
    kb.stt(A1.ap, modT[:, 16:32], 1.0, vT[:, 96:112], ALU.add, ALU.mult, [modT, vT], [A1])
    for dI in range(2):
        l0 = v2T[:, 48 + 16 * dI:56 + 16 * dI]; l1 = v2T[:, 56 + 16 * dI:64 + 16 * dI]
        o = lbT[:, 8 * dI:8 * dI + 8]
        kb.tt(o, l1, l0, ALU.subtract, [v2T], [lbT])
        kb.act(o, o, AF.Exp, [lbT], [lbT])
        kb.ts(o, o, 1.0, None, ALU.add, None, [lbT], [lbT])
        kb.op("dve", lambda e, o=o: e.reciprocal(out=o, in_=o), [lbT], [lbT])
    kb.ts(omlT.ap, lbT.ap, -1.0, 1.0, ALU.mult, ALU.add, [lbT], [omlT])
    qg = v2T[:, 80:84]; kvg = v2T[:, 84:86]; hgn = v2T[:, 86:87]; invf = v2T[:, 87:88]

    kb.top = mark_global + 0
    tmpv = None
    hT = kb.alloc("hT", [128, 16, 1024], BF16)
    xb = [kb.alloc("xb%d" % i, [128, D]) for i in range(2)]
    junk = kb.alloc("junk", [128, D], BF16)
    ss1 = kb.alloc("ss1", [128, 1])
    posi = kb.alloc("posi", [128, 1024], I32)
    ang = kb.alloc("ang", [128, 1024])
    kf = kb.alloc("kf", [128, 1024])
    kiT = kb.alloc("kiT", [128, 1024], I32)
    cosT = kb.alloc("cosT", [128, 1024]); sinT = kb.alloc("sinT", [128, 1024])
    sq = [kb.alloc("sq%d" % i, [128, 512]) for i in range(2)]
    rsb = kb.alloc("rsb", [128, 512])
    tA = kb.alloc("tA", [128, 512]); tB = kb.alloc("tB", [128, 512])
    stg32 = [kb.alloc("stg32_%d" % i, [128, 1024]) for i in range(2)]
    stg16 = [kb.alloc("stg16_%d" % i, [128, 1024], BF16) for i in range(2)]
    stgv = [kb.alloc("stgv%d" % i, [128, 4, 512], BF16) for i in range(2)]
    cnt = {"s32": 0, "s16": 0, "sv": 0, "sq": 0}

    def sincos(dst, shift):
        kb.ts(kf.ap, ang.ap, shift, 1.0 / TWO_PI, ALU.add, ALU.mult, [ang], [kf])
        kb.cp("dve", kiT.ap, kf.ap, [kf], [kiT])
        kb.cp("dve", kf.ap, kiT.ap, [kiT], [kf])
        kb.stt(dst.ap, kf.ap, -TWO_PI, ang.ap, ALU.mult, ALU.add, [kf, ang], [dst])
        kb.ts(dst.ap, dst.ap, shift, None, ALU.add, None, [dst], [dst])
        kb.ts(kf.ap, dst.ap, math.pi, TWO_PI, ALU.is_gt, ALU.mult, [dst], [kf])
        kb.tt(dst.ap, dst.ap, kf.ap, ALU.subtract, [dst, kf], [dst])
        kb.ts(kf.ap, dst.ap, -math.pi, TWO_PI, ALU.is_lt, ALU.mult, [dst], [kf])
        kb.tt(dst.ap, dst.ap, kf.ap, ALU.add, [dst, kf], [dst])
        kb.act(dst.ap, dst.ap, AF.Sin, [dst], [dst])

    def stats_norm(banks, ncc, nfeat, gcols, dstT, t0):
        sb = kb.bank(4, 8)
        for cc in range(ncc):
            s = sq[cnt["sq"] % 2]; cnt["sq"] += 1
            kb.act(s.ap, banks[cc].ap, AF.Square, [banks[cc]], [s])
            kb.mm(sb.ap, ones32.ap, s.ap, cc == 0, cc == ncc - 1, [ones32, s], [sb])
        kb.cp("dve", rsb.ap, sb.ap, [sb], [rsb])
        kb.rsqrt_mean(rsb.ap, nfeat, [rsb])
        for cc in range(ncc):
            kb.stt(dstT[:, cc, t0:t0 + 512], banks[cc].ap, gcols[:, cc:cc + 1], rsb.ap, ALU.mult, ALU.mult,
                   [banks[cc], rsb, v2T], [dstT])

    srcsB = []
    for g in range(4):
        for u in (list(range(12)) if g < 2 else [6, 7, 8, 9, 1]):
            srcsB.append(win_d[:, u * 512:(u + 1) * 512])
    streamB = UnitStream(srcsB)
    for g in range(4):
        own = g < 2
        tok0 = g * 1024
        def emit_tables():
            kb.dma("sp", "lposi", posi.ap, pos_d[:, tok0:tok0 + 1024], [], [posi])
            kb.cp("dve", ang.ap, posi.ap, [posi], [ang])
            kb.ts(ang.ap, ang.ap, invf, None, ALU.mult, None, [ang, v2T], [ang])
            sincos(sinT, 0.0)
            sincos(cosT, math.pi / 2)
            if own:
                kb.cp("act", CS[0:64, tok0:tok0 + 1024], cosT[0:64, :], [cosT], [CS])
                kb.cp("act", CS[64:128, tok0:tok0 + 1024], sinT[64:128, :], [sinT], [CS])

        def stage1(i):
            xt = xb[i % 2]
            r0 = tok0 + i * 128
            kb.dma("sp", "x%d" % (i % 2), xt.ap, x_d[r0:r0 + 128, :], [], [xt])
            kb.act(junk.ap, xt.ap, AF.Square, [xt], [junk, ss1], accum=ss1.ap)
            kb.rsqrt_mean(ss1.ap, D, [ss1])
            kb.ts(xt.ap, xt.ap, ss1[:, 0:1], None, ALU.mult, None, [xt, ss1], [xt])

        stage1(0)
        for i in range(8):
            xt = xb[i % 2]
            if i + 1 < 8:
                stage1(i + 1)
            for c4 in range(4):
                b = kb.bank(0, 4)
                for j in range(4):
                    c = c4 * 4 + j
                    kb.tr(b[:, j * 128:(j + 1) * 128], xt[:, c * 128:(c + 1) * 128], [xt], [b])
                for j in range(4):
                    c = c4 * 4 + j
                    o = hT[:, c, i * 128:(i + 1) * 128]
                    if j % 2 == 0:
                        kb.act(o, b[:, j * 128:(j + 1) * 128], AF.Identity, [b, A1, modT], [hT], scale=A1[:, c:c + 1], bias=shA[:, c:c + 1])
                    else:
                        kb.ts(o, b[:, j * 128:(j + 1) * 128], A1[:, c:c + 1], shA[:, c:c + 1], ALU.mult, ALU.add, [b, A1, modT], [hT])
        ulist = list(range(12)) if own else [6, 7, 8, 9, 1]
        for n_, u in enumerate(ulist):
            un = streamB.get()
            if n_ == 1:
                emit_tables()

            def fm(cc, tb, b):
                for k in range(16):
                    kb.mm(b.ap, un[:, k, cc * 128:(cc + 1) * 128], hT[:, k, tb * 512:(tb + 1) * 512], k == 0, k == 15, [un, hT], [b])

            if u == 1:
                for o0 in (384, 448):
                    kb.ts(un[:, :, o0:o0 + 32], un[:, :, 288:320], -1.0, None, ALU.mult, None, [un], [un])
                    kb.cp("dve", un[:, :, o0 + 32:o0 + 64], un[:, :, 256:288], [un], [un])
            if u in (0, 1):
                for tb in range(2):
                    t0 = tok0 + tb * 512
                    banks = []
                    for cc in range(4):
                        b = kb.bank(0, 4)
                        fm(cc, tb, b)
                        banks.append(b)
                    if u == 0:
                        stats_norm(banks, 4, 512, qg, qnT, t0)
                    else:
                        stats_norm(banks[0:2], 2, 256, kvg, kvnT, t0)
                        kb.tt(tA.ap, banks[2].ap, cosT[:, tb * 512:(tb + 1) * 512], ALU.mult, [banks[2], cosT], [tA])
                        kb.tt(tB.ap, banks[3].ap, sinT[:, tb * 512:(tb + 1) * 512], ALU.mult, [banks[3], sinT], [tB])
                        kb.tt(krT[:, t0:t0 + 512], tA.ap, tB.ap, ALU.add, [tA, tB], [krT])
            elif u in (2, 3, 10, 11):
                for cc in range(4):
                    head = (u % 2) * 4 + cc
                    s = stg16[cnt["s16"] % 2]; lane = "s16_%d" % (cnt["s16"] % 2); cnt["s16"] += 1
                    for tb in range(2):
                        b = kb.bank(0, 4)
                        fm(cc, tb, b)
                        o = s[:, tb * 512:(tb + 1) * 512]
                        if u < 4:
                            kb.cp("act", o, b.ap, [b], [s])
                        else:
                            kb.act(tA.ap, b.ap, AF.Exp, [b], [tA], scale=-1.0)
                            kb.act(tA.ap, tA.ap, AF.Ln, [tA, ones32], [tA], bias=ones32[:, 0:1])
                            kb.act(tA.ap, tA.ap, AF.Exp, [tA], [tA], scale=-1.0)
                            kb.tt(o, b.ap, tA.ap, ALU.mult, [b, tA], [s])
                    if u < 4:
                        kb.dma("sp", lane, hq_s[head][:, tok0:tok0 + 1024], s.ap, [s], [R_hq[head]])
                    else:
                        kb.dma("sp", lane, sg_s[head][:, tok0:tok0 + 1024], s.ap, [s], [R_sg[head]])
            elif u in (4, 5, 6, 7):
                for cc in range(4):
                    head = (u % 2) * 4 + cc
                    s = stg32[cnt["s32"] % 2]; lane = "s32_%d" % (cnt["s32"] % 2); cnt["s32"] += 1
                    for tb in range(2):
                        b = kb.bank(0, 4)
                        fm(cc, tb, b)
                        kb.cp("act" if tb == 0 else "dve", s[:, tb * 512:(tb + 1) * 512], b.ap, [b], [s])
                    if u < 6:
                        kb.dma("sp", lane, z1_s[head][:, tok0:tok0 + 1024], s.ap, [s], [R_z1[head]])
                    else:
                        kb.dma("sp", lane, z2_s[head][:, tok0:tok0 + 1024], s.ap, [s], [R_z2[head]])
            else:
                for i4 in range(2):
                    s = stgv[cnt["sv"] % 2]; lane = "sv_%d" % (cnt["sv"] % 2); cnt["sv"] += 1
                    for ii in range(4):
                        i = i4 * 4 + ii
                        b = kb.bank(0, 4)
                        for k in range(16):
                            kb.mm(b.ap, hT[:, k, i * 128:(i + 1) * 128], un[:, k, :], k == 0, k == 15, [un, hT], [b])
                        kb.cp("act" if ii % 2 == 0 else "dve", s[:, ii, :], b.ap, [b], [s])
                    r0 = tok0 + i4 * 512
                    kb.dma("sp", lane, v_s[r0:r0 + 512, (u - 8) * 512:(u - 7) * 512].rearrange("(t p) c -> p t c", p=128), s.ap, [s], [R_v])
    fw.barrier()
    if stage <= 1:
        return kb, dict(qnT=qnT, kvnT=kvnT, krT=krT, CS=CS)

    kb.top = mark_cd
    wqh = [kb.alloc("wqh%d" % i, [128, 4, 256], BF16) for i in range(2)]
    wkvh = [kb.alloc("wkvh%d" % i, [128, 2, 256], BF16) for i in range(2)]
    Kt = kb.alloc("Kt", [128, TALL], BF16)
    Vh = kb.alloc("Vh", [128, 32, 128], BF16)
    Qn = kb.alloc("Qn", [128, TOWN], BF16)
    Qr = kb.alloc("Qr", [128, TOWN], BF16)
    Pt = [kb.alloc("Pt%d" % i, [128, 512], BF16) for i in range(3)]
    rl = kb.alloc("rl", [128, 512])
    cst = [kb.alloc("cst%d" % i, [128, 512], BF16) for i in range(2)]
    mring = [kb.alloc("mring%d" % i, [128, 16, 128], BF16) for i in range(2)]
    modraw = kb.alloc("modraw", [128, 64])
    segF = kb.alloc("segF", [128, 1024]); segB = kb.alloc("segB", [128, 1024])
    Z = [kb.alloc("Z%d" % i, [128, TOWN]) for i in range(3)]
    W2 = kb.alloc("W2", [128, 1024]); W3 = kb.alloc("W3", [128, 1024])
    hqt = kb.alloc("hqt", [128, TOWN], BF16)
    sgt = kb.alloc("sgt", [128, TOWN], BF16)
    vt = kb.alloc("vt", [128, 32, 128], BF16)
    qtl = [kb.alloc("qtl%d" % i, [128, TOWN], BF16) for i in range(2)]
    ktl = [kb.alloc("ktl%d" % i, [128, TOWN], BF16) for i in range(2)]
    ebl = [kb.alloc("ebl%d" % i, [128, 32]) for i in range(3)]
    Sall = [kb.alloc("Sall%d" % i, [128, 32, 128], BF16) for i in range(2)]
    Sst = [kb.alloc("Sst%d" % i, [128, 128]) for i in range(2)]
    khtok = [kb.alloc("khtok%d" % i, [128, 128], BF16) for i in range(3)]
    Am = [kb.alloc("Am%d" % i, [128, 128], BF16) for i in range(6)]
    rst = [kb.alloc("rst%d" % i, [128, 512], BF16) for i in range(2)]
    otmp = kb.alloc("otmp", [128, 512])
    sqD = kb.alloc("sqD", [128, 512]); rsbD = kb.alloc("rsbD", [128, 512])
    kb.op("pool", lambda e: e.memset(segF.ap, 1.0), [], [segF])
    kb.op("pool", lambda e: e.memset(segF.ap.rearrange("p (c t) -> p c t", t=64)[:, :, 0:1], 0.0), [segF], [segF])
    kb.op("pool", lambda e: e.memset(segB.ap, 1.0), [], [segB])
    kb.op("pool", lambda e: e.memset(segB.ap.rearrange("p (c t) -> p c t", t=64)[:, :, 63:64], 0.0), [segB], [segB])
    qn_ = [0]
    HB = (4, 5, 7)

    def qbank(full=False):
        i = HB[qn_[0] % 3]
        qn_[0] += 1
        return T(kb.ps[i] if full else kb.ps[i][:, 0:128], kb.psr[i])

    def gen_attn():
        for h in range(8):
            wq_ = wqh[h % 2]; wk_ = wkvh[h % 2]
            kb.dma("pool", "wqh%d" % (h % 2), wq_.ap, wuq_d[:, h * 256:(h + 1) * 256].rearrange("(k p) n -> p k n", p=128), [], [wq_])
            kb.dma("pool", "wkvh%d" % (h % 2), wk_.ap, wukv_d[:, h * 256:(h + 1) * 256].rearrange("(k p) n -> p k n", p=128), [], [wk_])
            kb.ts(wq_[:, :, 192:224], wq_[:, :, 160:192], -1.0, None, ALU.mult, None, [wq_], [wq_])
            kb.cp("dve", wq_[:, :, 224:256], wq_[:, :, 128:160], [wq_], [wq_])
            yield
            for kg in range(8):
                b = kb.bank(0, 2)
                for c in range(2):
                    kb.mm(b.ap, wk_[:, c, 0:128], kvnT[:, c, kg * 512:(kg + 1) * 512], c == 0, c == 1, [wk_, kvnT], [b])
                kb.cp("act" if kg % 2 == 0 else "dve", Kt[:, kg * 512:(kg + 1) * 512], b.ap, [b], [Kt])
                yield
            for k4 in range(8):
                b = kb.bank(0, 2)
                for j in range(4):
                    kbk = k4 * 4 + j
                    for c in range(2):
                        kb.mm(b[:, j * 128:(j + 1) * 128], kvnT[:, c, kbk * 128:(kbk + 1) * 128], wk_[:, c, 128:256], c == 0, c == 1, [wk_, kvnT], [b])
                kb.cp("act" if k4 % 2 == 0 else "dve", Vh[:, k4 * 4:(k4 + 1) * 4, :].rearrange("p a b -> p (a b)"), b.ap, [b], [Vh])
                yield
            for tb in range(4):
                b = kb.bank(0, 2)
                for c in range(4):
                    kb.mm(b.ap, wq_[:, c, 0:128], qnT[:, c, tb * 512:(tb + 1) * 512], c == 0, c == 3, [wq_, qnT], [b])
                kb.cp("act", Qn[:, tb * 512:(tb + 1) * 512], b.ap, [b], [Qn])
                b = kb.bank(0, 2)
                for c in range(4):
                    kb.mm(b.ap, wq_[:, c, 128:256], qnT[:, c, tb * 512:(tb + 1) * 512], c == 0, c == 3, [wq_, qnT], [b])
                kb.tt(Qr[:, tb * 512:(tb + 1) * 512], b.ap, CS[:, tb * 512:(tb + 1) * 512], ALU.mult, [b, CS], [Qr])
                yield
            for qgp in range(4):
                qs = slice(qgp * 512, (qgp + 1) * 512)
                Ob = T(kb.ps[2], kb.psr[2])
                Lb = T(kb.ps[3], kb.psr[3])

                def score(kbk):
                    b = T(kb.ps[kbk % 2], kb.psr[kbk % 2])
                    kb.mm(b.ap, Kt[:, kbk * 128:(kbk + 1) * 128], Qn[:, qs], True, False, [Kt, Qn], [b])
                    kb.mm(b.ap, krT[:, kbk * 128:(kbk + 1) * 128], Qr[:, qs], False, True, [krT, Qr], [b])
                    return b
                sb_next = score(0)
                for kbk in range(32):
                    sbk = sb_next
                    if kbk + 1 < 32:
                        sb_next = score(kbk + 1)
                    p = Pt[kbk % 3]
                    kb.act(p.ap, sbk.ap, AF.Exp, [sbk], [p], scale=SCALE)
                    kb.mm(Ob.ap, Vh[:, kbk, :], p.ap, kbk == 0, kbk == 31, [Vh, p], [Ob])
                    kb.mm(Lb.ap, ones16.ap, p.ap, kbk == 0, kbk == 31, [ones16, p], [Lb])
                    yield
                cs_ = cst[qgp % 2]
                kb.act(rl.ap, Lb.ap, AF.Ln, [Lb], [rl])
                kb.act(rl.ap, rl.ap, AF.Exp, [rl], [rl], scale=-1.0)
                kb.tt(cs_.ap, Ob.ap, rl.ap, ALU.mult, [Ob, rl], [cs_])
                kb.dma("sp", "cst%d" % (qgp % 2), cat_s[h][:, qs], cs_.ap, [cs_], [R_cat[h]])
                yield

    def chain_half(zt, hf, zi, dI, h, fwd, with_q):
        sl = slice(hf * 1024, (hf + 1) * 1024)
        z = zt[:, sl]
        lb = lbT[:, 8 * dI + h:8 * dI + h + 1]; oml = omlT[:, 8 * dI + h:8 * dI + h + 1]
        seg = segF if fwd else segB
        kb.act(z, z, AF.Exp, [zt], [zt], scale=-1.0)
        yield
        kb.act(z, z, AF.Ln, [zt, ones32], [zt], bias=ones32[:, 0:1])
        yield
        kb.act(z, z, AF.Exp, [zt], [zt], scale=-1.0)
        kb.ts(z, z, oml, lb, ALU.mult, ALU.add, [zt, lbT, omlT], [zt])
        yield
        kb.act(W2.ap, z, AF.Ln, [zt], [W2])
        if fwd:
            kb.op("dve", lambda e: e.tensor_tensor_scan(out=W3.ap, data0=seg.ap, data1=W2.ap, initial=0.0, op0=ALU.mult, op1=ALU.add), [seg, W2], [W3])
        else:
            kb.op("dve", lambda e: e.tensor_tensor_scan(out=W3[:, ::-1], data0=seg[:, ::-1], data1=W2[:, ::-1], initial=0.0, op0=ALU.mult, op1=ALU.add), [seg, W2], [W3])
        yield
        kb.ts(z, z, -1.0, 1.0, ALU.mult, ALU.add, [zt], [zt])
        kb.act(W2.ap, W3.ap, AF.Exp, [W3], [W2])
        yield
        kb.act(W3.ap, W3.ap, AF.Exp, [W3], [W3], scale=-1.0)
        yield
        e3 = W2.ap.rearrange("p (c t) -> p c t", t=64)
        eb_ = ebl[zi][:, hf * 16:(hf + 1) * 16]
        kb.cp("dve", eb_, e3[:, :, 63 if fwd else 0], [W2], [ebl[zi]])
        if with_q:
            kb.tt(qtl[dI][:, sl], hqt[:, sl], W2.ap, ALU.mult, [hqt, W2], [qtl[dI]])
        yield
        kb.tt(z, z, W3.ap, ALU.mult, [zt, W3], [zt])
        if with_q:
            kb.cp("act", ktl[dI][:, sl], z, [zt], [ktl[dI]])
        z3 = z.rearrange("p (c t) -> p c t", t=64)
        kb.tt(z3, z3, eb_.unsqueeze(2).broadcast_to([128, 16, 64]), ALU.mult, [zt, ebl[zi]], [zt])
        yield

    def gen_hgrn():
        kcnt = 0; acnt = 0
        for h in range(8):
            kb.dma("sp", "dz0", Z[0].ap, z1_s[h], [R_z1[h]], [Z[0]])
            kb.dma("sp", "dz1", Z[1].ap, z2_s[h][:, 0:TOWN], [R_z2[h]], [Z[1]])
            kb.dma("sp", "dz2", Z[2].ap, z2_s[h][:, TOWN:TALL], [R_z2[h]], [Z[2]])
            kb.dma("sp", "dhq", hqt.ap, hq_s[h], [R_hq[h]], [hqt])
            kb.dma("sp", "dsg", sgt.ap, sg_s[h], [R_sg[h]], [sgt])
            kb.dma("sp", "dv", vt.ap, v_s[:, h * 128:(h + 1) * 128].rearrange("(t p) c -> p t c", p=128), [R_v], [vt])
            yield
            for zi, dI, fwd, with_q in ((0, 0, True, True), (1, 1, False, True), (2, 1, False, False)):
                for hf in range(2):
                    for _ in chain_half(Z[zi], hf, zi, dI, h, fwd, with_q):
                        yield
            for dI, seq in ((0, [(0, i) for i in range(16)]), (1, [(2, i) for i in range(15, -1, -1)] + [(1, i) for i in range(15, -1, -1)])):
                fwd = dI == 0
                scur = Sst[0]; snext = Sst[1]
                kb.op("pool", lambda e, scur=scur: e.memset(scur.ap, 0.0), [], [scur])
                if fwd:
                    kb.op("pool", lambda e: e.memset(Sall[0][:, 0, :], 0.0), [], [Sall[0]])

                def xpose(zi, i):
                    nonlocal kcnt
                    b = qbank()
                    kb.tr(b.ap, Z[zi][:, i * 128:(i + 1) * 128], [Z[zi]], [b])
                    kt_ = khtok[kcnt % 3]; kcnt += 1
                    kb.cp("dve", kt_.ap, b.ap, [b], [kt_])
                    return kt_
                kt_next = xpose(*seq[0])
                for n_, (zi, i) in enumerate(seq):
                    kt_ = kt_next
                    if n_ + 1 < len(seq):
                        kt_next = xpose(*seq[n_ + 1])
                    voff = 16 if zi == 2 else 0
                    for a in ((0, 1) if fwd else (1, 0)):
                        c = 2 * i + a
                        db = qbank()
                        kb.mm(db.ap, kt_[a * 64:(a + 1) * 64, :], vt[a * 64:(a + 1) * 64, voff + i, :], True, True, [kt_, vt], [db])
                        kb.stt(snext.ap, scur.ap, ebl[zi][:, c:c + 1], db.ap, ALU.mult, ALU.add, [scur, ebl[zi], db], [snext])
                        scur, snext = snext, scur
                        if fwd:
                            if c + 1 < 32:
                                kb.cp("pool", Sall[0][:, c + 1, :], scur.ap, [scur], [Sall[0]])
                        else:
                            if zi == 2 and c == 0:
                                kb.cp("pool", Sall[1][:, 31, :], scur.ap, [scur], [Sall[1]])
                            elif zi == 1 and c >= 1:
                                kb.cp("pool", Sall[1][:, c - 1, :], scur.ap, [scur], [Sall[1]])
                    yield
            ob = T(kb.ps[6], kb.psr[6])

            def amats(i):
                nonlocal acnt
                ts_ = slice(i * 128, (i + 1) * 128)
                ams = []
                for dI in range(2):
                    ab = qbank()
                    kb.mm(ab.ap, ktl[dI][:, ts_], qtl[dI][:, ts_], True, True, [ktl[dI], qtl[dI]], [ab])
                    am = Am[acnt % 6]; acnt += 1
                    msk = maskF if dI == 0 else maskB
                    kb.tt(am.ap, ab.ap, msk.ap, ALU.mult, [ab, msk], [am])
                    ams.append(am)
                return ams
            ams_next = amats(0)
            for i in range(16):
                ii = i % 4; i4 = i // 4
                ams = ams_next
                if i + 1 < 16:
                    ams_next = amats(i + 1)
                oc = ob[:, ii * 128:(ii + 1) * 128]
                first = True
                for dI in range(2):
                    kb.mm(oc, vt[:, i, :], ams[dI].ap, first, False, [vt, ams[dI]], [ob])
                    first = False
                    for a in range(2):
                        c = 2 * i + a
                        last = (dI == 1 and a == 1)
                        kb.mm(ob[:, ii * 128 + a * 64:ii * 128 + (a + 1) * 64], Sall[dI][:, c, :], qtl[dI][:, i * 128 + a * 64:i * 128 + (a + 1) * 64],
                              False, last, [Sall[dI], qtl[dI]], [ob])
                yield
                if ii == 3:
                    s = sqD
                    kb.act(s.ap, ob.ap, AF.Square, [ob], [s])
                    sb = qbank(full=True)
                    kb.mm(sb.ap, ones32.ap, s.ap, True, True, [ones32, s], [sb])
                    kb.cp("dve", rsbD.ap, sb.ap, [sb], [rsbD])
                    kb.rsqrt_mean(rsbD.ap, 128, [rsbD])
                    kb.stt(otmp.ap, ob.ap, hgn, rsbD.ap, ALU.mult, ALU.mult, [ob, rsbD, v2T], [otmp])
                    rs_ = rst[i4 % 2]
                    kb.tt(rs_.ap, otmp.ap, sgt[:, i4 * 512:(i4 + 1) * 512], ALU.mult, [otmp, sgt], [rs_])
                    kb.dma("sp", "rst%d" % (i4 % 2), cat_s[8 + h][:, i4 * 512:(i4 + 1) * 512], rs_.ap, [rs_], [R_cat[8 + h]])
                    yield

    def gen_modrest():

        def ld(j):
            s = mring[j % 2]
            kb.dma("pool", "mr%d" % (j % 2), s.ap, wmod_d[:, j * 128:(j + 1) * 128].rearrange("(k p) n -> p k n", p=128), [], [s])
            return s
        nxt = ld(32)
        for j in range(32, 96):
            un = nxt
            if j + 1 < 96:
                nxt = ld(j + 1)
            pm2 = qbank(full=True)
            for k in range(16):
                kb.mm(pm2[:, 0:1], un[:, k, :], cond[:, k:k + 1], k == 0, k == 15, [un, cond], [pm2])
            kb.cp("act", modraw[:, j - 32:j - 31], pm2[:, 0:1], [pm2], [modraw])
            yield
        kb.tt(modT[:, 32:96], modraw.ap, vT[:, 32:96], ALU.add, [modraw, vT], [modT])
        kb.tt(G1.ap, modT[:, 32:48], vT[:, 112:128], ALU.mult, [modT, vT], [G1])
        kb.stt(A2.ap, modT[:, 64:80], 1.0, v2T[:, 0:16], ALU.add, ALU.mult, [modT, v2T], [A2])
        kb.tt(G2.ap, modT[:, 80:96], v2T[:, 16:32], ALU.mult, [modT, v2T], [G2])
        yield

    gens = [(gen_attn(), 1), (gen_hgrn(), 1), (gen_modrest(), 6)]
    live = [True] * len(gens)
    rnd = 0
    while any(live):
        for gi, (g_, stride) in enumerate(gens):
            if live[gi] and rnd % stride == 0:
                try:
                    next(g_)
                except StopIteration:
                    live[gi] = False
        rnd += 1
    fw.barrier()
    if stage <= 3:
        return kb, {}

    kb.top = mark_small
    ring[:] = [kb.alloc("ringE%d" % i, [128, 16, 512], BF16) for i in range(3)]
    x1T = kb.alloc("x1T", [128, 16, 512])
    r32 = kb.alloc("r32", [128, 16, 512])
    r32b = r32.ap.rearrange("p a b -> p (a b)").bitcast(BF16)
    catg = T(r32b[:, 0:8192].rearrange("p (a b) -> p a b", a=16), r32.r)
    h2T = T(r32b[:, 8192:16384].rearrange("p (a b) -> p a b", a=16), r32.r)
    yT = r32
    uT = kb.alloc("uT", [128, 64, 512], BF16)
    mixT = T(uT.ap.rearrange("p a b -> p (a b)")[:, 0:16384].bitcast(F32).rearrange("p (a b) -> p a b", a=16), uT.r)
    xh = [kb.alloc("xh%d" % i, [128, 1024]) for i in range(2)]
    sq = [kb.alloc("sqE%d" % i, [128, 512]) for i in range(2)]
    rsE = kb.alloc("rsE", [128, 512])
    tEs = [kb.alloc("tE%d" % i, [128, 512]) for i in range(3)]
    rel = [kb.alloc("rel%d" % i, [128, 512], BF16) for i in range(2)]
    xcnt = [0]; scnt = [0]

    def stats(srcs_fn, n_chunks):
        sb = kb.bank(6, 8)
        for c in range(n_chunks):
            s = sq[scnt[0] % 2]; scnt[0] += 1
            src, rr = srcs_fn(c)
            kb.act(s.ap, src, AF.Square, rr, [s])
            kb.mm(sb.ap, ones32.ap, s.ap, c == 0, c == n_chunks - 1, [ones32, s], [sb])
        kb.cp("dve", rsE.ap, sb.ap, [sb], [rsE])
        kb.rsqrt_mean(rsE.ap, D, [rsE])

    srcsE = []
    for tg in range(4):
        srcsE += [wout_d[:, u * 512:(u + 1) * 512] for u in range(4)]
        srcsE += [wup_d[:, u * 512:(u + 1) * 512] for u in range(16)]
        srcsE += [wdn_d[rg * 2048:(rg + 1) * 2048, cs * 512:(cs + 1) * 512] for cs in range(4) for rg in range(4)]
    streamE = UnitStream(srcsE)
    for tg in range(4):
        t0 = tg * 512
        kb.dma("sp", "catg", catg.ap, cat_s[:, :, t0:t0 + 512].rearrange("c p t -> p c t"), R_cat, [catg])
        for j in range(4):
            for hf in range(2):
                xt = xh[xcnt[0] % 2]; lane = "xh%d" % (xcnt[0] % 2); xcnt[0] += 1
                r0 = t0 + j * 128
                kb.dma("sp", lane, xt.ap, x_d[r0:r0 + 128, hf * 1024:(hf + 1) * 1024], [], [xt])
                for c4 in range(2):
                    b = kb.bank(0, 4)
                    for jj in range(4):
                        kb.tr(b[:, jj * 128:(jj + 1) * 128], xt[:, (c4 * 4 + jj) * 128:(c4 * 4 + jj + 1) * 128], [xt], [b])
                    cb = hf * 8 + c4 * 4
                    kb.cp("act" if c4 == 0 else "dve", x1T[:, cb:cb + 4, j * 128:(j + 1) * 128],
                          b.ap.rearrange("p (a b) -> p a b", a=4), [b], [x1T])
        for u in range(4):
            un = streamE.get()
            for cc in range(4):
                c = u * 4 + cc
                b = kb.bank(0, 4)
                for k in range(16):
                    kb.mm(b.ap, un[:, k, cc * 128:(cc + 1) * 128], catg[:, k, :], k == 0, k == 15, [un, catg], [b])
                kb.cp("act", mixT[:, c, :], b.ap, [b], [mixT])
        stats(lambda c: (mixT[:, c, :], [mixT]), 16)
        for c in range(16):
            tE = tEs[c % 3]
            kb.tt(tE.ap, mixT[:, c, :], rsE.ap, ALU.mult, [mixT, rsE], [tE], eng="pool")
            kb.stt(x1T[:, c, :], tE.ap, G1[:, c:c + 1], x1T[:, c, :], ALU.mult, ALU.add, [x1T, tE, G1], [x1T])
        stats(lambda c: (x1T[:, c, :], [x1T]), 16)
        for c in range(16):
            tE = tEs[c % 3]
            kb.tt(tE.ap, x1T[:, c, :], rsE.ap, ALU.mult, [x1T, rsE], [tE])
            kb.act(h2T[:, c, :], tE.ap, AF.Identity, [tE, A2, modT], [h2T], scale=A2[:, c:c + 1], bias=shM[:, c:c + 1])
        for u in range(16):
            un = streamE.get()
            for cc in range(4):
                j = u * 4 + cc
                b = kb.bank(0, 4)
                for k in range(16):
                    kb.mm(b.ap, un[:, k, cc * 128:(cc + 1) * 128], h2T[:, k, :], k == 0, k == 15, [un, h2T], [b])
                r = rel[j % 2]
                kb.act(r.ap, b.ap, AF.Relu, [b], [r])
                kb.tt(uT[:, j, :], r.ap, r.ap, ALU.mult, [r], [uT])
        for cs in range(4):
            accs = [T(kb.ps[i], kb.psr[i]) for i in range(4)]
            for rg in range(4):
                un = streamE.get()
                for k in range(16):
                    j = rg * 16 + k
                    for cc in range(4):
                        kb.mm(accs[cc].ap, un[:, k, cc * 128:(cc + 1) * 128], uT[:, j, :], j == 0, j == 63, [un, uT], [accs[cc]])
            for cc in range(4):
                kb.cp("act" if cc % 2 == 0 else "dve", yT[:, cs * 4 + cc, :], accs[cc].ap, [accs[cc]], [yT])
        stats(lambda c: (yT[:, c, :], [yT]), 16)
        for c in range(16):
            tE = tEs[c % 3]
            kb.tt(tE.ap, yT[:, c, :], rsE.ap, ALU.mult, [yT, rsE], [tE], eng="pool")
            kb.stt(x1T[:, c, :], tE.ap, G2[:, c:c + 1], x1T[:, c, :], ALU.mult, ALU.add, [x1T, tE, G2], [x1T])
        for j in range(4):
            for hf in range(2):
                xt = xh[xcnt[0] % 2]; lane = "xh%d" % (xcnt[0] % 2); xcnt[0] += 1
                for c4 in range(2):
                    b = kb.bank(4, 6)
                    for jj in range(4):
                        c = hf * 8 + c4 * 4 + jj
                        kb.tr(b[:, jj * 128:(jj + 1) * 128], x1T[:, c, j * 128:(j + 1) * 128], [x1T], [b])
                    kb.cp("act" if c4 == 0 else "dve", xt[:, c4 * 512:(c4 + 1) * 512], b.ap, [b], [xt])
                r0 = t0 + j * 128
                kb.dma("sp", lane, out_d[r0:r0 + 128, hf * 1024:(hf + 1) * 1024], xt.ap, [xt], [R_out])
    fw.barrier()
    return kb, {}


_CACHE = {}


def _prep_inputs(inputs):
    f32 = np.float32
    x = np.asarray(inputs["x"], f32); c = np.asarray(inputs["c"], f32)
    positions = np.asarray(inputs["positions"]).astype(np.int32)
    w_in = np.asarray(inputs["w_in"], f32)[0]
    offs = [0, 512, 768, 832, 1856, 2880, 3904, 4928, 5952]
    seg = lambda i: w_in[:, offs[i]:offs[i + 1]]
    w_uq = np.asarray(inputs["w_uq"], f32)[0].reshape(512, 8, 192)
    w_uq_r = np.concatenate([w_uq[:, :, 0:128], w_uq[:, :, 128:192], w_uq[:, :, 128:192]], axis=2).reshape(512, 2048)
    invf = (10000.0 ** (-np.arange(32, dtype=f32) / 32)).astype(f32)
    lbf = np.asarray(inputs["hgrn_lb_logits_fwd"], f32); lbb = np.asarray(inputs["hgrn_lb_logits_bwd"], f32)
    vecs = np.concatenate([np.asarray(inputs["b_mod"], f32)[0].reshape(96, 128),
                           np.asarray(inputs["pre_mix_g"], f32)[0].reshape(16, 128),
                           np.asarray(inputs["post_mix_g"], f32)[0].reshape(16, 128)], axis=0)
    common = {
        "w_mod": np.ascontiguousarray(np.asarray(inputs["w_mod"], f32)[0]),
        "w_uq": np.ascontiguousarray(w_uq_r),
        "w_ukv": np.ascontiguousarray(np.asarray(inputs["w_ukv"], f32)[0]),
        "w_out": np.ascontiguousarray(np.asarray(inputs["w_out"], f32)[0]),
        "w_up": np.ascontiguousarray(np.asarray(inputs["w_up"], f32)[0]),
        "w_down": np.ascontiguousarray(np.asarray(inputs["w_down"], f32)[0]),
        "vecs": np.ascontiguousarray(vecs),
    }
    kr = seg(2)
    win_p = []
    for p in range(2):
        f1, f2 = (seg(4), seg(5)) if p == 0 else (seg(5), seg(4))
        win_p.append(np.ascontiguousarray(np.concatenate([seg(0), seg(1), kr, kr, kr, kr, seg(3), f1, f2, seg(6), seg(7)], axis=1)))
    maps = []
    for b in range(4):
        for p in range(2):
            l1, l2 = (lbf, lbb) if p == 0 else (lbb, lbf)
            v2 = np.zeros((128, 128), f32)
            v2[0:16] = np.asarray(inputs["pre_mlp_g"], f32)[0].reshape(16, 128)
            v2[16:32] = np.asarray(inputs["post_mlp_g"], f32)[0].reshape(16, 128)
            v2[32:48] = c[b].reshape(16, 128)
            v2[48:56] = l1[0].reshape(8, 128); v2[56:64] = l1[1].reshape(8, 128)
            v2[64:72] = l2[0].reshape(8, 128); v2[72:80] = l2[1].reshape(8, 128)
            v2[80:84] = np.asarray(inputs["q_norm_g"], f32)[0].reshape(4, 128)
            v2[84:86] = np.asarray(inputs["kv_norm_g"], f32)[0].reshape(2, 128)
            v2[86] = np.asarray(inputs["hgrn_norm_g"], f32)[0]
            v2[87] = np.tile(invf, 4)
            xs = x[b] if p == 0 else x[b][::-1]
            ps = positions[b] if p == 0 else positions[b][::-1]
            m = dict(common)
            m["x"] = np.ascontiguousarray(xs)
            m["pos"] = np.ascontiguousarray(np.broadcast_to(ps[None, :], (128, TALL))).astype(np.int32)
            m["vecs2"] = v2
            m["w_in"] = win_p[p]
            maps.append(m)
    return maps


def kernel(**inputs):
    if "nc" not in _CACHE:
        kb, _ = build_program()
        kb.fw.emit()
        kb.fw.close()
        _CACHE["nc"] = kb.nc
    nc = _CACHE["nc"]
    maps = _prep_inputs(inputs)
    res = run_bass_kernel_spmd(nc, maps, core_ids=list(range(8)))
    out = np.empty((4, TALL, D), np.float32)
    for b in range(4):
        for p in range(2):
            o = np.asarray(res.results[b * 2 + p]["out"], np.float32)
            if p == 0:
                out[b, 0:TOWN] = o
            else:
                out[b, TOWN:TALL] = o[::-1]
    return out
```
